# Optimizing a Trainium2 kernel written in Bass

```python
import jax
import jax.numpy as jnp
from jax import lax
import numpy as np

D_MODEL = 1024
BATCH = 4
SEQ = 4096
DEPTH = 1
DEC_BATCH = 128
DEC_SEQ = 1
PAST_LEN = 8192
PAGE_SIZE = 128

HEAD_DIM = 64
N_Q_HEADS = 8
N_KV_HEADS = 2
GROUP = N_Q_HEADS // N_KV_HEADS
NSA_WIDTH = N_Q_HEADS * HEAD_DIM
SCALE = HEAD_DIM ** -0.5
ROPE_DIM = HEAD_DIM // 4
ROPE_THETA = 500000.0
CMP_BLOCK = 32
CMP_STRIDE = 16
CMP_HIDDEN = 64
SEL_BLOCK = 64
N_SEL = 16
WINDOW = 512
Q_BLOCK = 128
FORCE_BONUS = 1.0e4
D_RNN = D_MODEL // 2
RNN_HEADS = 8
RNN_HD = D_RNN // RNN_HEADS
CONV_W = 4
RG_C = 8.0
EPS = 1e-6
SPLIT_SIZES = (NSA_WIDTH, 6 * N_KV_HEADS * HEAD_DIM, 3 * N_Q_HEADS, NSA_WIDTH, D_RNN, D_RNN, 2 * D_MODEL)
D_IN = sum(SPLIT_SIZES)

kernel_name = 'nsa_rglru_gated_hybrid_step'


def _rmsnorm(x, g):
    xf = x.astype(jnp.float32)
    y = xf * lax.rsqrt(jnp.mean(xf * xf, axis=-1, keepdims=True) + EPS)
    return (y * g.astype(jnp.float32)).astype(x.dtype)


def _rope(x, pos):
    half = ROPE_DIM // 2
    inv = ROPE_THETA ** (-jnp.arange(half, dtype=jnp.float32) / half)
    ang = pos.astype(jnp.float32)[:, None] * inv
    ang = ang.reshape(ang.shape[0], *([1] * (x.ndim - 3)), half)
    cos, sin = jnp.cos(ang), jnp.sin(ang)
    x1, x2 = x[..., :half].astype(jnp.float32), x[..., half:ROPE_DIM].astype(jnp.float32)
    rot = jnp.concatenate([x1 * cos - x2 * sin, x2 * cos + x1 * sin], axis=-1).astype(x.dtype)
    return jnp.concatenate([rot, x[..., ROPE_DIM:]], axis=-1)


def _masked_softmax(s, mask):
    s = jnp.where(mask, s.astype(jnp.float32), -jnp.inf)
    m = jnp.max(s, axis=-1, keepdims=True)
    m = jnp.where(jnp.isfinite(m), m, 0.0)
    e = jnp.where(mask, jnp.exp(s - m), 0.0)
    d = jnp.sum(e, axis=-1, keepdims=True)
    return e / jnp.where(d > 0, d, 1.0)


def _compress(rows, pe, w1, w2):
    b, t = rows.shape[:2]
    sub = rows.reshape(b, t // CMP_STRIDE, CMP_STRIDE, N_KV_HEADS, HEAD_DIM)
    lo = jnp.einsum('bsjhd,jde->bshe', sub, w1[:CMP_STRIDE])
    hi = jnp.einsum('bsjhd,jde->bshe', sub, w1[CMP_STRIDE:])
    pos_bias = jnp.einsum('jd,jde->e', pe, w1)
    hid = jax.nn.silu(lo[:, :-1] + hi[:, 1:] + pos_bias)
    return jnp.einsum('bche,ed->bchd', hid, w2)


def _sel_importance(p, ns):
    r = SEL_BLOCK // CMP_STRIDE
    lead = CMP_BLOCK // CMP_STRIDE - 1
    pp = jnp.pad(p, [(0, 0)] * (p.ndim - 1) + [(lead, r)])
    imp = pp[..., :r * ns].reshape(*p.shape[:-1], ns, r).sum(axis=-1)
    for k in range(r, r + lead):
        imp = imp + pp[..., k:k + r * ns:r]
    return imp


def _nsa_cmp_sel(q, q_pos, kc_raw, vc_raw, ks, vs, g_kc, pe_k, w1_k, w2_k, pe_v, w1_v, w2_v):
    b, tq = q.shape[:2]
    tk = ks.shape[1]
    kc = _compress(kc_raw, pe_k, w1_k, w2_k)
    vc = _compress(vc_raw, pe_v, w1_v, w2_v)
    c_end = jnp.arange(kc.shape[1]) * CMP_STRIDE + (CMP_BLOCK - 1)
    kc = _rope(_rmsnorm(kc, g_kc), c_end)
    ns = tk // SEL_BLOCK
    n_top = min(N_SEL, ns)
    ks_b = ks.reshape(b, ns, SEL_BLOCK, N_KV_HEADS, HEAD_DIM).transpose(0, 3, 1, 2, 4)
    vs_b = vs.reshape(b, ns, SEL_BLOCK, N_KV_HEADS, HEAD_DIM).transpose(0, 3, 1, 2, 4)
    bi = jnp.arange(b)[:, None, None, None]
    hi = jnp.arange(N_KV_HEADS)[None, :, None, None]
    sel_off = jnp.arange(SEL_BLOCK)
    blk_ids = jnp.arange(ns)[None, :]
    qb = min(Q_BLOCK, tq)
    nqb = tq // qb
    m_sel = n_top * SEL_BLOCK

    def one_block(args):
        q_blk, pos = args
        s_c = jnp.einsum('bqhgd,bchd->bhgqc', q_blk, kc) * SCALE
        p_c = _masked_softmax(s_c, c_end[None, :] <= pos[:, None])
        o_c = jnp.einsum('bhgqc,bchd->bqhgd', p_c, vc)
        imp = _sel_importance(p_c.sum(axis=2), ns)
        cur = pos[:, None] // SEL_BLOCK
        forced = (blk_ids == 0) | (blk_ids == cur) | (blk_ids == cur - 1)
        _, idx = lax.top_k(imp + jnp.where(forced, FORCE_BONUS, 0.0), n_top)
        k_g = ks_b[bi, hi, idx]
        v_g = vs_b[bi, hi, idx].reshape(b, N_KV_HEADS, qb, m_sel, HEAD_DIM)
        tok = idx[..., None] * SEL_BLOCK + sel_off
        mask = (tok <= pos[:, None, None]).reshape(b, N_KV_HEADS, 1, qb, m_sel)
        s_s = jnp.einsum('bqhgd,bhqkld->bhgqkl', q_blk, k_g).reshape(b, N_KV_HEADS, GROUP, qb, m_sel) * SCALE
        p_s = _masked_softmax(s_s, mask)
        o_s = jnp.einsum('bhgqm,bhqmd->bqhgd', p_s, v_g)
        return o_c, o_s

    qs = q.reshape(b, nqb, qb, N_KV_HEADS, GROUP, HEAD_DIM).swapaxes(0, 1)
    o_c, o_s = lax.map(one_block, (qs, q_pos.reshape(nqb, qb)))
    shape = (b, tq, N_KV_HEADS, GROUP, HEAD_DIM)
    return o_c.swapaxes(0, 1).reshape(shape), o_s.swapaxes(0, 1).reshape(shape)


def _window_attn(q, k, v, q_pos, k_pos):
    s = jnp.einsum('...qhgd,...khd->...hgqk', q, k) * SCALE
    d = q_pos[..., :, None] - k_pos[..., None, :]
    mask = (d >= 0) & (d <= WINDOW) & (k_pos[..., None, :] >= 0)
    p = _masked_softmax(s, mask[..., None, None, :, :])
    return jnp.einsum('...hgqk,...khd->...qhgd', p, v)


def _prompt_window(q, k, v):
    b, t = q.shape[:2]
    nqb = t // Q_BLOCK
    nback = WINDOW // Q_BLOCK

    def band(a):
        ab = a.reshape(b, nqb, Q_BLOCK, N_KV_HEADS, HEAD_DIM)
        ab = jnp.pad(ab, ((0, 0), (nback, 0), (0, 0), (0, 0), (0, 0)))
        return jnp.concatenate([ab[:, i:i + nqb] for i in range(nback + 1)], axis=2)

    k_pos = (jnp.arange(nqb)[:, None] - nback) * Q_BLOCK + jnp.arange((nback + 1) * Q_BLOCK)[None, :]
    q_pos = jnp.arange(t).reshape(nqb, Q_BLOCK)
    qb = q.reshape(b, nqb, Q_BLOCK, N_KV_HEADS, GROUP, HEAD_DIM)
    o = _window_attn(qb, band(k), band(v), q_pos, k_pos)
    return o.reshape(b, t, N_KV_HEADS, GROUP, HEAD_DIM)


def _causal_conv(x, buf, w, bias):
    t = x.shape[1]
    xp = jnp.concatenate([buf.astype(x.dtype), x], axis=1)
    y = bias
    for k in range(CONV_W):
        y = y + xp[:, k:k + t] * w[k]
    return y.astype(x.dtype), xp[:, -(CONV_W - 1):]


def _rglru(xc, h0, w_ra, b_ra, w_rx, b_rx, lam):
    b, t = xc.shape[:2]
    xf = xc.astype(jnp.float32)
    xh = xf.reshape(b, t, RNN_HEADS, RNN_HD)
    r = jax.nn.sigmoid(jnp.einsum('bthi,hij->bthj', xh, w_ra) + b_ra).reshape(b, t, D_RNN)
    i = jax.nn.sigmoid(jnp.einsum('bthi,hij->bthj', xh, w_rx) + b_rx).reshape(b, t, D_RNN)
    log_a = -RG_C * r * jax.nn.softplus(-lam.astype(jnp.float32))
    a = jnp.exp(log_a)
    u = jnp.sqrt(-jnp.expm1(2.0 * log_a)) * (i * xf)

    def step(h, au):
        h = au[0] * h + au[1]
        return h, h

    h_last, hs = lax.scan(step, h0.astype(jnp.float32), (a.swapaxes(0, 1), u.swapaxes(0, 1)))
    return hs.swapaxes(0, 1).astype(xc.dtype), h_last


def _split_cols(p):
    out, start = [], 0
    for n in SPLIT_SIZES:
        out.append(p[..., start:start + n])
        start += n
    return out


def _mixer_inputs(x, pos, g_norm, w_in, g_q, g_ks, g_kw):
    b, t = x.shape[:2]
    u = _rmsnorm(x, g_norm)
    q, kv, g_nsa, z_nsa, x_rnn, z_rnn, g_mrg = _split_cols(u @ w_in)
    q = _rope(_rmsnorm(q.reshape(b, t, N_KV_HEADS, GROUP, HEAD_DIM), g_q), pos)
    kv = kv.reshape(b, t, 3, 2, N_KV_HEADS, HEAD_DIM)
    kc, vc = kv[:, :, 0, 0], kv[:, :, 0, 1]
    ks = _rope(_rmsnorm(kv[:, :, 1, 0], g_ks), pos)
    vs = kv[:, :, 1, 1]
    kw = _rope(_rmsnorm(kv[:, :, 2, 0], g_kw), pos)
    vw = kv[:, :, 2, 1]
    return q, kc, vc, ks, vs, kw, vw, g_nsa, z_nsa, x_rnn, z_rnn, g_mrg


def _mixer_output(x, o_cmp, o_sel, o_win, g_nsa, z_nsa, h_rnn, z_rnn, g_mrg, w_pa, w_pb, w_out):
    b, t = x.shape[:2]
    gb = jax.nn.sigmoid(g_nsa.astype(jnp.float32)).reshape(b, t, N_KV_HEADS, GROUP, 3)
    o = gb[..., 0:1] * o_cmp + gb[..., 1:2] * o_sel + gb[..., 2:3] * o_win
    a_br = (o.reshape(b, t, NSA_WIDTH) * jax.nn.silu(z_nsa.astype(jnp.float32))).astype(x.dtype)
    r_br = (h_rnn.astype(jnp.float32) * jax.nn.silu(z_rnn.astype(jnp.float32))).astype(x.dtype)
    gm = jax.nn.sigmoid(g_mrg.astype(jnp.float32))
    merged = gm[..., :D_MODEL] * (a_br @ w_pa) + gm[..., D_MODEL:] * (r_br @ w_pb)
    return (x + merged.astype(x.dtype) @ w_out).astype(x.dtype)


def _paged_rows(cache, page_table, new_rows):
    nb = page_table.shape[0]
    past = cache[page_table].reshape(nb, -1, 2, N_KV_HEADS, HEAD_DIM)
    t = past.shape[1] + new_rows.shape[1]
    t_pad = -(-t // SEL_BLOCK) * SEL_BLOCK
    pad = jnp.zeros((nb, t_pad - t, 2, N_KV_HEADS, HEAD_DIM), past.dtype)
    rows = jnp.concatenate([past, new_rows.astype(past.dtype), pad], axis=1)
    return rows[:, :, 0], rows[:, :, 1]


def setup_inputs(seed: int = 0) -> dict:
    key = jax.random.key(seed)
    k = jax.random.split(key, 32)
    f32 = jnp.float32

    def nrm(kk, shape, scale=1.0):
        return jax.random.normal(kk, shape, f32) * scale

    n_pages = PAST_LEN // PAGE_SIZE
    n_used = DEC_BATCH * n_pages
    n_pool = n_used + max(1, n_used // 4)
    wbuf = min(WINDOW, PAST_LEN)
    page_table = jax.random.permutation(k[5], n_pool)[:n_used].reshape(DEC_BATCH, n_pages).astype(jnp.int32)
    ua = jax.random.uniform(k[20], (D_RNN,), f32, 0.9, 0.999)
    sa = ua ** (1.0 / RG_C)
    lam = jnp.log(sa) - jnp.log1p(-sa)
    return {
        'x_prompt': nrm(k[0], (BATCH, SEQ, D_MODEL)),
        'x_sample': nrm(k[1], (DEC_BATCH, DEC_SEQ, D_MODEL)),
        'cache_kv_cmp': nrm(k[2], (n_pool, PAGE_SIZE, 2, N_KV_HEADS, HEAD_DIM)),
        'cache_kv_sel': nrm(k[3], (n_pool, PAGE_SIZE, 2, N_KV_HEADS, HEAD_DIM)),
        'cache_kv_win': nrm(k[4], (DEC_BATCH, wbuf, 2, N_KV_HEADS, HEAD_DIM)),
        'state_conv': nrm(k[6], (DEC_BATCH, CONV_W - 1, D_RNN)),
        'state_h': nrm(k[7], (DEC_BATCH, D_RNN), 0.5),
        'page_table': page_table,
        'g_norm': 1.0 + nrm(k[8], (D_MODEL,), 0.02),
        'w_in': nrm(k[9], (D_MODEL, D_IN), D_MODEL ** -0.5),
        'g_q': 1.0 + nrm(k[10], (HEAD_DIM,), 0.02),
        'g_kc': 1.0 + nrm(k[11], (HEAD_DIM,), 0.02),
        'g_ks': 1.0 + nrm(k[12], (HEAD_DIM,), 0.02),
        'g_kw': 1.0 + nrm(k[13], (HEAD_DIM,), 0.02),
        'pe_k': nrm(k[14], (CMP_BLOCK, HEAD_DIM), 0.1),
        'w1_k': nrm(k[15], (CMP_BLOCK, HEAD_DIM, CMP_HIDDEN), (CMP_BLOCK * HEAD_DIM) ** -0.5),
        'w2_k': nrm(k[16], (CMP_HIDDEN, HEAD_DIM), CMP_HIDDEN ** -0.5),
        'pe_v': nrm(k[17], (CMP_BLOCK, HEAD_DIM), 0.1),
        'w1_v': nrm(k[18], (CMP_BLOCK, HEAD_DIM, CMP_HIDDEN), (CMP_BLOCK * HEAD_DIM) ** -0.5),
        'w2_v': nrm(k[19], (CMP_HIDDEN, HEAD_DIM), CMP_HIDDEN ** -0.5),
        'conv_w': nrm(k[21], (CONV_W, D_RNN), CONV_W ** -0.5),
        'conv_b': nrm(k[22], (D_RNN,), 0.01),
        'w_ra': nrm(k[23], (RNN_HEADS, RNN_HD, RNN_HD), RNN_HD ** -0.5),
        'b_ra': nrm(k[24], (RNN_HEADS, RNN_HD), 0.01),
        'w_rx': nrm(k[25], (RNN_HEADS, RNN_HD, RNN_HD), RNN_HD ** -0.5),
        'b_rx': nrm(k[26], (RNN_HEADS, RNN_HD), 0.01),
        'lam': lam,
        'w_pa': nrm(k[27], (NSA_WIDTH, D_MODEL), NSA_WIDTH ** -0.5),
        'w_pb': nrm(k[28], (D_RNN, D_MODEL), D_RNN ** -0.5),
        'w_out': nrm(k[29], (D_MODEL, D_MODEL), D_MODEL ** -0.5),
    }


def reference(x_prompt, x_sample, cache_kv_cmp, cache_kv_sel, cache_kv_win, state_conv, state_h, page_table,
              g_norm, w_in, g_q, g_kc, g_ks, g_kw, pe_k, w1_k, w2_k, pe_v, w1_v, w2_v,
              conv_w, conv_b, w_ra, b_ra, w_rx, b_rx, lam, w_pa, w_pb, w_out):
    b_p, t_p = x_prompt.shape[:2]
    pos_p = jnp.arange(t_p)
    (q_p, kc_p, vc_p, ks_p, vs_p, kw_p, vw_p,
     gn_p, zn_p, xr_p, zr_p, gm_p) = _mixer_inputs(x_prompt, pos_p, g_norm, w_in, g_q, g_ks, g_kw)
    o_cmp_p, o_sel_p = _nsa_cmp_sel(q_p, pos_p, kc_p, vc_p, ks_p, vs_p, g_kc,
                                    pe_k, w1_k, w2_k, pe_v, w1_v, w2_v)
    o_win_p = _prompt_window(q_p, kw_p, vw_p)
    xc_p, conv_prompt = _causal_conv(xr_p, jnp.zeros((b_p, CONV_W - 1, D_RNN), xr_p.dtype), conv_w, conv_b)
    hs_p, h_prompt = _rglru(xc_p, jnp.zeros((b_p, D_RNN), jnp.float32), w_ra, b_ra, w_rx, b_rx, lam)
    y_prompt = _mixer_output(x_prompt, o_cmp_p, o_sel_p, o_win_p, gn_p, zn_p, hs_p, zr_p, gm_p, w_pa, w_pb, w_out)
    kv_cmp_prompt = jnp.stack([kc_p, vc_p], axis=2)
    kv_sel_prompt = jnp.stack([ks_p, vs_p], axis=2)
    kv_win_prompt = jnp.stack([kw_p, vw_p], axis=2)[:, -min(WINDOW, t_p):]

    pos_s = PAST_LEN + jnp.arange(x_sample.shape[1])
    (q_s, kc_s, vc_s, ks_s, vs_s, kw_s, vw_s,
     gn_s, zn_s, xr_s, zr_s, gm_s) = _mixer_inputs(x_sample, pos_s, g_norm, w_in, g_q, g_ks, g_kw)
    kv_cmp_sample = jnp.stack([kc_s, vc_s], axis=2)
    kv_sel_sample = jnp.stack([ks_s, vs_s], axis=2)
    kc_all, vc_all = _paged_rows(cache_kv_cmp, page_table, kv_cmp_sample)
    ks_all, vs_all = _paged_rows(cache_kv_sel, page_table, kv_sel_sample)
    o_cmp_s, o_sel_s = _nsa_cmp_sel(q_s, pos_s, kc_all, vc_all, ks_all, vs_all, g_kc,
                                    pe_k, w1_k, w2_k, pe_v, w1_v, w2_v)
    wbuf = cache_kv_win.shape[1]
    win_rows = jnp.concatenate([cache_kv_win, jnp.stack([kw_s, vw_s], axis=2).astype(cache_kv_win.dtype)], axis=1)
    k_pos_w = PAST_LEN - wbuf + jnp.arange(win_rows.shape[1])
    o_win_s = _window_attn(q_s, win_rows[:, :, 0], win_rows[:, :, 1], pos_s, k_pos_w)
    kv_win_sample = win_rows[:, -min(WINDOW, win_rows.shape[1]):]
    xc_s, conv_sample = _causal_conv(xr_s, state_conv, conv_w, conv_b)
    hs_s, h_sample = _rglru(xc_s, state_h, w_ra, b_ra, w_rx, b_rx, lam)
    y_sample = _mixer_output(x_sample, o_cmp_s, o_sel_s, o_win_s, gn_s, zn_s, hs_s, zr_s, gm_s, w_pa, w_pb, w_out)

    return (y_prompt, y_sample, kv_cmp_prompt, kv_cmp_sample, kv_sel_prompt, kv_sel_sample,
            kv_win_prompt, kv_win_sample, conv_prompt, conv_sample, h_prompt, h_sample)
```

```python
import numpy as np
from contextlib import ExitStack
import concourse.bass as bass
import concourse.mybir as mybir
from concourse.bass_utils import run_bass_kernel_spmd

F32 = mybir.dt.float32
BF16 = mybir.dt.bfloat16
I32 = mybir.dt.int32
AF = mybir.ActivationFunctionType
ALU = mybir.AluOpType
AX = mybir.AxisListType

D_MODEL = 1024
SEQ = 4096
NT = 32
NPAIR = 16
EPS = 1e-6
SCALE = 0.125
CQ, CKV, CGN, CZN, CXR, CZR, CGM = 0, 512, 1280, 1304, 1816, 2328, 2840
D_IN = 4888
ROPE_THETA = 500000.0
PAST = 8192


class StopBuild(Exception):
    pass


def ck(name):
    import os
    if os.environ.get("STOP", "") == name:
        raise StopBuild(name)


class Buf:
    __slots__ = ("name", "w", "r")

    def __init__(self, name):
        self.name = name
        self.w = None
        self.r = []


class Sched:
    def __init__(self, nc, ndma=40):
        self.nc = nc
        self.eng = {"pe": nc.tensor, "act": nc.scalar, "dve": nc.vector,
                    "pool": nc.gpsimd, "sp": nc.sync}
        self.sem, self.cnt, self._cms = {}, {}, []
        for k in self.eng:
            cm = nc.semaphore("s_" + k)
            self._cms.append(cm)
            self.sem[k] = cm.__enter__()
            self.cnt[k] = 0
        self.dsem, self.dcnt = [], []
        for i in range(ndma):
            cm = nc.semaphore("d%d" % i)
            self._cms.append(cm)
            self.dsem.append(cm.__enter__())
            self.dcnt.append(0)
        self.dnext = 0
        self.seen = {k: {} for k in self.eng}
        self.pend = {}
        self.ninst = 0

    def close(self):
        for cm in reversed(self._cms):
            cm.__exit__(None, None, None)

    def _wait(self, e, ev):
        if ev is None:
            return
        key, val = ev
        if key == "pe" and e == "pe":
            return
        if isinstance(key, str) and val == self.cnt[key] + 1:
            self._materialize(key)
        if self.seen[e].get(key, 0) >= val:
            return
        sem = self.sem[key] if isinstance(key, str) else self.dsem[key]
        self.eng[e].wait_ge(sem, val)
        self.seen[e][key] = val

    def _deps(self, e, reads, writes):
        for b in reads:
            self._wait(e, b.w)
        for b in writes:
            self._wait(e, b.w)
            for ev in b.r:
                self._wait(e, ev)

    def _commit(self, ev, reads, writes):
        for b in reads:
            b.r.append(ev)
            if len(b.r) > 8:
                best = {}
                for k, v in b.r:
                    if best.get(k, 0) < v:
                        best[k] = v
                b.r = list(best.items())
        for b in writes:
            b.w = ev
            b.r = []

    def _materialize(self, key):
        ins = self.pend.get(key)
        if ins is not None:
            self.cnt[key] += 1
            ins.then_inc(self.sem[key], 1)
            self.pend[key] = None

    def op(self, e, fn, reads=(), writes=(), inc=True):
        self._deps(e, reads, writes)
        ins = fn()
        self.pend[e] = ins
        self._commit((e, self.cnt[e] + 1), reads, writes)
        self.ninst += 1
        return ins

    def dma(self, q, out, in_, reads=(), writes=(), fn=None):
        self._deps(q, reads, writes)
        slot = self.dnext
        self.dnext = (self.dnext + 1) % len(self.dsem)
        if self.dcnt[slot] > 0:
            self._wait(q, (slot, self.dcnt[slot]))
        ins = self.eng[q].dma_start(out=out, in_=in_) if fn is None else fn()
        self.dcnt[slot] += 16
        ins.then_inc(self.dsem[slot], 16)
        ev = (slot, self.dcnt[slot])
        self._commit(ev, reads, writes)
        self.ninst += 1
        return ev

    def _all_events(self):
        for k in self.eng:
            self._materialize(k)
        evs = [(k, self.cnt[k]) for k in self.eng if self.cnt[k] > 0]
        evs += [(i, self.dcnt[i]) for i in range(len(self.dsem)) if self.dcnt[i] > 0]
        return evs

    def barrier(self):
        evs = self._all_events()
        for e in self.eng:
            for ev in evs:
                self._wait(e, ev)

    def finish(self, e="sp"):
        for ev in self._all_events():
            self._wait(e, ev)


def _rope_tab(pos):
    half = 8
    inv = ROPE_THETA ** (-np.arange(half, dtype=np.float32) / half)
    ang = np.asarray(pos, np.float32)[:, None] * inv[None, :].astype(np.float32)
    return np.concatenate([np.cos(ang), np.sin(ang)], axis=1).astype(np.float32)


def _core_tables(p):
    t = {}
    own_tiles = [2 * i + p for i in range(NPAIR)]
    pos_own = np.concatenate([np.arange(g * 128, g * 128 + 128) for g in own_tiles])
    t["ropeO"] = _rope_tab(pos_own)
    c = np.arange(256)
    cend = 16 * c + 31
    cm = np.zeros((NPAIR, 2, 128, 128), np.float32)
    fb = np.zeros((NPAIR, 128, 64), np.float32)
    for i, g in enumerate(own_tiles):
        pos = g * 128 + np.arange(128)
        m = (cend[:, None] <= pos[None, :]) & (c[:, None] < 255)
        cm[i] = m.reshape(2, 128, 128)
        cur = pos // 64
        blk = np.arange(64)
        f = (blk[None, :] == 0) | (blk[None, :] == cur[:, None]) | (blk[None, :] == cur[:, None] - 1)
        fb[i] = np.where(f, 1.0e4, 0.0)
    t["cmaskT"] = cm
    t["forcedB"] = fb
    r = np.arange(128)
    lower = (r[:, None] <= r[None, :]).astype(np.float32)
    upper = (r[:, None] >= r[None, :]).astype(np.float32)
    ones = np.ones((128, 128), np.float32)
    zeros = np.zeros((128, 128), np.float32)
    if p == 0:
        t["dmask"] = np.stack([lower, zeros])
        t["wmask"] = np.stack([upper, ones, ones, ones, lower, zeros])
    else:
        t["dmask"] = np.stack([ones, lower])
        t["wmask"] = np.stack([zeros, upper, ones, ones, ones, lower])
    sc = np.zeros((128, 2), np.float32)
    sc[:, p] = 1.0
    t["selcol"] = sc
    return t


def _shared_tables():
    t = {}
    t["ident"] = np.eye(128, dtype=np.float32)
    t["ropeA"] = _rope_tab(np.arange(SEQ))
    cpos = 16 * (np.arange(513) - 1) + 31
    t["ropeC"] = _rope_tab(cpos)
    t["ropeS"] = np.repeat(_rope_tab(np.array([PAST])), 16, axis=0)
    E = np.zeros((64, 32, 128), np.float32)
    for j in range(32):
        E[2 * j, j, 0:64] = 1.0
        E[2 * j + 1, j, 64:128] = 1.0
    t["Eall"] = E
    A = np.zeros((256, 64), np.float32)
    for j in range(64):
        for c in range(4 * j - 1, 4 * j + 4):
            if 0 <= c < 255:
                A[c, j] = 1.0
    t["Aimp"] = A
    return t


def build_program(npair=NPAIR, ns_seq=16, pool_rows=81920):
    nc = bass.Bass("TRN2", target_bir_lowering=False)

    def din(name, shape, dt=F32):
        return nc.dram_tensor(name, list(shape), dt, kind="ExternalInput").ap()

    def dout(name, shape, dt=F32):
        return nc.dram_tensor(name, list(shape), dt, kind="ExternalOutput").ap()

    xb_d = din("xb", [SEQ, D_MODEL])
    xown_d = din("xown", [SEQ // 2, D_MODEL])
    w_in_d = din("w_in", [D_MODEL, D_IN])
    g_norm_d = din("g_norm", [D_MODEL])
    g_q_d, g_kc_d, g_ks_d, g_kw_d = (din(n, [64]) for n in ("g_q", "g_kc", "g_ks", "g_kw"))
    pe_k_d, pe_v_d = din("pe_k", [32, 64]), din("pe_v", [32, 64])
    w1_k_d, w1_v_d = din("w1_k", [32, 64, 64]), din("w1_v", [32, 64, 64])
    w2_k_d, w2_v_d = din("w2_k", [64, 64]), din("w2_v", [64, 64])
    conv_w_d, conv_b_d = din("conv_w", [4, 512]), din("conv_b", [512])
    w_ra_d, b_ra_d = din("w_ra", [8, 64, 64]), din("b_ra", [8, 64])
    w_rx_d, b_rx_d = din("w_rx", [8, 64, 64]), din("b_rx", [8, 64])
    lam_d = din("lam", [512])
    w_pa_d, w_pb_d, w_out_d = din("w_pa", [512, 1024]), din("w_pb", [512, 1024]), din("w_out", [1024, 1024])
    ident_d = din("ident", [128, 128])
    ropeA_d, ropeO_d, ropeC_d, ropeS_d = din("ropeA", [SEQ, 16]), din("ropeO", [SEQ // 2, 16]), din("ropeC", [513, 16]), din("ropeS", [16, 16])
    cmaskT_d = din("cmaskT", [NPAIR, 2, 128, 128])
    forcedB_d = din("forcedB", [NPAIR, 128, 64])
    dmask_d = din("dmask", [2, 128, 128])
    wmask_d = din("wmask", [6, 128, 128])
    selcol_d = din("selcol", [128, 2])
    Eall_d = din("Eall", [64, 32, 128])
    Aimp_d = din("Aimp", [256, 64])
    xs_d = din("xs", [16, D_MODEL])
    ptrep_d = din("ptrep", [128, 64], I32)
    pm8_d = din("pm8", [128, 1])
    poolc_d = din("poolc", [pool_rows, 4096])
    pools_d = din("pools", [pool_rows, 4096])
    ckw_d = din("ckw", [16, 512, 256])
    sconv_d = din("sconv", [16, 3, 512])
    sh_d = din("sh", [16, 512])
    As_d = din("As", [512, 129])
    cmS_d = din("cmS", [128, 4])
    fS_d = din("fS", [32, 129])
    oys_d = dout("oys", [16, D_MODEL])
    okvcs_d = dout("okvcs", [16, 256])
    okvss_d = dout("okvss", [16, 256])
    okvws_d = dout("okvws", [16, 512, 256])
    oconvs_d = dout("oconvs", [16, 3, 512])
    ohs_d = dout("ohs", [16, 512])
    oy_d = dout("oy", [SEQ // 2, D_MODEL])
    okvc_d = dout("okvc", [SEQ, 256])
    okvs_d = dout("okvs", [SEQ, 256])
    okvw_d = dout("okvw", [512, 256])
    oconv_d = dout("oconv", [3, 512])
    oh_d = dout("oh", [512])

    WbD = nc.dram_tensor("WbD", [128, 8, D_IN], BF16, kind="Internal").ap(); bWbD = Buf("WbD")
    WpaD = nc.dram_tensor("WpaD", [128, 4, 1024], BF16, kind="Internal").ap()
    WpbD = nc.dram_tensor("WpbD", [128, 4, 1024], BF16, kind="Internal").ap()
    WoutD = nc.dram_tensor("WoutD", [128, 8, 1024], BF16, kind="Internal").ap()
    scrOc = nc.dram_tensor("scrOc", [4, 32, 194], F32, kind="Internal").ap()
    scrOs = nc.dram_tensor("scrOs", [4, 32, 65], F32, kind="Internal").ap()
    scrOw = nc.dram_tensor("scrOw", [4, 32, 65], F32, kind="Internal").ap()
    scrSel = nc.dram_tensor("scrSel", [32, 512], F32, kind="Internal").ap()
    scrG = nc.dram_tensor("scrG", [16, 24], F32, kind="Internal").ap()
    scrZ = nc.dram_tensor("scrZ", [16, 512], F32, kind="Internal").ap()
    scrA = nc.dram_tensor("scrA", [32, 256], F32, kind="Internal").ap()

    S = Sched(nc)
    V, G, A_, T = nc.vector, nc.gpsimd, nc.scalar, nc.tensor
    import os as _os
    DBG = int(_os.environ.get("DBG_PAIR", "-1"))
    dbg_out = {}
    if DBG >= 0:
        for nm, shp in (("d_obr", [3, 128, 512]), ("d_acc", [128, 512]), ("d_hs", [128, 512]), ("d_sel", [2, 128, 64]), ("d_imp", [2, 128, 64])):
            dbg_out[nm] = dout(nm, shp)

    with ExitStack() as es:
        cur_es = [es]

        def sb(name, shape, dt=F32):
            return cur_es[0].enter_context(nc.sbuf_tensor("s_" + name, list(shape), dt))

        def ps(name, shape, dt=F32):
            return es.enter_context(nc.psum_tensor("p_" + name, list(shape), dt))

        wu = [sb("wu%d" % i, [128, 4096], BF16) for i in range(4)]; bwu = [Buf("wu%d" % i) for i in range(4)]
        identf = sb("identf", [128, 128], F32); bidf = Buf("identf")
        identb = sb("identb", [128, 128], BF16); bidb = Buf("identb")
        gnc = sb("gnc", [128, 8], F32); bgnc = Buf("gnc")
        gqb = sb("gqb", [128, 64], F32); gkcb = sb("gkcb", [128, 64], F32)
        gksb = sb("gksb", [128, 64], F32); gkwb = sb("gkwb", [128, 64], F32)
        bgq, bgkc, bgks, bgkw = Buf("gq"), Buf("gkc"), Buf("gks"), Buf("gkw")
        cwT = sb("cwT", [128, 4, 4], F32); bcw = Buf("cwT")
        cbT = sb("cbT", [128, 4], F32); bcb = Buf("cbT")
        braT = sb("braT", [128, 4], F32); brxT = sb("brxT", [128, 4], F32); bbr = Buf("brT")
        clT = sb("clT", [128, 4], F32); cl2T = sb("cl2T", [128, 4], F32); bcl = Buf("clT")
        WraB = sb("WraB", [128, 4, 128], BF16); WrxB = sb("WrxB", [128, 4, 128], BF16); bWr = Buf("WrB")
        W2k = sb("W2k", [64, 64], BF16); W2v = sb("W2v", [64, 64], BF16); bW2 = Buf("W2")
        pbk = sb("pbk", [64, 1], F32); pbv = sb("pbv", [64, 1], F32); bpb = Buf("pb")
        nr_sq = sb("nr_sq", [128, 512]); nr_ss = sb("nr_ss", [128, 8]); nr_t = sb("nr_t", [128, 4, 8, 8]); bnr = Buf("nr")
        esP = ExitStack()
        cur_es[0] = esP
        WbA = sb("WbA", [128, 8, 1280], BF16); bWbA = Buf("WbA")
        Wgn = sb("Wgn", [128, 8, 24], BF16); bWgn = Buf("Wgn")
        KST = sb("KST", [64, 2, SEQ], BF16); bKST = [Buf("KST%d" % t) for t in range(NT)]
        VSa = sb("VSa", [128, NT, 2, 65], BF16); bVSa = [Buf("VSa%d" % t) for t in range(NT)]
        KWT = sb("KWT", [64, 2, 8 * 128], BF16); bKWT = [Buf("KWT%d" % t) for t in range(8)]
        VWa = sb("VWa", [128, 8, 2, 65], BF16); bVWa = [Buf("VWa%d" % t) for t in range(8)]
        KCT = sb("KCT", [64, 2, 256], BF16); bKCT = Buf("KCT")
        VCT = sb("VCT", [64, 2, 256], BF16); bVCT = Buf("VCT")
        VCA = sb("VCA", [128, 2, 2, 129], BF16); bVCA = Buf("VCA")
        Eall = sb("Eall", [64, 32, 128], BF16); bEall = Buf("Eall")
        XTp = sb("XTp", [64, 4, 272], BF16); bXTp = Buf("XTp")
        xr = sb("xr", [128, 4, 259], F32); bxr = Buf("xr")
        hprev = sb("hprev", [128, 4], F32); bhprev = Buf("hprev")
        W1k = sb("W1k", [64, 32, 64], BF16); W1v = sb("W1v", [64, 32, 64], BF16); bW1 = Buf("W1")
        selcol = sb("selcol", [128, 2], F32); bselcol = Buf("selcol")
        dmask = sb("dmask", [128, 2, 128], F32); bdmask = Buf("dmask")
        wmask = sb("wmask", [128, 6, 128], F32); bwmask = Buf("wmask")
        ropeCt = sb("ropeCt", [16, NPAIR, 16], F32); bropeC = Buf("ropeCt")

        pmm = [ps("pmm0", [128, 512]), ps("pmm1", [128, 512])]; bpmm = [Buf("pmm0"), Buf("pmm1")]
        ptr = ps("ptr", [128, 1024], BF16); bptr = Buf("ptr")
        pst = [ps("pst0", [128, 512]), ps("pst1", [128, 512])]; bpst = [Buf("pst0"), Buf("pst1")]
        pmk = ps("pmk", [128, 2, 256]); _bp = Buf("pmk"); bpmk = [_bp, _bp]
        pO = ps("pO", [128, 4, 256]); bpO = Buf("pO")
        st = {"pmm": 0, "pst": 0, "pmk": 0}

        def nxt(kind):
            st[kind] ^= 1
            return st[kind]

        def bc_mid(ap, P, n, w):
            return ap.unsqueeze(1).to_broadcast([P, n, w])

        def bc_last(ap, P, n, w):
            return ap.unsqueeze(2).to_broadcast([P, n, w])

        with ExitStack() as es2:
            stg = [es2.enter_context(nc.sbuf_tensor("stg%d" % i, [128, 2444], F32)) for i in range(2)]
            bstg = [Buf("stg0"), Buf("stg1")]
            tmpc = es2.enter_context(nc.sbuf_tensor("tmpc", [128, 16, 64], F32)); btmpc = Buf("tmpc")
            S.dma("sp", identf[:], ident_d, writes=[bidf])
            S.op("act", lambda: A_.copy(out=identb[:], in_=identf[:]), [bidf], [bidb])
            S.dma("sp", None, None, writes=[bgnc], fn=lambda: nc.sync.dma_start(
                out=gnc[:], in_=g_norm_d.rearrange("(k p) -> p k", p=128), allow_slow_non_contiguous=True))
            n = 0
            engs = ["dve", "pool"]
            stgb = [es2.enter_context(nc.sbuf_tensor("stgb%d" % i, [128, 2444], BF16)) for i in range(2)]
            bstgb = [Buf("stgb0"), Buf("stgb1")]
            for k in range(8):
                for hf in range(2):
                    sl = n % 2
                    S.dma("sp", stg[sl][:], w_in_d[k * 128:(k + 1) * 128, hf * 2444:(hf + 1) * 2444], writes=[bstg[sl]])
                    e = engs[n % 2]
                    E_ = V if e == "dve" else G
                    S.op(e, lambda: E_.tensor_scalar(out=stgb[sl][:], in0=stg[sl][:], scalar1=gnc[:, k:k + 1], scalar2=None,
                                                     op0=ALU.mult), [bstg[sl], bgnc], [bstgb[sl]])
                    S.dma("sp", WbD[:, k, hf * 2444:(hf + 1) * 2444], stgb[sl][:], reads=[bstgb[sl]], writes=[bWbD])
                    n += 1
            for (wd, wD, nk) in ((w_pa_d, WpaD, 4), (w_pb_d, WpbD, 4), (w_out_d, WoutD, 8)):
                for k in range(nk):
                    sl = n % 2
                    S.dma("sp", stg[sl][:, 0:1024], wd[k * 128:(k + 1) * 128, :], writes=[bstg[sl]])
                    e = engs[n % 2]
                    E_ = V if e == "dve" else G
                    S.op(e, lambda: E_.tensor_copy(out=stgb[sl][:, 0:1024], in_=stg[sl][:, 0:1024]), [bstg[sl]], [bstgb[sl]])
                    S.dma("sp", wD[:, k, :], stgb[sl][:, 0:1024], reads=[bstgb[sl]], writes=[bWbD])
                    n += 1
            S.dma("sp", WbA[:, :, 0:768], WbD[:, :, CKV:CKV + 768], reads=[bWbD], writes=[bWbA])
            S.dma("sp", WbA[:, :, 768:1280], WbD[:, :, CXR:CXR + 512], reads=[bWbD], writes=[bWbA])
            S.dma("sp", None, None, reads=[bWbD], writes=[bWgn], fn=lambda: nc.sync.dma_start(out=Wgn[:], in_=WbD[:, :, CGN:CGN + 24], allow_slow_non_contiguous=True))
            for hf in range(2):
                sl = n % 2
                S.dma("sp", stg[sl][0:64, 0:2048], Eall_d[:, hf * 16:(hf + 1) * 16, :].rearrange("b j k -> b (j k)"), writes=[bstg[sl]])
                S.op("dve", lambda sl=sl, hf=hf: V.tensor_copy(
                    out=Eall[:, hf * 16:(hf + 1) * 16, :].rearrange("b j k -> b (j k)"), in_=stg[sl][0:64, 0:2048]), [bstg[sl]], [bEall])
                n += 1
            for (gd, gt_, bg) in ((g_q_d, gqb, bgq), (g_kc_d, gkcb, bgkc), (g_ks_d, gksb, bgks), (g_kw_d, gkwb, bgkw)):
                S.dma("sp", None, None, writes=[bg], fn=lambda gd=gd, gt_=gt_: nc.sync.dma_start(out=gt_[:], in_=gd.partition_broadcast(128)))
            for tap in range(4):
                S.dma("sp", None, None, writes=[bcw], fn=lambda: nc.sync.dma_start(
                    out=cwT[:, :, tap], in_=conv_w_d[tap].rearrange("(g c) -> c g", c=128), allow_slow_non_contiguous=True))
            S.dma("sp", None, None, writes=[bcb], fn=lambda: nc.sync.dma_start(
                out=cbT[:], in_=conv_b_d.rearrange("(g c) -> c g", c=128), allow_slow_non_contiguous=True))
            S.dma("sp", None, None, writes=[bbr], fn=lambda: nc.sync.dma_start(
                out=braT[:], in_=b_ra_d.rearrange("(g a) c -> (a c) g", a=2), allow_slow_non_contiguous=True))
            S.dma("sp", None, None, writes=[bbr], fn=lambda: nc.sync.dma_start(
                out=brxT[:], in_=b_rx_d.rearrange("(g a) c -> (a c) g", a=2), allow_slow_non_contiguous=True))
            S.dma("sp", None, None, writes=[bcl], fn=lambda: nc.sync.dma_start(
                out=clT[:], in_=lam_d.rearrange("(g c) -> c g", c=128), allow_slow_non_contiguous=True))
            S.op("act", lambda: A_.activation(out=clT[:], in_=clT[:], func=AF.Exp, scale=-1.0), [bcl], [bcl])
            S.op("act", lambda: A_.activation(out=clT[:], in_=clT[:], func=AF.Ln, bias=1.0), [bcl], [bcl])
            S.op("dve", lambda: V.tensor_scalar(out=cl2T[:], in0=clT[:], scalar1=-16.0, scalar2=None, op0=ALU.mult), [bcl], [bcl])
            S.op("dve", lambda: V.tensor_scalar(out=clT[:], in0=clT[:], scalar1=-8.0, scalar2=None, op0=ALU.mult), [bcl], [bcl])
            for (wd, wsb) in ((w_ra_d, WraB), (w_rx_d, WrxB)):
                S.op("pool", lambda: G.memset(stg[0][:, 0:512], 0.0), [], [bstg[0]])
                for g in range(4):
                    for a in range(2):
                        S.dma("sp", stg[0][a * 64:(a + 1) * 64, g * 128 + a * 64: g * 128 + a * 64 + 64], wd[2 * g + a], writes=[bstg[0]])
                S.op("dve", lambda wsb=wsb: V.tensor_copy(out=wsb[:].rearrange("p g c -> p (g c)"), in_=stg[0][:, 0:512]), [bstg[0]], [bWr])
            for (wd, wsb) in ((w1_k_d, W1k), (w1_v_d, W1v)):
                S.dma("sp", None, None, writes=[bstg[1]], fn=lambda wd=wd: nc.sync.dma_start(
                    out=stg[1][0:64, 0:2048].rearrange("d (j e) -> d j e", j=32), in_=wd.rearrange("j d e -> d j e")))
                S.op("dve", lambda wsb=wsb: V.tensor_copy(out=wsb[:].rearrange("d j e -> d (j e)"), in_=stg[1][0:64, 0:2048]), [bstg[1]], [bW1])
            for (wd, wsb) in ((w2_k_d, W2k), (w2_v_d, W2v)):
                S.dma("sp", stg[1][0:64, 0:64], wd, writes=[bstg[1]])
                S.op("dve", lambda wsb=wsb: V.tensor_copy(out=wsb[:], in_=stg[1][0:64, 0:64]), [bstg[1]], [bW2])
            peT = es2.enter_context(nc.sbuf_tensor("peT", [64, 2, 32], F32)); bpe = Buf("peT")
            peTb = es2.enter_context(nc.sbuf_tensor("peTb", [64, 2, 32], BF16))
            S.dma("sp", None, None, writes=[bpe], fn=lambda: nc.sync.dma_start(out=peT[:, 0, :], in_=pe_k_d.rearrange("j d -> d j"), allow_slow_non_contiguous=True))
            S.dma("sp", None, None, writes=[bpe], fn=lambda: nc.sync.dma_start(out=peT[:, 1, :], in_=pe_v_d.rearrange("j d -> d j"), allow_slow_non_contiguous=True))
            S.op("dve", lambda: V.tensor_copy(out=peTb[:], in_=peT[:]), [bpe], [bpe])
            for kind, (w1s, pbt) in enumerate(((W1k, pbk), (W1v, pbv))):
                for j in range(32):
                    S.op("pe", lambda kind=kind, w1s=w1s, j=j: T.matmul(pmm[0][0:64, kind:kind + 1], lhsT=w1s[:, j, :], rhs=peTb[:, kind, j:j + 1],
                                                                       start=(j == 0), stop=(j == 31)), [bW1, bpe], [bpmm[0]])
                S.op("dve", lambda kind=kind, pbt=pbt: V.tensor_copy(out=pbt[:], in_=pmm[0][0:64, kind:kind + 1]), [bpmm[0]], [bpb])
            S.dma("sp", selcol[:], selcol_d, writes=[bselcol])
            S.dma("sp", dmask[:], dmask_d.rearrange("m k q -> k m q"), writes=[bdmask])
            S.dma("sp", wmask[:], wmask_d.rearrange("m k q -> k m q"), writes=[bwmask])
            S.dma("sp", None, None, writes=[bropeC], fn=lambda: nc.sync.dma_start(
                out=ropeCt[:], in_=ropeC_d[0:256, :].rearrange("(i m) e -> m i e", m=16)))
            S.op("pool", lambda: G.memset(VCA[:], 1.0), [], [bVCA])
            for ch in range(2):
                S.dma("sp", stg[0][:, 0:64], Aimp_d[ch * 128:(ch + 1) * 128, :], writes=[bstg[0]])
                for hk in range(2):
                    S.op("dve", lambda ch=ch, hk=hk: V.tensor_copy(out=VCA[:, ch, hk, 65:129], in_=stg[0][:, 0:64]), [bstg[0]], [bVCA])
            S.op("pool", lambda: G.memset(VSa[:], 1.0), [], bVSa)
            S.op("pool", lambda: G.memset(VWa[:], 1.0), [], bVWa)
            S.op("pool", lambda: G.memset(KCT[:], 0.0), [], [bKCT])
            S.op("pool", lambda: G.memset(VCT[:], 0.0), [], [bVCT])
            S.op("pool", lambda: G.memset(XTp[:], 0.0), [], [bXTp])
            S.op("pool", lambda: G.memset(xr[:], 0.0), [], [bxr])
            S.op("pool", lambda: G.memset(hprev[:], 0.0), [], [bhprev])
            S.barrier()

        xt = [sb("xt%d" % i, [128, 1024]) for i in range(2)]; bxt = [Buf("xt%d" % i) for i in range(2)]
        xo = [sb("xo%d" % i, [128, 1024]) for i in range(2)]; bxo = [Buf("xo%d" % i) for i in range(2)]
        rpA = [sb("rpA%d" % i, [128, 16]) for i in range(2)]; brpA = [Buf("rpA%d" % i) for i in range(2)]
        rpO = [sb("rpO%d" % i, [128, 16]) for i in range(2)]; brpO = [Buf("rpO%d" % i) for i in range(2)]
        cmk = [sb("cmk%d" % i, [128, 2, 128]) for i in range(2)]; bcmk = [Buf("cmk%d" % i) for i in range(2)]
        fbt = [sb("fbt%d" % i, [128, 64]) for i in range(2)]; bfbt = [Buf("fbt%d" % i) for i in range(2)]
        ss = sb("ss", [128, 2]); bss = Buf("ss")
        xn = sb("xn", [128, 1024], BF16); bxn = Buf("xn")
        uT = sb("uT", [128, 8, 256], BF16); buT = Buf("uT")
        uTo = sb("uTo", [128, 8, 128], BF16); buTo = Buf("uTo")
        uTt = sb("uTt", [128, 8, 128], BF16); buTt = Buf("uTt")
        kvr = [sb("kvr%d" % i, [128, 768]) for i in range(2)]; bkvr = [Buf("kvr%d" % i) for i in range(2)]
        kvb = sb("kvb", [128, 768], BF16); bkvb = Buf("kvb")
        xc = sb("xc", [128, 256]); bxc = Buf("xc")
        xcb = sb("xcb", [128, 256], BF16); bxcb = Buf("xcb")
        gr = sb("gr", [128, 256]); gi = sb("gi", [128, 256]); ga = sb("ga", [128, 256]); ga2 = sb("ga2", [128, 256]); bg_ = Buf("gates")
        hsf = sb("hsf", [128, 256]); bhsf = Buf("hsf")
        hsb = sb("hsb", [128, 4, 256], BF16); bhsb = Buf("hsb")
        hidk = sb("hidk", [64, 32], BF16); hidv = sb("hidv", [64, 32], BF16); bhid = Buf("hid")
        kcn = sb("kcn", [16, 2, 64]); kcnb = sb("kcnb", [16, 2, 64], BF16); bkcn = Buf("kcn")
        qf = sb("qf", [128, 512]); bqf = Buf("qf")
        qb = sb("qb", [128, 512], BF16); bqb = Buf("qb")
        QT = sb("QT", [64, 8, 128], BF16); bQT = Buf("QT")
        gns = sb("gns", [128, 24]); bgns = Buf("gns")
        zs = sb("zs", [128, 512]); bzs = Buf("zs")
        zr = sb("zr", [128, 4, 128]); bzr = Buf("zr")
        gm = sb("gm", [128, 2, 512]); bgm = Buf("gm")
        Eb = [sb("Eb%d" % i, [128, 4, 128], BF16) for i in range(2)]; bEb = [Buf("Eb%d" % i) for i in range(2)]
        Pt = [sb("Pt%d" % i, [128, 4, 128], BF16) for i in range(2)]; bPt = [Buf("Pt%d" % i) for i in range(2)]
        mk2 = sb("mk2", [128, 128]); bmk2 = Buf("mk2")
        rden = sb("rden", [128, 4]); coef = sb("coef", [128, 4]); brd = Buf("rden")
        acc = sb("acc", [128, 8, 64]); bacc = Buf("acc")
        tmpo = sb("tmpo", [128, 4, 64]); btmpo = Buf("tmpo")
        impq = sb("impq", [128, 64]); impw = sb("impw", [128, 64]); m8 = sb("m8", [128, 16]); bimp = Buf("imp")
        selb = sb("selb", [128, 64], BF16); bselb = Buf("selb")
        selT = sb("selT", [64, 128], BF16); bselT = Buf("selT")
        ab = sb("ab", [128, 512], BF16); bab = Buf("ab")
        aT = sb("aT", [128, 4, 128], BF16); baT = Buf("aT")
        rt1 = sb("rt1", [128, 4, 128]); brt1 = Buf("rt1")
        rT = sb("rT", [128, 4, 128], BF16); brT = Buf("rT")
        mt1 = sb("mt1", [128, 512]); mt2 = sb("mt2", [128, 512]); bmt = Buf("mt")
        mT = sb("mT", [128, 8, 128], BF16); bmT = Buf("mT")

        ust = {"n": 0}
        cur = {"i": -1}
        dbgt = sb("dbgt", [128, 4, 64]); bdbgt = Buf("dbgt")

        def unit(src_ap, nk):
            sl = ust["n"] % 4
            ust["n"] += 1
            W = 4096 // nk
            view = wu[sl][:].rearrange("p (k c) -> p k c", k=nk)
            S.dma("sp", view, src_ap, reads=[bWbD], writes=[bwu[sl]])
            return view, bwu[sl]

        def norm_rope(x3, P, nh, gb, bg, rope, brope, bx):
            sq = nr_sq[0:P, 0:nh * 64].rearrange("p (h d) -> p h d", h=nh)
            S.op("dve", lambda: V.tensor_tensor(out=sq, in0=x3, in1=x3, op=ALU.mult), [bx], [bnr])
            S.op("dve", lambda: V.tensor_reduce(out=nr_ss[0:P, 0:nh], in_=sq, axis=AX.X, op=ALU.add), [bnr], [bnr])
            S.op("act", lambda: A_.activation(out=nr_ss[0:P, 0:nh], in_=nr_ss[0:P, 0:nh], func=AF.Sqrt, scale=1.0 / 64, bias=EPS), [bnr], [bnr])
            S.op("dve", lambda: V.reciprocal(out=nr_ss[0:P, 0:nh], in_=nr_ss[0:P, 0:nh]), [bnr], [bnr])
            S.op("dve", lambda: V.tensor_tensor(out=x3, in0=x3, in1=bc_last(nr_ss[0:P, 0:nh], P, nh, 64), op=ALU.mult), [bx, bnr], [bx])
            S.op("dve", lambda: V.tensor_tensor(out=x3, in0=x3, in1=bc_mid(gb[0:P, :], P, nh, 64), op=ALU.mult), [bx, bg], [bx])
            x1, x2 = x3[:, :, 0:8], x3[:, :, 8:16]
            cosb, sinb = bc_mid(rope[0:P, 0:8], P, nh, 8), bc_mid(rope[0:P, 8:16], P, nh, 8)
            ta, tb, tc, td = (nr_t[0:P, i, 0:nh, :] for i in range(4))
            S.op("dve", lambda: V.tensor_tensor(out=ta, in0=x1, in1=cosb, op=ALU.mult), [bx, brope], [bnr])
            S.op("dve", lambda: V.tensor_tensor(out=tb, in0=x2, in1=sinb, op=ALU.mult), [bx, brope], [bnr])
            S.op("dve", lambda: V.tensor_tensor(out=tc, in0=x2, in1=cosb, op=ALU.mult), [bx, brope], [bnr])
            S.op("dve", lambda: V.tensor_tensor(out=td, in0=x1, in1=sinb, op=ALU.mult), [bx, brope], [bnr])
            S.op("dve", lambda: V.tensor_tensor(out=x1, in0=ta, in1=tb, op=ALU.subtract), [bnr], [bx])
            S.op("dve", lambda: V.tensor_tensor(out=x2, in0=tc, in1=td, op=ALU.add), [bnr], [bx])

        def load_x(i):
            for tt_ in range(2):
                t = 2 * i + tt_
                S.dma("sp", xt[tt_][:], xb_d[t * 128:(t + 1) * 128, :], writes=[bxt[tt_]])
                S.dma("sp", rpA[tt_][:], ropeA_d[t * 128:(t + 1) * 128, :], writes=[brpA[tt_]])

        def load_own(i):
            sl = i % 2
            S.dma("sp", xo[sl][:], xown_d[i * 128:(i + 1) * 128, :], writes=[bxo[sl]])
            S.dma("sp", rpO[sl][:], ropeO_d[i * 128:(i + 1) * 128, :], writes=[brpO[sl]])
            S.dma("sp", cmk[sl][:], cmaskT_d[i].rearrange("c k q -> k c q"), writes=[bcmk[sl]])
            S.dma("sp", fbt[sl][:], forcedB_d[i], writes=[bfbt[sl]])

        def attn_tile(hk, KT_ap, bK, Vaug_ap, bV, mask_kind, mask_arg, first, last, ncol):
            a = nxt("pst")
            S.op("pe", lambda: T.matmul(pst[a][:], lhsT=KT_ap, rhs=QT[:, 4 * hk:4 * hk + 4, :].rearrange("d h q -> d (h q)"),
                                        start=True, stop=True), [bK, bQT], [bpst[a]])
            S.op("act", lambda: A_.activation(out=Eb[a][:].rearrange("k h q -> k (h q)"), in_=pst[a][:], func=AF.Exp, scale=SCALE),
                 [bpst[a]], [bEb[a]])
            if mask_kind == "sel":
                j, dm = mask_arg
                m = nxt("pmk")
                S.op("pe", lambda: T.matmul(pmk[:, m, 0:128], lhsT=Eall[:, j, :], rhs=selT[:], start=True, stop=True),
                     [bEall, bselT], [bpmk[m]])
                if dm is None:
                    mk_ap, bm = pmk[:, m, 0:128], bpmk[m]
                else:
                    S.op("dve", lambda: V.tensor_tensor(out=mk2[:], in0=pmk[:, m, 0:128], in1=dmask[:, dm, :], op=ALU.mult),
                         [bpmk[m], bdmask], [bmk2])
                    mk_ap, bm = mk2[:], bmk2
            else:
                mk_ap, bm = mask_arg
            S.op("dve", lambda: V.tensor_tensor(out=Pt[a][:], in0=Eb[a][:], in1=bc_mid(mk_ap, 128, 4, 128), op=ALU.mult),
                 [bEb[a], bm], [bPt[a]])
            for h in range(4):
                S.op("pe", lambda h=h: T.matmul(pO[:, h, 0:ncol], lhsT=Pt[a][:, h, :], rhs=Vaug_ap, start=(first and h % 2 == 0), stop=last,
                                                skip_group_check=True), [bPt[a], bV], [bpO])

        def finish_branch(hk, br, first_branch):
            S.op("dve", lambda: V.tensor_scalar(out=rden[:], in0=pO[:, :, 64], scalar1=1e-30, scalar2=None, op0=ALU.max), [bpO], [brd])
            S.op("dve", lambda: V.reciprocal(out=rden[:], in_=rden[:]), [brd], [brd])
            gate = gns[:, hk * 12:(hk + 1) * 12].rearrange("p (h b) -> p h b", b=3)[:, :, br]
            S.op("dve", lambda: V.tensor_tensor(out=coef[:], in0=rden[:], in1=gate, op=ALU.mult), [brd, bgns], [brd])
            dst = acc[:, 4 * hk:4 * hk + 4, :]
            if DBG >= 0 and cur["i"] == DBG:
                S.op("dve", lambda: V.tensor_tensor(out=dbgt[:], in0=pO[:, :, 0:64], in1=bc_last(rden[:], 128, 4, 64), op=ALU.mult), [bpO, brd], [bdbgt])
                S.dma("pool", dbg_out["d_obr"][br, :, hk * 256:(hk + 1) * 256], dbgt[:].rearrange("p h d -> p (h d)"), reads=[bdbgt])
            if first_branch:
                S.op("dve", lambda: V.tensor_tensor(out=dst, in0=pO[:, :, 0:64], in1=bc_last(coef[:], 128, 4, 64), op=ALU.mult),
                     [bpO, brd], [bacc])
            else:
                S.op("dve", lambda: V.tensor_tensor(out=tmpo[:], in0=pO[:, :, 0:64], in1=bc_last(coef[:], 128, 4, 64), op=ALU.mult),
                     [bpO, brd], [btmpo])
                S.op("pool", lambda: G.tensor_tensor(out=dst, in0=dst, in1=tmpo[:], op=ALU.add), [btmpo, bacc], [bacc])

        try:
          ck("setup")
          load_x(0)
          for i in range(npair):
              load_own(i)
              cur["i"] = i
              for tt_ in range(2):
                  S.op("act", lambda: A_.activation(out=xn[:], in_=xt[tt_][:], func=AF.Square, accum_out=ss[:, 0:1]), [bxt[tt_]], [bxn, bss])
                  S.op("act", lambda: A_.activation(out=ss[:, 1:2], in_=ss[:, 0:1], func=AF.Sqrt, scale=1.0 / D_MODEL, bias=EPS), [bss], [bss])
                  S.op("dve", lambda: V.reciprocal(out=ss[:, 1:2], in_=ss[:, 1:2]), [bss], [bss])
                  S.op("pool", lambda: G.tensor_scalar(out=xn[:], in0=xt[tt_][:], scalar1=ss[:, 1:2], scalar2=None, op0=ALU.mult), [bxt[tt_], bss], [bxn])
                  for k in range(8):
                      S.op("pe", lambda: T.transpose(out=ptr[:, k * 128:(k + 1) * 128], in_=xn[:, k * 128:(k + 1) * 128], identity=identb[:]), [bxn, bidb], [bptr])
                  S.op("act", lambda: A_.copy(out=uT[:, :, tt_ * 128:(tt_ + 1) * 128], in_=ptr[:].rearrange("p (k t) -> p k t", k=8)), [bptr], [buT])
              ck("A2")
              for tt_ in range(2):
                  t = 2 * i + tt_
                  ks_ = tt_
                  a = nxt("pmm")
                  for k in range(8):
                      S.op("pe", lambda: T.matmul(pmm[a][:], lhsT=uT[:, k, tt_ * 128:(tt_ + 1) * 128], rhs=WbA[:, k, 0:512],
                                                  start=(k == 0), stop=(k == 7)), [buT, bWbA], [bpmm[a]])
                  S.op("act", lambda: A_.copy(out=kvr[ks_][:, 0:512], in_=pmm[a][:]), [bpmm[a]], [bkvr[ks_]])
                  a = nxt("pmm")
                  for k in range(8):
                      S.op("pe", lambda: T.matmul(pmm[a][:, 0:256], lhsT=uT[:, k, tt_ * 128:(tt_ + 1) * 128], rhs=WbA[:, k, 512:768],
                                                  start=(k == 0), stop=(k == 7)), [buT, bWbA], [bpmm[a]])
                  S.op("act", lambda: A_.copy(out=kvr[ks_][:, 512:768], in_=pmm[a][:, 0:256]), [bpmm[a]], [bkvr[ks_]])
                  ck("A2a")
                  norm_rope(kvr[ks_][:, 256:384].rearrange("p (h d) -> p h d", h=2), 128, 2, gksb, bgks, rpA[tt_], brpA[tt_], bkvr[ks_])
                  norm_rope(kvr[ks_][:, 512:640].rearrange("p (h d) -> p h d", h=2), 128, 2, gkwb, bgkw, rpA[tt_], brpA[tt_], bkvr[ks_])
                  ck("A2b")
                  S.dma("pool", okvc_d[t * 128:(t + 1) * 128, :], kvr[ks_][:, 0:256], reads=[bkvr[ks_]])
                  S.dma("pool", okvs_d[t * 128:(t + 1) * 128, :], kvr[ks_][:, 256:512], reads=[bkvr[ks_]])
                  if t >= NT - 4:
                      S.dma("pool", okvw_d[(t - (NT - 4)) * 128:(t - (NT - 4) + 1) * 128, :], kvr[ks_][:, 512:768], reads=[bkvr[ks_]])
                  ck("A2c")
                  S.op("pool", lambda: G.tensor_copy(out=kvb[:], in_=kvr[ks_][:]), [bkvr[ks_]], [bkvb])
                  wsl = t % 8
                  S.op("pool", lambda: G.tensor_copy(out=VSa[:, t, :, 0:64], in_=kvb[:, 384:512].rearrange("p (h d) -> p h d", h=2)), [bkvb], [bVSa[t]])
                  S.op("pool", lambda: G.tensor_copy(out=VWa[:, wsl, :, 0:64], in_=kvb[:, 640:768].rearrange("p (h d) -> p h d", h=2)), [bkvb], [bVWa[wsl]])
                  ck("A2d")
                  srcs = [0, 64, 128, 192, 256, 320, 512, 576]
                  for n_, c0 in enumerate(srcs):
                      S.op("pe", lambda: T.transpose(out=ptr[0:64, n_ * 128:(n_ + 1) * 128], in_=kvb[:, c0:c0 + 64], identity=identb[:]), [bkvb, bidb], [bptr])
                  ck("A2t")
                  S.op("dve", lambda: V.tensor_copy(out=XTp[:, :, 16 + tt_ * 128:16 + (tt_ + 1) * 128], in_=ptr[0:64, 0:512].rearrange("p (k t) -> p k t", k=4)), [bptr], [bXTp])
                  ck("A2e1")
                  S.op("dve", lambda: V.tensor_copy(out=KST[:, :, t * 128:(t + 1) * 128], in_=ptr[0:64, 512:768].rearrange("p (k t) -> p k t", k=2)), [bptr], [bKST[t]])
                  ck("A2e2")
                  S.op("dve", lambda: V.tensor_copy(out=KWT[:, :, wsl * 128:(wsl + 1) * 128], in_=ptr[0:64, 768:1024].rearrange("p (k t) -> p k t", k=2)), [bptr], [bKWT[wsl]])
                  ck("A2e")
              ck("A3")
              for g in range(4):
                  a = nxt("pmm")
                  for k in range(8):
                      S.op("pe", lambda: T.matmul(pmm[a][:, 0:256], lhsT=WbA[:, k, 768 + g * 128:768 + (g + 1) * 128], rhs=uT[:, k, :],
                                                  start=(k == 0), stop=(k == 7)), [buT, bWbA], [bpmm[a]])
                  S.op("act", lambda: A_.copy(out=xr[:, g, 3:259], in_=pmm[a][:, 0:256]), [bpmm[a]], [bxr])
              if i + 1 < npair:
                  load_x(i + 1)
              for g in range(4):
                  S.op("dve", lambda: V.tensor_scalar(out=xc[:], in0=xr[:, g, 3:259], scalar1=cwT[:, g, 3:4], scalar2=cbT[:, g:g + 1],
                                                      op0=ALU.mult, op1=ALU.add), [bxr, bcw, bcb], [bxc])
                  for tap in range(3):
                      S.op("dve", lambda: V.scalar_tensor_tensor(out=xc[:], in0=xr[:, g, tap:tap + 256], scalar=cwT[:, g, tap:tap + 1],
                                                                 in1=xc[:], op0=ALU.mult, op1=ALU.add), [bxr, bcw, bxc], [bxc])
                  S.op("pool", lambda: G.tensor_copy(out=xcb[:], in_=xc[:]), [bxc], [bxcb])
                  a = nxt("pmm")
                  S.op("pe", lambda: T.matmul(pmm[a][:, 0:256], lhsT=WraB[:, g, :], rhs=xcb[:], start=True, stop=True), [bWr, bxcb], [bpmm[a]])
                  S.op("pe", lambda: T.matmul(pmm[a][:, 256:512], lhsT=WrxB[:, g, :], rhs=xcb[:], start=True, stop=True), [bWr, bxcb], [bpmm[a]])
                  S.op("act", lambda: A_.activation(out=gr[:], in_=pmm[a][:, 0:256], func=AF.Sigmoid, bias=braT[:, g:g + 1]), [bpmm[a], bbr], [bg_])
                  S.op("act", lambda: A_.activation(out=gi[:], in_=pmm[a][:, 256:512], func=AF.Sigmoid, bias=brxT[:, g:g + 1]), [bpmm[a], bbr], [bg_])
                  S.op("act", lambda: A_.activation(out=ga[:], in_=gr[:], func=AF.Exp, scale=clT[:, g:g + 1]), [bg_, bcl], [bg_])
                  S.op("act", lambda: A_.activation(out=ga2[:], in_=gr[:], func=AF.Exp, scale=cl2T[:, g:g + 1]), [bg_, bcl], [bg_])
                  S.op("act", lambda: A_.activation(out=ga2[:], in_=ga2[:], func=AF.Sqrt, scale=-1.0, bias=1.0), [bg_], [bg_])
                  S.op("dve", lambda: V.tensor_tensor(out=gi[:], in0=gi[:], in1=xc[:], op=ALU.mult), [bg_, bxc], [bg_])
                  S.op("dve", lambda: V.tensor_tensor(out=gi[:], in0=gi[:], in1=ga2[:], op=ALU.mult), [bg_], [bg_])
                  S.op("dve", lambda: V.tensor_tensor_scan(out=hsf[:], data0=ga[:], data1=gi[:], initial=hprev[:, g:g + 1], op0=ALU.mult, op1=ALU.add),
                       [bg_, bhprev], [bhsf])
                  S.op("dve", lambda: V.tensor_copy(out=hprev[:, g:g + 1], in_=hsf[:, 255:256]), [bhsf], [bhprev])
                  S.op("pool", lambda: G.tensor_copy(out=hsb[:, g, :], in_=hsf[:]), [bhsf], [bhsb])
              if i == npair - 1:
                  for t3 in range(3):
                      S.dma("pool", None, None, reads=[bxr], fn=lambda: nc.gpsimd.dma_start(
                          out=oconv_d[t3].rearrange("(g c) -> c g", c=128), in_=xr[:, :, 256 + t3], allow_slow_non_contiguous=True))
                  S.dma("pool", None, None, reads=[bhprev], fn=lambda: nc.gpsimd.dma_start(
                      out=oh_d.rearrange("(g c) -> c g", c=128), in_=hprev[:], allow_slow_non_contiguous=True))
              S.op("pool", lambda: G.tensor_copy(out=xr[:, :, 0:3], in_=xr[:, :, 256:259]), [bxr], [bxr])
              ck("A4")
              m0 = 1 if i == 0 else 0
              for kind, (w1s, hid, pbt) in enumerate(((W1k, hidk, pbk), (W1v, hidv, pbv))):
                  a = nxt("pmm")
                  for h in range(2):
                      for j in range(32):
                          S.op("pe", lambda: T.matmul(pmm[a][0:64, h * 16:(h + 1) * 16], lhsT=w1s[:, j, :],
                                                      rhs=XTp[:, kind * 2 + h, j:j + 241:16], start=(j == 0), stop=(j == 31)),
                               [bW1, bXTp], [bpmm[a]])
                  S.op("act", lambda: A_.activation(out=hid[:], in_=pmm[a][0:64, 0:32], func=AF.Silu, bias=pbt[:, 0:1]), [bpmm[a], bpb], [bhid])
              S.op("pool", lambda: G.tensor_copy(out=XTp[:, :, 0:16], in_=XTp[:, :, 256:272]), [bXTp], [bXTp])
              a = nxt("pmm")
              for h in range(2):
                  S.op("pe", lambda: T.matmul(pmm[a][0:16, h * 64:(h + 1) * 64], lhsT=hidk[:, h * 16:(h + 1) * 16], rhs=W2k[:], start=True, stop=True),
                       [bhid, bW2], [bpmm[a]])
              S.op("act", lambda: A_.copy(out=kcn[:].rearrange("p h d -> p (h d)"), in_=pmm[a][0:16, 0:128]), [bpmm[a]], [bkcn])
              norm_rope(kcn[:], 16, 2, gkcb, bgkc, ropeCt[:, i, :], bropeC, bkcn)
              S.op("pool", lambda: G.tensor_copy(out=kcnb[:], in_=kcn[:]), [bkcn], [bkcn])
              for h in range(2):
                  S.op("pe", lambda: T.transpose(out=ptr[0:64, h * 16:(h + 1) * 16], in_=kcnb[:, h, :], identity=identb[0:16, 0:16]), [bkcn, bidb], [bptr])
              c0 = 16 * i - 1 + m0
              S.op("dve", lambda: V.tensor_copy(out=KCT[:, :, c0:16 * i + 15], in_=ptr[0:64, 0:32].rearrange("p (h m) -> p h m", h=2)[:, :, m0:16]), [bptr], [bKCT])
              a = nxt("pmm")
              S.op("pe", lambda: T.matmul(pmm[a][0:64, 0:32], lhsT=W2v[:], rhs=hidv[:], start=True, stop=True), [bhid, bW2], [bpmm[a]])
              S.op("act", lambda: A_.copy(out=VCT[:, :, c0:16 * i + 15], in_=pmm[a][0:64, 0:32].rearrange("p (h m) -> p h m", h=2)[:, :, m0:16]), [bpmm[a]], [bVCT])
              for ch in range(2):
                  for h in range(2):
                      n_ = ch * 2 + h
                      S.op("pe", lambda: T.transpose(out=ptr[:, n_ * 64:(n_ + 1) * 64], in_=VCT[:, h, ch * 128:(ch + 1) * 128], identity=identb[0:64, 0:64]),
                           [bVCT, bidb], [bptr])
              S.op("dve", lambda: V.tensor_copy(out=VCA[:, :, :, 0:64], in_=ptr[:, 0:256].rearrange("p (c h d) -> p c h d", c=2, h=2)), [bptr], [bVCA])
              ck("B1")
              so = i % 2
              S.op("dve", lambda: V.tensor_scalar(out=uTt[:], in0=uT[:, :, 0:128], scalar1=selcol[:, 0:1], scalar2=None, op0=ALU.mult), [buT, bselcol], [buTt])
              S.op("dve", lambda: V.scalar_tensor_tensor(out=uTo[:], in0=uT[:, :, 128:256], scalar=selcol[:, 1:2], in1=uTt[:], op0=ALU.mult, op1=ALU.add),
                   [buT, bselcol, buTt], [buTo])
              ck("B2")
              wq, bwq = unit(WbD[:, :, CQ:CQ + 512], 8)
              a = nxt("pmm")
              for k in range(8):
                  S.op("pe", lambda: T.matmul(pmm[a][:], lhsT=uTo[:, k, :], rhs=wq[:, k, :], start=(k == 0), stop=(k == 7)), [buTo, bwq], [bpmm[a]])
              S.op("act", lambda: A_.copy(out=qf[:], in_=pmm[a][:]), [bpmm[a]], [bqf])
              norm_rope(qf[:].rearrange("p (h d) -> p h d", h=8), 128, 8, gqb, bgq, rpO[so], brpO[so], bqf)
              S.op("pool", lambda: G.tensor_copy(out=qb[:], in_=qf[:]), [bqf], [bqb])
              for h in range(8):
                  S.op("pe", lambda: T.transpose(out=ptr[0:64, h * 128:(h + 1) * 128], in_=qb[:, h * 64:(h + 1) * 64], identity=identb[:]), [bqb, bidb], [bptr])
              S.op("act", lambda: A_.copy(out=QT[:].rearrange("d h q -> d (h q)"), in_=ptr[0:64, :]), [bptr], [bQT])
              a = nxt("pmm")
              for k in range(8):
                  S.op("pe", lambda: T.matmul(pmm[a][:, 0:24], lhsT=uTo[:, k, :], rhs=Wgn[:, k, :], start=(k == 0), stop=(k == 7)), [buTo, bWgn], [bpmm[a]])
              S.op("act", lambda: A_.activation(out=gns[:], in_=pmm[a][:, 0:24], func=AF.Sigmoid), [bpmm[a]], [bgns])
              wz, bwz = unit(WbD[:, :, CZN:CZN + 512], 8)
              a = nxt("pmm")
              for k in range(8):
                  S.op("pe", lambda: T.matmul(pmm[a][:], lhsT=uTo[:, k, :], rhs=wz[:, k, :], start=(k == 0), stop=(k == 7)), [buTo, bwz], [bpmm[a]])
              S.op("act", lambda: A_.activation(out=zs[:], in_=pmm[a][:], func=AF.Silu), [bpmm[a]], [bzs])
              wzr, bwzr = unit(WbD[:, :, CZR:CZR + 512], 8)
              a = nxt("pmm")
              for g in range(4):
                  for k in range(8):
                      S.op("pe", lambda: T.matmul(pmm[a][:, g * 128:(g + 1) * 128], lhsT=wzr[:, k, g * 128:(g + 1) * 128], rhs=uTo[:, k, :],
                                                  start=(k == 0), stop=(k == 7)), [buTo, bwzr], [bpmm[a]])
              S.op("act", lambda: A_.activation(out=zr[:].rearrange("p g t -> p (g t)"), in_=pmm[a][:], func=AF.Silu), [bpmm[a]], [bzr])
              ck("B4B5")
              for hk in range(2):
                  for ch in range(2):
                      attn_tile(hk, KCT[:, hk, ch * 128:(ch + 1) * 128], bKCT, VCA[:, ch, hk, :], bVCA, "tab", (cmk[so][:, ch, :], bcmk[so]), ch == 0, ch == 1, 129)
                  S.op("dve", lambda: V.tensor_scalar(out=rden[:], in0=pO[:, :, 64], scalar1=1e-30, scalar2=None, op0=ALU.max), [bpO], [brd])
                  S.op("dve", lambda: V.reciprocal(out=rden[:], in_=rden[:]), [brd], [brd])
                  for h in range(4):
                      src1 = fbt[so][:] if h == 0 else impq[:]
                      S.op("dve", lambda: V.scalar_tensor_tensor(out=impq[:], in0=pO[:, h, 65:129], scalar=rden[:, h:h + 1], in1=src1, op0=ALU.mult, op1=ALU.add),
                           [bpO, brd, bfbt[so], bimp], [bimp])
                  finish_branch(hk, 0, True)
                  S.op("dve", lambda: V.max(out=m8[:, 0:8], in_=impq[:]), [bimp], [bimp])
                  S.op("dve", lambda: V.match_replace(out=impw[:], in_to_replace=m8[:, 0:8], in_values=impq[:], imm_value=-1e30), [bimp], [bimp])
                  S.op("dve", lambda: V.max(out=m8[:, 8:16], in_=impw[:]), [bimp], [bimp])
                  S.op("dve", lambda: V.tensor_scalar(out=selb[:], in0=impq[:], scalar1=m8[:, 15:16], scalar2=None, op0=ALU.is_ge), [bimp], [bselb])
                  if DBG == i:
                      S.op("dve", lambda: V.tensor_copy(out=dbgt[:, 0, :], in_=selb[:]), [bselb], [bdbgt])
                      S.dma("pool", dbg_out["d_sel"][hk], dbgt[:, 0, :], reads=[bdbgt])
                      S.dma("pool", dbg_out["d_imp"][hk], impq[:], reads=[bimp])
                  S.op("pe", lambda: T.transpose(out=ptr[0:64, 0:128], in_=selb[:], identity=identb[:]), [bselb, bidb], [bptr])
                  S.op("act", lambda: A_.copy(out=selT[:], in_=ptr[0:64, 0:128]), [bptr], [bselT])
                  nj = 2 * i + 2
                  for j in range(nj):
                      dm = None if j < 2 * i else j - 2 * i
                      attn_tile(hk, KST[:, hk, j * 128:(j + 1) * 128], bKST[j], VSa[:, j, hk, :], bVSa[j], "sel", (j, dm), j == 0, j == nj - 1, 65)
                  finish_branch(hk, 1, False)
                  js = [(m, 2 * i - 4 + m) for m in range(6) if 2 * i - 4 + m >= 0]
                  for n_, (m, j) in enumerate(js):
                      wsl = j % 8
                      attn_tile(hk, KWT[:, hk, wsl * 128:(wsl + 1) * 128], bKWT[wsl], VWa[:, wsl, hk, :], bVWa[wsl], "tab", (wmask[:, m, :], bwmask),
                                n_ == 0, n_ == len(js) - 1, 65)
                  finish_branch(hk, 2, False)
              ck("B6")
              S.op("dve", lambda: V.tensor_tensor(out=ab[:], in0=acc[:].rearrange("p h d -> p (h d)"), in1=zs[:], op=ALU.mult), [bacc, bzs], [bab])
              for k in range(4):
                  S.op("pe", lambda: T.transpose(out=ptr[:, k * 128:(k + 1) * 128], in_=ab[:, k * 128:(k + 1) * 128], identity=identb[:]), [bab, bidb], [bptr])
              S.op("act", lambda: A_.copy(out=aT[:].rearrange("p k t -> p (k t)"), in_=ptr[:, 0:512]), [bptr], [baT])
              S.op("pool", lambda: G.tensor_scalar(out=rt1[:], in0=hsb[:, :, 0:128], scalar1=selcol[:, 0:1], scalar2=None, op0=ALU.mult), [bhsb, bselcol], [brt1])
              S.op("dve", lambda: V.scalar_tensor_tensor(out=rt1[:], in0=hsb[:, :, 128:256], scalar=selcol[:, 1:2], in1=rt1[:], op0=ALU.mult, op1=ALU.add),
                   [bhsb, bselcol, brt1], [brt1])
              if DBG == i:
                  S.dma("pool", dbg_out["d_acc"], acc[:].rearrange("p h d -> p (h d)"), reads=[bacc])
                  S.dma("pool", dbg_out["d_hs"], rt1[:].rearrange("p g t -> p (g t)"), reads=[brt1])
              S.op("pool", lambda: G.tensor_tensor(out=rT[:], in0=rt1[:], in1=zr[:], op=ALU.mult), [brt1, bzr], [brT])
              ck("B7")
              wpa, bwpa = None, None
              for rnd in range(2):
                  for ab_ in range(2):
                      wg, bwg = unit(WbD[:, :, CGM + ab_ * 1024 + rnd * 512:CGM + ab_ * 1024 + (rnd + 1) * 512], 8)
                      a = nxt("pmm")
                      for g in range(4):
                          for k in range(8):
                              S.op("pe", lambda: T.matmul(pmm[a][:, g * 128:(g + 1) * 128], lhsT=wg[:, k, g * 128:(g + 1) * 128], rhs=uTo[:, k, :],
                                                          start=(k == 0), stop=(k == 7)), [buTo, bwg], [bpmm[a]])
                      S.op("act", lambda: A_.activation(out=gm[:, ab_, :], in_=pmm[a][:], func=AF.Sigmoid), [bpmm[a]], [bgm])
                  if rnd == 0:
                      wpa, bwpa = unit(WpaD, 4)
                      wpb, bwpb = unit(WpbD, 4)
                  a = nxt("pmm")
                  for g in range(4):
                      fc = rnd * 4 + g
                      for k in range(4):
                          S.op("pe", lambda: T.matmul(pmm[a][:, g * 128:(g + 1) * 128], lhsT=wpa[:, k, fc * 128:(fc + 1) * 128], rhs=aT[:, k, :],
                                                      start=(k == 0), stop=(k == 3)), [baT, bwpa], [bpmm[a]])
                  S.op("dve", lambda: V.tensor_tensor(out=mt1[:], in0=pmm[a][:], in1=gm[:, 0, :], op=ALU.mult), [bpmm[a], bgm], [bmt])
                  a = nxt("pmm")
                  for g in range(4):
                      fc = rnd * 4 + g
                      for k in range(4):
                          S.op("pe", lambda: T.matmul(pmm[a][:, g * 128:(g + 1) * 128], lhsT=wpb[:, k, fc * 128:(fc + 1) * 128], rhs=rT[:, k, :],
                                                      start=(k == 0), stop=(k == 3)), [brT, bwpb], [bpmm[a]])
                  S.op("dve", lambda: V.tensor_tensor(out=mt2[:], in0=pmm[a][:], in1=gm[:, 1, :], op=ALU.mult), [bpmm[a], bgm], [bmt])
                  S.op("pool", lambda: G.tensor_tensor(out=mT[:, rnd * 4:(rnd + 1) * 4, :].rearrange("p g t -> p (g t)"), in0=mt1[:], in1=mt2[:], op=ALU.add), [bmt], [bmT])
              wo0, bwo0 = unit(WoutD[:, 0:4, :], 4)
              wo1, bwo1 = unit(WoutD[:, 4:8, :], 4)
              for hf in range(2):
                  a = nxt("pmm")
                  for k in range(8):
                      wsrc = wo0 if k < 4 else wo1
                      S.op("pe", lambda: T.matmul(pmm[a][:], lhsT=mT[:, k, :], rhs=wsrc[:, k % 4, hf * 512:(hf + 1) * 512], start=(k == 0), stop=(k == 7)),
                           [bmT, bwo0, bwo1], [bpmm[a]])
                  S.op("dve", lambda: V.tensor_tensor(out=xo[so][:, hf * 512:(hf + 1) * 512], in0=pmm[a][:], in1=xo[so][:, hf * 512:(hf + 1) * 512], op=ALU.add),
                       [bpmm[a], bxo[so]], [bxo[so]])
              S.dma("pool", oy_d[i * 128:(i + 1) * 128, :], xo[so][:], reads=[bxo[so]])

        except StopBuild:
            pass
        S.barrier()
        esP.close()
        esS = ExitStack()
        cur_es[0] = esS
        NS = 16

        def unitw(src_ap, nk, w):
            sl = ust["n"] % 4
            ust["n"] += 1
            view = wu[sl][:].rearrange("p (k c) -> p k c", k=nk)[:, :, 0:w]
            S.dma("sp", view, src_ap, reads=[bWbD], writes=[bwu[sl]])
            return view, bwu[sl]

        def norm_rope2(x3, P, nh, gb, bg, cosb, sinb, brope, bx):
            sq = nr_sq[0:P, 0:nh * 64].rearrange("p (h d) -> p h d", h=nh)
            S.op("dve", lambda: V.tensor_tensor(out=sq, in0=x3, in1=x3, op=ALU.mult), [bx], [bnr])
            S.op("dve", lambda: V.tensor_reduce(out=nr_ss[0:P, 0:nh], in_=sq, axis=AX.X, op=ALU.add), [bnr], [bnr])
            S.op("act", lambda: A_.activation(out=nr_ss[0:P, 0:nh], in_=nr_ss[0:P, 0:nh], func=AF.Sqrt, scale=1.0 / 64, bias=EPS), [bnr], [bnr])
            S.op("dve", lambda: V.reciprocal(out=nr_ss[0:P, 0:nh], in_=nr_ss[0:P, 0:nh]), [bnr], [bnr])
            S.op("dve", lambda: V.tensor_tensor(out=x3, in0=x3, in1=bc_last(nr_ss[0:P, 0:nh], P, nh, 64), op=ALU.mult), [bx, bnr], [bx])
            S.op("dve", lambda: V.tensor_tensor(out=x3, in0=x3, in1=bc_mid(gb[0:P, :], P, nh, 64), op=ALU.mult), [bx, bg], [bx])
            x1, x2 = x3[:, :, 0:8], x3[:, :, 8:16]
            ta, tb, tc, td = (nr_t[0:P, i, 0:nh, :] for i in range(4))
            S.op("dve", lambda: V.tensor_tensor(out=ta, in0=x1, in1=cosb, op=ALU.mult), [bx, brope], [bnr])
            S.op("dve", lambda: V.tensor_tensor(out=tb, in0=x2, in1=sinb, op=ALU.mult), [bx, brope], [bnr])
            S.op("dve", lambda: V.tensor_tensor(out=tc, in0=x2, in1=cosb, op=ALU.mult), [bx, brope], [bnr])
            S.op("dve", lambda: V.tensor_tensor(out=td, in0=x1, in1=sinb, op=ALU.mult), [bx, brope], [bnr])
            S.op("dve", lambda: V.tensor_tensor(out=x1, in0=ta, in1=tb, op=ALU.subtract), [bnr], [bx])
            S.op("dve", lambda: V.tensor_tensor(out=x2, in0=tc, in1=td, op=ALU.add), [bnr], [bx])

        pS = sb("pS", [16, D_IN]); bpS = Buf("pS")
        xs = sb("xs", [16, 1024]); bxs = Buf("xs")
        xsn = sb("xsn", [16, 1024], BF16); bxsn = Buf("xsn")
        ss2 = sb("ss2", [16, 2]); bss2 = Buf("ss2")
        uTs = sb("uTs", [128, 8, 16], BF16); buTs = Buf("uTs")
        rpS = sb("rpS", [16, 16]); brpS = Buf("rpS")
        qTs = sb("qTs", [128, 8, 16], BF16); bqTs = Buf("qTs")
        knT = sb("knT", [64, 4, 16], BF16); bknT = Buf("knT")
        Vsn = sb("Vsn", [16, 2, 65], BF16); Vwn = sb("Vwn", [16, 2, 65], BF16); bVn = Buf("Vn")
        tb16 = sb("tb16", [16, 1024], BF16); btb16 = Buf("tb16")
        tf16 = sb("tf16", [16, 2048]); btf16 = Buf("tf16")
        idxi = sb("idxi", [128, 64], I32); idxf = sb("idxf", [128, 64]); pm8 = sb("pm8", [128, 1]); bidx = Buf("idx")
        Gt = [sb("Gt%d" % i, [128, 4096]) for i in range(2)]; bGt = [Buf("Gt%d" % i) for i in range(2)]
        Es = sb("Es", [128, 256]); bEs = Buf("Es")
        Pts = sb("Pts", [128, 256], BF16); bPts = Buf("Pts")
        Oc = sb("Oc", [4, 2, 194]); bOc = Buf("Oc")
        Os = sb("Os", [4, 2, 65]); bOs = Buf("Os")
        Ow = sb("Ow", [4, 2, 65]); bOw = Buf("Ow")
        Oc32 = sb("Oc32", [32, 4, 194]); Os32 = sb("Os32", [32, 4, 65]); Ow32 = sb("Ow32", [32, 4, 65]); bO32 = Buf("O32")
        fS = sb("fS", [32, 129]); bfS = Buf("fS")
        imp32 = sb("imp32", [32, 136]); impw32 = sb("impw32", [32, 136]); m8s = sb("m8s", [32, 16]); bimp32 = Buf("imp32")
        rd32 = sb("rd32", [32, 3, 4]); brd32 = Buf("rd32")
        sel4 = sb("sel4", [32, 128, 4]); bsel4 = Buf("sel4")
        selM = sb("selM", [128, 32, 4]); bselM = Buf("selM")
        RQ = sb("RQ", [128, 32, 8], BF16); bRQ = Buf("RQ")
        Pn = sb("Pn", [16, 4], BF16); Pnf = sb("Pnf", [16, 4]); bPn = Buf("Pn")
        g32 = sb("g32", [32, 12]); z32 = sb("z32", [32, 256]); acc32 = sb("acc32", [32, 4, 64]); tmp32 = sb("tmp32", [32, 4, 64]); coef32 = sb("coef32", [32, 4]); b32 = Buf("b32")
        a16 = sb("a16", [16, 512]); ba16 = Buf("a16")
        aTs = sb("aTs", [128, 4, 16], BF16); baTs = Buf("aTs")
        rTs = sb("rTs", [128, 4, 16], BF16); brTs = Buf("rTs")
        mTs = sb("mTs", [128, 8, 16], BF16); bmTs = Buf("mTs")
        cwB = sb("cwB", [16, 4, 512]); cbB = sb("cbB", [16, 512]); scv = sb("scv", [16, 3, 512]); bcwB = Buf("cwB")
        xcs = sb("xcs", [16, 512]); bxcs = Buf("xcs")
        xcT = sb("xcT", [128, 4, 16]); xcTb = sb("xcTb", [128, 4, 16], BF16); bxcT = Buf("xcT")
        h0T = sb("h0T", [128, 4, 16]); hnT = sb("hnT", [128, 4, 16]); bhT = Buf("hT")
        zrT = sb("zrT", [128, 4, 16]); bzrT = Buf("zrT")
        gs = [sb("gs%d" % i, [128, 16]) for i in range(4)]; bgs = Buf("gs")
        bscr = {n: Buf(n) for n in ("Oc", "Os", "Ow", "Sel", "G", "Z", "A")}

        esC = ExitStack()
        cur_es[0] = esC
        Gb = sb("Gb", [128, 4, 16, 64], BF16); bGb = Buf("Gb")
        YT = sb("YT", [128, 4, 8, 512], BF16); bYT = Buf("YT")
        W1p = [sb("W1p%d" % i, [128, 16, 64], BF16) for i in range(2)]; bW1p = Buf("W1p")
        hidS = [sb("hidS%d" % i, [64, 2, 512], BF16) for i in range(2)]; bhidS = Buf("hidS")
        kcs = sb("kcs", [128, 8, 64]); bkcs = Buf("kcs")
        kcsb = sb("kcsb", [128, 8, 64], BF16); bkcsb = Buf("kcsb")
        KCTs = sb("KCTs", [64, 8, 128], BF16); bKCTs = Buf("KCTs")
        VCAs = sb("VCAs", [128, 4, 2, 194], BF16); bVCAs = Buf("VCAs")
        ropeCs = sb("ropeCs", [128, 4, 16]); bropeCs = Buf("ropeCs")
        cmS = sb("cmS", [128, 4]); bcmS = Buf("cmS")
        S.dma("sp", xs[:], xs_d, writes=[bxs])
        S.dma("sp", rpS[:], ropeS_d, writes=[brpS])
        S.op("act", lambda: A_.activation(out=xsn[:], in_=xs[:], func=AF.Square, accum_out=ss2[:, 0:1]), [bxs], [bxsn, bss2])
        S.op("act", lambda: A_.activation(out=ss2[:, 1:2], in_=ss2[:, 0:1], func=AF.Sqrt, scale=1.0 / D_MODEL, bias=EPS), [bss2], [bss2])
        S.op("dve", lambda: V.reciprocal(out=ss2[:, 1:2], in_=ss2[:, 1:2]), [bss2], [bss2])
        S.op("dve", lambda: V.tensor_scalar(out=xsn[:], in0=xs[:], scalar1=ss2[:, 1:2], scalar2=None, op0=ALU.mult), [bxs, bss2], [bxsn])
        for k in range(8):
            S.op("pe", lambda: T.transpose(out=ptr[:, k * 16:(k + 1) * 16], in_=xsn[:, k * 128:(k + 1) * 128], identity=identb[0:16, 0:16]), [bxsn, bidb], [bptr])
        S.op("act", lambda: A_.copy(out=uTs[:].rearrange("p k s -> p (k s)"), in_=ptr[:, 0:128]), [bptr], [buTs])
        for u in range(10):
            c0 = u * 512
            w = min(512, D_IN - c0)
            wv, bwv = unitw(WbD[:, :, c0:c0 + w], 8, w)
            a = nxt("pmm")
            for k in range(8):
                S.op("pe", lambda: T.matmul(pmm[a][0:16, 0:w], lhsT=uTs[:, k, :], rhs=wv[:, k, :], start=(k == 0), stop=(k == 7)), [buTs, bwv], [bpmm[a]])
            S.op("act", lambda: A_.copy(out=pS[:, c0:c0 + w], in_=pmm[a][0:16, 0:w]), [bpmm[a]], [bpS])
        cosS, sinS = rpS[:, 0:8], rpS[:, 8:16]
        norm_rope2(pS[:, 0:512].rearrange("p (h d) -> p h d", h=8), 16, 8, gqb, bgq, bc_mid(cosS, 16, 8, 8), bc_mid(sinS, 16, 8, 8), brpS, bpS)
        norm_rope2(pS[:, CKV + 256:CKV + 384].rearrange("p (h d) -> p h d", h=2), 16, 2, gksb, bgks, bc_mid(cosS, 16, 2, 8), bc_mid(sinS, 16, 2, 8), brpS, bpS)
        norm_rope2(pS[:, CKV + 512:CKV + 640].rearrange("p (h d) -> p h d", h=2), 16, 2, gkwb, bgkw, bc_mid(cosS, 16, 2, 8), bc_mid(sinS, 16, 2, 8), brpS, bpS)
        S.dma("pool", okvcs_d, pS[:, CKV:CKV + 256], reads=[bpS])
        S.dma("pool", okvss_d, pS[:, CKV + 256:CKV + 512], reads=[bpS])
        S.dma("pool", okvws_d[:, 511, :], pS[:, CKV + 512:CKV + 768], reads=[bpS])
        S.dma("pool", okvws_d[:, 0:511, :], ckw_d[:, 1:512, :])
        S.dma("pool", oconvs_d[:, 0:2, :], sconv_d[:, 1:3, :])
        S.dma("pool", oconvs_d[:, 2, :], pS[:, CXR:CXR + 512], reads=[bpS])
        S.op("pool", lambda: G.tensor_copy(out=tb16[:, 0:512], in_=pS[:, 0:512]), [bpS], [btb16])
        for h in range(8):
            S.op("pe", lambda: T.transpose(out=ptr[0:64, h * 16:(h + 1) * 16], in_=tb16[:, h * 64:(h + 1) * 64], identity=identb[0:16, 0:16]), [btb16, bidb], [bptr])
        S.op("act", lambda: A_.copy(out=qTs[0:64, :, :].rearrange("p h s -> p (h s)"), in_=ptr[0:64, 0:128]), [bptr], [bqTs])
        S.op("dve", lambda: V.tensor_copy(out=qTs[64:128, :, :], in_=qTs[0:64, :, :]), [bqTs], [bqTs])
        S.op("pool", lambda: G.tensor_copy(out=tb16[:, 512:640], in_=pS[:, CKV + 256:CKV + 384]), [bpS], [btb16])
        S.op("pool", lambda: G.tensor_copy(out=tb16[:, 640:768], in_=pS[:, CKV + 512:CKV + 640]), [bpS], [btb16])
        for n_ in range(4):
            S.op("pe", lambda: T.transpose(out=ptr[0:64, n_ * 16:(n_ + 1) * 16], in_=tb16[:, 512 + n_ * 64:512 + (n_ + 1) * 64], identity=identb[0:16, 0:16]), [btb16, bidb], [bptr])
        S.op("act", lambda: A_.copy(out=knT[:].rearrange("p n s -> p (n s)"), in_=ptr[0:64, 0:64]), [bptr], [bknT])
        S.op("pool", lambda: G.memset(Vsn[:], 1.0), [], [bVn])
        S.op("pool", lambda: G.memset(Vwn[:], 1.0), [], [bVn])
        S.op("pool", lambda: G.tensor_copy(out=Vsn[:, :, 0:64], in_=pS[:, CKV + 384:CKV + 512].rearrange("p (h d) -> p h d", h=2)), [bpS], [bVn])
        S.op("pool", lambda: G.tensor_copy(out=Vwn[:, :, 0:64], in_=pS[:, CKV + 640:CKV + 768].rearrange("p (h d) -> p h d", h=2)), [bpS], [bVn])
        S.dma("sp", None, None, writes=[bcwB], fn=lambda: nc.sync.dma_start(out=cwB[:].rearrange("p t c -> p (t c)"), in_=conv_w_d.rearrange("t c -> (t c)").partition_broadcast(16)))
        S.dma("sp", None, None, writes=[bcwB], fn=lambda: nc.sync.dma_start(out=cbB[:], in_=conv_b_d.partition_broadcast(16)))
        S.dma("sp", scv[:], sconv_d, writes=[bcwB])
        S.op("dve", lambda: V.tensor_tensor(out=xcs[:], in0=pS[:, CXR:CXR + 512], in1=cwB[:, 3, :], op=ALU.mult), [bpS, bcwB], [bxcs])
        S.op("dve", lambda: V.tensor_tensor(out=xcs[:], in0=xcs[:], in1=cbB[:], op=ALU.add), [bxcs, bcwB], [bxcs])
        for tap in range(3):
            S.op("dve", lambda: V.tensor_tensor(out=tf16[:, 0:512], in0=scv[:, tap, :], in1=cwB[:, tap, :], op=ALU.mult), [bcwB], [btf16])
            S.op("dve", lambda: V.tensor_tensor(out=xcs[:], in0=xcs[:], in1=tf16[:, 0:512], op=ALU.add), [bxcs, btf16], [bxcs])
        a = nxt("pmm")
        for g in range(4):
            S.op("pe", lambda: T.transpose(out=pmm[a][:, g * 16:(g + 1) * 16], in_=xcs[:, g * 128:(g + 1) * 128], identity=identf[0:16, 0:16]), [bxcs, bidf], [bpmm[a]])
        S.op("act", lambda: A_.copy(out=xcT[:].rearrange("p g s -> p (g s)"), in_=pmm[a][:, 0:64]), [bpmm[a]], [bxcT])
        S.op("dve", lambda: V.tensor_copy(out=xcTb[:], in_=xcT[:]), [bxcT], [bxcT])
        for g in range(4):
            S.dma("sp", None, None, writes=[bhT], fn=lambda: nc.sync.dma_start(out=h0T[:, g, :], in_=sh_d[:, g * 128:(g + 1) * 128].rearrange("s c -> c s"), allow_slow_non_contiguous=True))
        for g in range(4):
            a = nxt("pmm")
            S.op("pe", lambda: T.matmul(pmm[a][:, 0:16], lhsT=WraB[:, g, :], rhs=xcTb[:, g, :], start=True, stop=True), [bWr, bxcT], [bpmm[a]])
            S.op("pe", lambda: T.matmul(pmm[a][:, 16:32], lhsT=WrxB[:, g, :], rhs=xcTb[:, g, :], start=True, stop=True), [bWr, bxcT], [bpmm[a]])
            S.op("act", lambda: A_.activation(out=gs[0][:], in_=pmm[a][:, 0:16], func=AF.Sigmoid, bias=braT[:, g:g + 1]), [bpmm[a], bbr], [bgs])
            S.op("act", lambda: A_.activation(out=gs[1][:], in_=pmm[a][:, 16:32], func=AF.Sigmoid, bias=brxT[:, g:g + 1]), [bpmm[a], bbr], [bgs])
            S.op("act", lambda: A_.activation(out=gs[2][:], in_=gs[0][:], func=AF.Exp, scale=clT[:, g:g + 1]), [bgs, bcl], [bgs])
            S.op("act", lambda: A_.activation(out=gs[3][:], in_=gs[0][:], func=AF.Exp, scale=cl2T[:, g:g + 1]), [bgs, bcl], [bgs])
            S.op("act", lambda: A_.activation(out=gs[3][:], in_=gs[3][:], func=AF.Sqrt, scale=-1.0, bias=1.0), [bgs], [bgs])
            S.op("dve", lambda: V.tensor_tensor(out=gs[1][:], in0=gs[1][:], in1=xcT[:, g, :], op=ALU.mult), [bgs, bxcT], [bgs])
            S.op("dve", lambda: V.tensor_tensor(out=gs[1][:], in0=gs[1][:], in1=gs[3][:], op=ALU.mult), [bgs], [bgs])
            S.op("dve", lambda: V.tensor_tensor(out=gs[2][:], in0=gs[2][:], in1=h0T[:, g, :], op=ALU.mult), [bgs, bhT], [bgs])
            S.op("dve", lambda: V.tensor_tensor(out=hnT[:, g, :], in0=gs[2][:], in1=gs[1][:], op=ALU.add), [bgs], [bhT])
        for g in range(4):
            S.dma("pool", None, None, reads=[bhT], fn=lambda: nc.gpsimd.dma_start(out=ohs_d[:, g * 128:(g + 1) * 128].rearrange("s c -> c s"), in_=hnT[:, g, :], allow_slow_non_contiguous=True))
        S.op("act", lambda: A_.activation(out=tf16[:, 0:512], in_=pS[:, CZR:CZR + 512], func=AF.Silu), [bpS], [btf16])
        a = nxt("pmm")
        for g in range(4):
            S.op("pe", lambda: T.transpose(out=pmm[a][:, g * 16:(g + 1) * 16], in_=tf16[:, g * 128:(g + 1) * 128], identity=identf[0:16, 0:16]), [btf16, bidf], [bpmm[a]])
        S.op("dve", lambda: V.tensor_tensor(out=rTs[:].rearrange("p g s -> p (g s)"), in0=pmm[a][:, 0:64], in1=hnT[:].rearrange("p g s -> p (g s)"), op=ALU.mult), [bpmm[a], bhT], [brTs])
        for kind, wd in enumerate((w1_k_d, w1_v_d)):
            S.dma("sp", Gt[0][:, 0:1024].rearrange("p (c e) -> p c e", c=16), wd.rearrange("(c a) d e -> (a d) c e", a=2), writes=[bGt[0]])
            S.op("dve", lambda: V.tensor_copy(out=W1p[kind][:].rearrange("p c e -> p (c e)"), in_=Gt[0][:, 0:1024]), [bGt[0]], [bW1p])
        S.op("pool", lambda: G.memset(VCAs[:], 1.0), [], [bVCAs])
        for ch in range(4):
            S.dma("sp", Gt[1][:, 0:129], As_d[ch * 128:(ch + 1) * 128, :], writes=[bGt[1]])
            for hk in range(2):
                S.op("dve", lambda: V.tensor_copy(out=VCAs[:, ch, hk, 65:194], in_=Gt[1][:, 0:129]), [bGt[1]], [bVCAs])
        S.dma("sp", ropeCs[:], ropeC_d[1:513, :].rearrange("(c p) e -> p c e", p=128), writes=[bropeCs])
        S.dma("sp", cmS[:], cmS_d, writes=[bcmS])
        S.dma("sp", fS[:], fS_d, writes=[bfS])
        S.dma("sp", idxi[:], ptrep_d, writes=[bidx])
        S.dma("sp", pm8[:], pm8_d, writes=[bidx])
        S.op("dve", lambda: V.tensor_copy(out=idxf[:], in_=idxi[:]), [bidx], [bidx])
        S.op("dve", lambda: V.tensor_scalar(out=idxf[:], in0=idxf[:], scalar1=8.0, scalar2=None, op0=ALU.mult), [bidx], [bidx])
        S.op("dve", lambda: V.tensor_scalar(out=idxf[:], in0=idxf[:], scalar1=pm8[:, 0:1], scalar2=None, op0=ALU.add), [bidx], [bidx])
        S.op("dve", lambda: V.tensor_copy(out=idxi[:], in_=idxf[:]), [bidx], [bidx])
        S.op("pool", lambda: G.memset(hidS[0][:], 0.0), [], [bhidS])
        S.op("pool", lambda: G.memset(hidS[1][:], 0.0), [], [bhidS])
        gcount = {"n": 0}

        def gather(pool_d, s, t):
            sl = gcount["n"] % 2
            gcount["n"] += 1
            col = s * 4 + t
            S.dma("pool", None, None, reads=[bidx], writes=[bGt[sl]], fn=lambda: nc.gpsimd.indirect_dma_start(
                out=Gt[sl][:], out_offset=None, in_=pool_d, in_offset=bass.IndirectOffsetOnAxis(ap=idxi[:, col:col + 1], axis=0)))
            return Gt[sl], bGt[sl]

        for s in range(ns_seq):
            for t in range(4):
                gt_, bgt_ = gather(poolc_d, s, t)
                src = gt_[:].rearrange("p (j k d) -> p k j d", j=16, k=4)
                S.op("dve", lambda: V.tensor_copy(out=Gb[:, 0:2, :, :], in_=src[:, 0:2, :, :]), [bgt_], [bGb])
                S.op("act", lambda: A_.copy(out=Gb[:, 2:4, :, :], in_=src[:, 2:4, :, :]), [bgt_], [bGb])
                for kvh in range(4):
                    for jc in range(8):
                        S.op("pe", lambda: T.transpose(out=ptr[:, jc * 128:(jc + 1) * 128], in_=Gb[:, kvh, 2 * jc:2 * jc + 2, :].rearrange("p j d -> p (j d)"), identity=identb[:]), [bGb, bidb], [bptr])
                    eng = "act" if kvh % 2 == 0 else "dve"
                    if eng == "act":
                        S.op("act", lambda: A_.copy(out=YT[:, kvh, :, t * 128:(t + 1) * 128], in_=ptr[:].rearrange("p (c s) -> p c s", c=8)), [bptr], [bYT])
                    else:
                        S.op("dve", lambda: V.tensor_copy(out=YT[:, kvh, :, t * 128:(t + 1) * 128], in_=ptr[:].rearrange("p (c s) -> p c s", c=8)), [bptr], [bYT])
            for kind in range(2):
                for h in range(2):
                    kvh = kind * 2 + h
                    a = nxt("pst")
                    for jc in range(8):
                        S.op("pe", lambda: T.matmul(pst[a][0:64, 0:511], lhsT=W1p[kind][:, jc, :], rhs=YT[:, kvh, jc, 0:511], start=(jc == 0), stop=False), [bW1p, bYT], [bpst[a]])
                    for jc in range(8):
                        S.op("pe", lambda: T.matmul(pst[a][0:64, 0:511], lhsT=W1p[kind][:, 8 + jc, :], rhs=YT[:, kvh, jc, 1:512], start=False, stop=(jc == 7)), [bW1p, bYT], [bpst[a]])
                    pbt = pbk if kind == 0 else pbv
                    S.op("act", lambda: A_.activation(out=hidS[kind][:, h, 0:511], in_=pst[a][0:64, 0:511], func=AF.Silu, bias=pbt[:, 0:1]), [bpst[a], bpb], [bhidS])
            a = nxt("pmm")
            for h in range(2):
                for ch in range(4):
                    n_ = h * 4 + ch
                    S.op("pe", lambda: T.matmul(pmm[a][:, n_ * 64:(n_ + 1) * 64], lhsT=hidS[0][:, h, ch * 128:(ch + 1) * 128], rhs=W2k[:], start=True, stop=True), [bhidS, bW2], [bpmm[a]])
            S.op("act", lambda: A_.copy(out=kcs[:].rearrange("p n d -> p (n d)"), in_=pmm[a][:]), [bpmm[a]], [bkcs])
            for h in range(2):
                norm_rope2(kcs[:, h * 4:(h + 1) * 4, :], 128, 4, gkcb, bgkc, ropeCs[:, :, 0:8], ropeCs[:, :, 8:16], bropeCs, bkcs)
            S.op("pool", lambda: G.tensor_copy(out=kcsb[:], in_=kcs[:]), [bkcs], [bkcsb])
            for n_ in range(8):
                S.op("pe", lambda: T.transpose(out=ptr[0:64, n_ * 128:(n_ + 1) * 128], in_=kcsb[:, n_, :], identity=identb[:]), [bkcsb, bidb], [bptr])
            S.op("act", lambda: A_.copy(out=KCTs[:].rearrange("p n c -> p (n c)"), in_=ptr[0:64, :]), [bptr], [bKCTs])
            a = nxt("pmm")
            for h in range(2):
                for ch in range(4):
                    n_ = h * 4 + ch
                    S.op("pe", lambda: T.matmul(pmm[a][:, n_ * 64:(n_ + 1) * 64], lhsT=hidS[1][:, h, ch * 128:(ch + 1) * 128], rhs=W2v[:], start=True, stop=True), [bhidS, bW2], [bpmm[a]])
            S.op("dve", lambda: V.tensor_copy(out=VCAs[:, :, :, 0:64], in_=pmm[a][:].rearrange("p (h c d) -> p c h d", h=2, c=4)), [bpmm[a]], [bVCAs])
            for hk in range(2):
                for ch in range(4):
                    S.op("pe", lambda: T.matmul(pmk[:, 0, ch * 4:(ch + 1) * 4], lhsT=KCTs[:, hk * 4 + ch, :], rhs=qTs[0:64, 4 * hk:4 * hk + 4, s], start=True, stop=True), [bKCTs, bqTs], [bpmk[0]])
                S.op("act", lambda: A_.activation(out=Es[:, 0:16], in_=pmk[:, 0, 0:16], func=AF.Exp, scale=SCALE), [bpmk[0]], [bEs])
                S.op("dve", lambda: V.tensor_tensor(out=Pts[:, 0:16].rearrange("p (c h) -> p c h", c=4), in0=Es[:, 0:16].rearrange("p (c h) -> p c h", c=4),
                                                    in1=bc_last(cmS[:], 128, 4, 4), op=ALU.mult), [bEs, bcmS], [bPts])
                for ch in range(4):
                    S.op("pe", lambda: T.matmul(pO[0:4, 0, 0:194], lhsT=Pts[:, ch * 4:(ch + 1) * 4], rhs=VCAs[:, ch, hk, :], start=(ch == 0), stop=(ch == 3)), [bPts, bVCAs], [bpO])
                S.op("act", lambda: A_.copy(out=Oc[:, hk, :], in_=pO[0:4, 0, 0:194]), [bpO], [bOc])
            S.dma("sp", scrOc[:, 2 * s:2 * s + 2, :], Oc[:], reads=[bOc], writes=[bscr["Oc"]])
        S.dma("sp", Oc32[:], scrOc.rearrange("h q e -> q h e"), reads=[bscr["Oc"]], writes=[bO32])
        S.op("dve", lambda: V.tensor_scalar(out=rd32[:, 0, :], in0=Oc32[:, :, 64], scalar1=1e-30, scalar2=None, op0=ALU.max), [bO32], [brd32])
        S.op("dve", lambda: V.reciprocal(out=rd32[:, 0, :], in_=rd32[:, 0, :]), [brd32], [brd32])
        S.op("pool", lambda: G.memset(imp32[:], -1e30), [], [bimp32])
        for h in range(4):
            src1 = fS[:] if h == 0 else imp32[:, 0:129]
            S.op("dve", lambda: V.scalar_tensor_tensor(out=imp32[:, 0:129], in0=Oc32[:, h, 65:194], scalar=rd32[:, 0, h:h + 1], in1=src1, op0=ALU.mult, op1=ALU.add),
                 [bO32, brd32, bfS, bimp32], [bimp32])
        S.op("dve", lambda: V.max(out=m8s[:, 0:8], in_=imp32[:]), [bimp32], [bimp32])
        S.op("dve", lambda: V.match_replace(out=impw32[:], in_to_replace=m8s[:, 0:8], in_values=imp32[:], imm_value=-1e30), [bimp32], [bimp32])
        S.op("dve", lambda: V.max(out=m8s[:, 8:16], in_=impw32[:]), [bimp32], [bimp32])
        S.op("dve", lambda: V.tensor_scalar(out=impw32[:, 0:128], in0=imp32[:, 0:128], scalar1=m8s[:, 15:16], scalar2=None, op0=ALU.is_ge), [bimp32], [bimp32])
        S.op("dve", lambda: V.tensor_copy(out=sel4[:], in_=bc_last(impw32[:, 0:128], 32, 128, 4)), [bimp32], [bsel4])
        S.dma("sp", scrSel, sel4[:].rearrange("q b r -> q (b r)"), reads=[bsel4], writes=[bscr["Sel"]])
        for hf in range(2):
            S.dma("sp", None, None, reads=[bscr["Sel"]], writes=[bselM], fn=lambda: nc.sync.dma_start(
                out=selM[:, hf * 16:(hf + 1) * 16, :], in_=scrSel[hf * 16:(hf + 1) * 16, :].rearrange("q (t p) -> p q t", p=128), allow_slow_non_contiguous=True))
        S.barrier()
        esC.close()
        cur_es[0] = esS
        Kb = sb("Kb", [128, 2, 16, 64], BF16); bKb = Buf("Kb")
        Vs = sb("Vs", [128, 4, 16, 2, 65], BF16); bVs = Buf("Vs")
        KTs = sb("KTs", [128, 8, 128], BF16); bKTs = Buf("KTs")
        Wt = sb("Wt", [128, 4, 256]); bWt = Buf("Wt")
        Kwb = sb("Kwb", [128, 4, 2, 64], BF16); bKwb = Buf("Kwb")
        Vw = sb("Vw", [128, 4, 2, 65], BF16); bVw = Buf("Vw")
        KwT = sb("KwT", [64, 8, 128], BF16); bKwT = Buf("KwT")
        S.op("pool", lambda: G.memset(RQ[:], 0.0), [], [bRQ])
        for hk in range(2):
            S.op("dve", lambda: V.tensor_copy(out=RQ[0:64, :, :].rearrange("p (s k) e -> p s k e", k=2)[:, :, hk, 0:4], in_=qTs[0:64, 4 * hk:4 * hk + 4, :].rearrange("p h s -> p s h")), [bqTs], [bRQ])
            S.op("dve", lambda: V.tensor_copy(out=RQ[64:128, :, :].rearrange("p (s k) e -> p s k e", k=2)[:, :, hk, 4:8], in_=qTs[64:128, 4 * hk:4 * hk + 4, :].rearrange("p h s -> p s h")), [bqTs], [bRQ])
        S.op("pool", lambda: G.memset(Vs[:], 1.0), [], [bVs])
        for s in range(ns_seq):
            for t in range(4):
                gt_, bgt_ = gather(pools_d, s, t)
                src = gt_[:].rearrange("p (j k d) -> p k j d", j=16, k=4)
                S.op("dve", lambda: V.tensor_copy(out=Kb[:], in_=src[:, 0:2, :, :]), [bgt_], [bKb])
                S.op("act", lambda: A_.copy(out=Vs[:, t, :, :, 0:64], in_=gt_[:].rearrange("p (j k d) -> p j k d", j=16, k=4)[:, :, 2:4, :]), [bgt_], [bVs])
                for hk in range(2):
                    for jp in range(8):
                        S.op("pe", lambda: T.transpose(out=ptr[:, jp * 128:(jp + 1) * 128], in_=Kb[:, hk, 2 * jp:2 * jp + 2, :].rearrange("p j d -> p (j d)"), identity=identb[:]), [bKb, bidb], [bptr])
                    S.op("act", lambda: A_.copy(out=KTs[:].rearrange("p c s -> p (c s)"), in_=ptr[:]), [bptr], [bKTs])
                    for jp in range(8):
                        c0 = (t * 8 + jp) * 8
                        S.op("pe", lambda: T.matmul(pst[hk][:, c0:c0 + 8], lhsT=KTs[:, jp, :], rhs=RQ[:, s * 2 + hk, :], start=True, stop=True), [bKTs, bRQ], [bpst[hk]])
            for hk in range(2):
                q_ = s * 2 + hk
                S.op("act", lambda: A_.activation(out=Es[:], in_=pst[hk][:, 0:256], func=AF.Exp, scale=SCALE), [bpst[hk]], [bEs])
                S.op("dve", lambda: V.tensor_tensor(out=Pts[:].rearrange("p (t c) -> p t c", t=4), in0=Es[:].rearrange("p (t c) -> p t c", t=4),
                                                    in1=bc_last(selM[:, q_, :], 128, 4, 64), op=ALU.mult), [bEs, bselM], [bPts])
                first = True
                for t in range(4):
                    for jp in range(8):
                        for a2 in range(2):
                            c0 = ((t * 8 + jp) * 2 + a2) * 4
                            S.op("pe", lambda: T.matmul(pO[0:4, 0, 0:65], lhsT=Pts[:, c0:c0 + 4], rhs=Vs[:, t, 2 * jp + a2, hk, :], start=first, stop=False,
                                                        skip_group_check=True), [bPts, bVs], [bpO])
                            first = False
                S.op("pe", lambda: T.matmul(pmk[0:16, 0, 0:4], lhsT=knT[:, hk, :], rhs=qTs[0:64, 4 * hk:4 * hk + 4, s], start=True, stop=True), [bknT, bqTs], [bpmk[0]])
                S.op("act", lambda: A_.activation(out=Pnf[:], in_=pmk[0:16, 0, 0:4], func=AF.Exp, scale=SCALE), [bpmk[0]], [bPn])
                S.op("dve", lambda: V.tensor_scalar(out=Pn[:], in0=Pnf[:], scalar1=identf[0:16, s:s + 1], scalar2=None, op0=ALU.mult), [bPn, bidf], [bPn])
                S.op("pe", lambda: T.matmul(pO[0:4, 0, 0:65], lhsT=Pn[:], rhs=Vsn[:, hk, :], start=False, stop=True, skip_group_check=True), [bPn, bVn], [bpO])
                S.op("act", lambda: A_.copy(out=Os[:, hk, :], in_=pO[0:4, 0, 0:65]), [bpO], [bOs])
            S.dma("sp", scrOs[:, 2 * s:2 * s + 2, :], Os[:], reads=[bOs], writes=[bscr["Os"]])
        S.op("pool", lambda: G.memset(Vw[:], 1.0), [], [bVw])
        for s in range(ns_seq):
            S.dma("sp", Wt[:], ckw_d[s].rearrange("(t p) e -> p t e", p=128), writes=[bWt])
            S.op("dve", lambda: V.tensor_copy(out=Kwb[:], in_=Wt[:, :, 0:128].rearrange("p t (h d) -> p t h d", h=2)), [bWt], [bKwb])
            S.op("pool", lambda: G.tensor_copy(out=Vw[:, :, :, 0:64], in_=Wt[:, :, 128:256].rearrange("p t (h d) -> p t h d", h=2)), [bWt], [bVw])
            for t in range(4):
                for hk in range(2):
                    n_ = t * 2 + hk
                    S.op("pe", lambda: T.transpose(out=ptr[0:64, n_ * 128:(n_ + 1) * 128], in_=Kwb[:, t, hk, :], identity=identb[:]), [bKwb, bidb], [bptr])
            S.op("act", lambda: A_.copy(out=KwT[:].rearrange("p n c -> p (n c)"), in_=ptr[0:64, :]), [bptr], [bKwT])
            for hk in range(2):
                for t in range(4):
                    c0 = (hk * 4 + t) * 4
                    S.op("pe", lambda: T.matmul(pmk[:, 1, c0:c0 + 4], lhsT=KwT[:, t * 2 + hk, :], rhs=qTs[0:64, 4 * hk:4 * hk + 4, s], start=True, stop=True), [bKwT, bqTs], [bpmk[1]])
            S.op("act", lambda: A_.activation(out=Pts[:, 0:32], in_=pmk[:, 1, 0:32], func=AF.Exp, scale=SCALE), [bpmk[1]], [bPts])
            for hk in range(2):
                q_ = s * 2 + hk
                for t in range(4):
                    c0 = (hk * 4 + t) * 4
                    S.op("pe", lambda: T.matmul(pO[0:4, 0, 0:65], lhsT=Pts[:, c0:c0 + 4], rhs=Vw[:, t, hk, :], start=(t == 0), stop=False, skip_group_check=True), [bPts, bVw], [bpO])
                S.op("pe", lambda: T.matmul(pmk[0:16, 0, 0:4], lhsT=knT[:, 2 + hk, :], rhs=qTs[0:64, 4 * hk:4 * hk + 4, s], start=True, stop=True), [bknT, bqTs], [bpmk[0]])
                S.op("act", lambda: A_.activation(out=Pnf[:], in_=pmk[0:16, 0, 0:4], func=AF.Exp, scale=SCALE), [bpmk[0]], [bPn])
                S.op("dve", lambda: V.tensor_scalar(out=Pn[:], in0=Pnf[:], scalar1=identf[0:16, s:s + 1], scalar2=None, op0=ALU.mult), [bPn, bidf], [bPn])
                S.op("pe", lambda: T.matmul(pO[0:4, 0, 0:65], lhsT=Pn[:], rhs=Vwn[:, hk, :], start=False, stop=True, skip_group_check=True), [bPn, bVn], [bpO])
                S.op("act", lambda: A_.copy(out=Ow[:, hk, :], in_=pO[0:4, 0, 0:65]), [bpO], [bOw])
            S.dma("sp", scrOw[:, 2 * s:2 * s + 2, :], Ow[:], reads=[bOw], writes=[bscr["Ow"]])
        S.dma("sp", Os32[:], scrOs.rearrange("h q e -> q h e"), reads=[bscr["Os"]], writes=[bO32])
        S.dma("sp", Ow32[:], scrOw.rearrange("h q e -> q h e"), reads=[bscr["Ow"]], writes=[bO32])
        S.op("act", lambda: A_.activation(out=tf16[:, 0:24], in_=pS[:, CGN:CGN + 24], func=AF.Sigmoid), [bpS], [btf16])
        S.dma("sp", scrG, tf16[:, 0:24], reads=[btf16], writes=[bscr["G"]])
        S.dma("sp", g32[:], scrG.rearrange("s (k e) -> (s k) e", k=2), reads=[bscr["G"]], writes=[b32])
        S.op("act", lambda: A_.activation(out=tf16[:, 512:1024], in_=pS[:, CZN:CZN + 512], func=AF.Silu), [bpS], [btf16])
        S.dma("sp", scrZ, tf16[:, 512:1024], reads=[btf16], writes=[bscr["Z"]])
        S.dma("sp", z32[:], scrZ.rearrange("s (k e) -> (s k) e", k=2), reads=[bscr["Z"]], writes=[b32])
        for br, O32 in enumerate((Oc32, Os32, Ow32)):
            if br > 0:
                S.op("dve", lambda: V.tensor_scalar(out=rd32[:, br, :], in0=O32[:, :, 64], scalar1=1e-30, scalar2=None, op0=ALU.max), [bO32], [brd32])
                S.op("dve", lambda: V.reciprocal(out=rd32[:, br, :], in_=rd32[:, br, :]), [brd32], [brd32])
            S.op("dve", lambda: V.tensor_tensor(out=coef32[:], in0=rd32[:, br, :], in1=g32[:].rearrange("q (h b) -> q h b", b=3)[:, :, br], op=ALU.mult), [brd32, b32], [b32])
            if br == 0:
                S.op("dve", lambda: V.tensor_tensor(out=acc32[:], in0=O32[:, :, 0:64], in1=bc_last(coef32[:], 32, 4, 64), op=ALU.mult), [bO32, b32], [b32])
            else:
                S.op("dve", lambda: V.tensor_tensor(out=tmp32[:], in0=O32[:, :, 0:64], in1=bc_last(coef32[:], 32, 4, 64), op=ALU.mult), [bO32, b32], [b32])
                S.op("dve", lambda: V.tensor_tensor(out=acc32[:], in0=acc32[:], in1=tmp32[:], op=ALU.add), [b32], [b32])
        S.op("dve", lambda: V.tensor_tensor(out=acc32[:].rearrange("q h d -> q (h d)"), in0=acc32[:].rearrange("q h d -> q (h d)"), in1=z32[:], op=ALU.mult), [b32], [b32])
        S.dma("sp", scrA, acc32[:].rearrange("q h d -> q (h d)"), reads=[b32], writes=[bscr["A"]])
        S.dma("sp", a16[:], scrA.rearrange("(s k) e -> s (k e)", k=2), reads=[bscr["A"]], writes=[ba16])
        S.op("dve", lambda: V.tensor_copy(out=tb16[:, 0:512], in_=a16[:]), [ba16], [btb16])
        for k in range(4):
            S.op("pe", lambda: T.transpose(out=ptr[:, k * 16:(k + 1) * 16], in_=tb16[:, k * 128:(k + 1) * 128], identity=identb[0:16, 0:16]), [btb16, bidb], [bptr])
        S.op("act", lambda: A_.copy(out=aTs[:].rearrange("p k s -> p (k s)"), in_=ptr[:, 0:64]), [bptr], [baTs])
        S.op("act", lambda: A_.activation(out=tf16[:], in_=pS[:, CGM:CGM + 2048], func=AF.Sigmoid), [bpS], [btf16])
        wpa, bwpa = unit(WpaD, 4)
        wpb, bwpb = unit(WpbD, 4)
        mS = xs_m = sb("mS", [16, 1024]); bmS = Buf("mS")
        t1 = sb("t1s", [16, 512]); bt1 = Buf("t1s")
        for hf in range(2):
            a = nxt("pmm")
            for k in range(4):
                S.op("pe", lambda: T.matmul(pmm[a][0:16, :], lhsT=aTs[:, k, :], rhs=wpa[:, k, hf * 512:(hf + 1) * 512], start=(k == 0), stop=(k == 3)), [baTs, bwpa], [bpmm[a]])
            S.op("dve", lambda: V.tensor_tensor(out=t1[:], in0=pmm[a][0:16, :], in1=tf16[:, hf * 512:(hf + 1) * 512], op=ALU.mult), [bpmm[a], btf16], [bt1])
            a = nxt("pmm")
            for k in range(4):
                S.op("pe", lambda: T.matmul(pmm[a][0:16, :], lhsT=rTs[:, k, :], rhs=wpb[:, k, hf * 512:(hf + 1) * 512], start=(k == 0), stop=(k == 3)), [brTs, bwpb], [bpmm[a]])
            S.op("dve", lambda: V.tensor_tensor(out=mS[:, hf * 512:(hf + 1) * 512], in0=pmm[a][0:16, :], in1=tf16[:, 1024 + hf * 512:1024 + (hf + 1) * 512], op=ALU.mult), [bpmm[a], btf16], [bmS])
            S.op("dve", lambda: V.tensor_tensor(out=mS[:, hf * 512:(hf + 1) * 512], in0=mS[:, hf * 512:(hf + 1) * 512], in1=t1[:], op=ALU.add), [bmS, bt1], [bmS])
        S.op("dve", lambda: V.tensor_copy(out=tb16[:], in_=mS[:]), [bmS], [btb16])
        for k in range(8):
            S.op("pe", lambda: T.transpose(out=ptr[:, k * 16:(k + 1) * 16], in_=tb16[:, k * 128:(k + 1) * 128], identity=identb[0:16, 0:16]), [btb16, bidb], [bptr])
        S.op("act", lambda: A_.copy(out=mTs[:].rearrange("p k s -> p (k s)"), in_=ptr[:, 0:128]), [bptr], [bmTs])
        wo0, bwo0 = unit(WoutD[:, 0:4, :], 4)
        wo1, bwo1 = unit(WoutD[:, 4:8, :], 4)
        for hf in range(2):
            a = nxt("pmm")
            for k in range(8):
                wsrc = wo0 if k < 4 else wo1
                S.op("pe", lambda: T.matmul(pmm[a][0:16, :], lhsT=mTs[:, k, :], rhs=wsrc[:, k % 4, hf * 512:(hf + 1) * 512], start=(k == 0), stop=(k == 7)), [bmTs, bwo0, bwo1], [bpmm[a]])
            S.op("dve", lambda: V.tensor_tensor(out=xs[:, hf * 512:(hf + 1) * 512], in0=pmm[a][0:16, :], in1=xs[:, hf * 512:(hf + 1) * 512], op=ALU.add), [bpmm[a], bxs], [bxs])
        S.dma("pool", oys_d, xs[:], reads=[bxs])
        S.finish("sp")
        esS.close()
        S.finish("sp")
    S.close()
    return nc, S


_WEIGHT_NAMES = ["w_in", "g_norm", "g_q", "g_kc", "g_ks", "g_kw", "pe_k", "w1_k", "w2_k", "pe_v", "w1_v", "w2_v",
                 "conv_w", "conv_b", "w_ra", "b_ra", "w_rx", "b_rx", "lam", "w_pa", "w_pb", "w_out"]


def kernel(**inputs):
    inp = {k: np.asarray(v) for k, v in inputs.items()}
    x_prompt = inp["x_prompt"]
    shared = _shared_tables()
    nc, S = build_program()
    print('ninst', S.ninst, flush=True)
    in_maps = []
    poolc = np.ascontiguousarray(inp["cache_kv_cmp"], dtype=np.float32).reshape(81920, 4096)
    pools = np.ascontiguousarray(inp["cache_kv_sel"], dtype=np.float32).reshape(81920, 4096)
    pt = np.asarray(inp["page_table"]).astype(np.int32)
    As = np.zeros((512, 129), np.float32)
    for j in range(129):
        for cc in range(4 * j - 1, 4 * j + 4):
            if 0 <= cc <= 510:
                As[cc, j] = 1.0
    cmS = (np.arange(4)[None, :] * 128 + np.arange(128)[:, None] <= 510).astype(np.float32)
    fS = np.zeros((32, 129), np.float32)
    fS[:, [0, 127, 128]] = 1.0e4
    pm8 = (np.arange(128) % 8).astype(np.float32).reshape(128, 1)
    for c in range(8):
        b, p = c // 2, c % 2
        m = {}
        sl = slice(16 * c, 16 * c + 16)
        m["xs"] = np.ascontiguousarray(inp["x_sample"][sl, 0, :], dtype=np.float32)
        ptc = pt[sl]
        m["ptrep"] = np.ascontiguousarray(np.repeat(ptc.reshape(16, 4, 16), 8, axis=2).transpose(2, 0, 1).reshape(128, 64))
        m["pm8"] = pm8
        m["poolc"] = poolc
        m["pools"] = pools
        m["ckw"] = np.ascontiguousarray(inp["cache_kv_win"][sl], dtype=np.float32).reshape(16, 512, 256)
        m["sconv"] = np.ascontiguousarray(inp["state_conv"][sl], dtype=np.float32)
        m["sh"] = np.ascontiguousarray(inp["state_h"][sl], dtype=np.float32)
        m["As"] = As
        m["cmS"] = cmS
        m["fS"] = fS
        xb = np.ascontiguousarray(x_prompt[b])
        m["xb"] = xb
        m["xown"] = np.ascontiguousarray(xb.reshape(NT, 128, D_MODEL)[p::2].reshape(SEQ // 2, D_MODEL))
        for n in _WEIGHT_NAMES:
            m[n] = np.ascontiguousarray(inp[n], dtype=np.float32)
        m.update(shared)
        m.update(_core_tables(p))
        in_maps.append(m)
    res = run_bass_kernel_spmd(nc, in_maps, core_ids=list(range(8)))
    R = res.results
    B = 4
    y_prompt = np.zeros((B, SEQ, D_MODEL), np.float32)
    kv_cmp_p = np.zeros((B, SEQ, 2, 2, 64), np.float32)
    kv_sel_p = np.zeros((B, SEQ, 2, 2, 64), np.float32)
    kv_win_p = np.zeros((B, 512, 2, 2, 64), np.float32)
    conv_p = np.zeros((B, 3, 512), np.float32)
    h_p = np.zeros((B, 512), np.float32)
    for c in range(8):
        b, p = c // 2, c % 2
        y_prompt[b].reshape(NT, 128, D_MODEL)[p::2] = R[c]["oy"].reshape(NPAIR, 128, D_MODEL)
        if p == 0:
            kv_cmp_p[b] = R[c]["okvc"].reshape(SEQ, 2, 2, 64)
            kv_sel_p[b] = R[c]["okvs"].reshape(SEQ, 2, 2, 64)
            kv_win_p[b] = R[c]["okvw"].reshape(512, 2, 2, 64)
            conv_p[b] = R[c]["oconv"]
            h_p[b] = R[c]["oh"]
    DB = 128
    y_sample = np.concatenate([R[c]["oys"] for c in range(8)], 0).reshape(DB, 1, D_MODEL)
    kv_cmp_s = np.concatenate([R[c]["okvcs"] for c in range(8)], 0).reshape(DB, 1, 2, 2, 64)
    kv_sel_s = np.concatenate([R[c]["okvss"] for c in range(8)], 0).reshape(DB, 1, 2, 2, 64)
    kv_win_s = np.concatenate([R[c]["okvws"] for c in range(8)], 0).reshape(DB, 512, 2, 2, 64)
    conv_s = np.concatenate([R[c]["oconvs"] for c in range(8)], 0).reshape(DB, 3, 512)
    h_s = np.concatenate([R[c]["ohs"] for c in range(8)], 0).reshape(DB, 512)
    return (y_prompt, y_sample, kv_cmp_p, kv_cmp_s, kv_sel_p, kv_sel_s, kv_win_p, kv_win_s, conv_p, conv_s, h_p, h_s)
```

```python
import numpy as np
from contextlib import ExitStack
import concourse.bass as bass
import concourse.mybir as mybir
from concourse.bass_utils import run_bass_kernel_spmd

F32 = mybir.dt.float32
BF16 = mybir.dt.bfloat16
I32 = mybir.dt.int32
AF = mybir.ActivationFunctionType
ALU = mybir.AluOpType
AX = mybir.AxisListType

D_MODEL = 1024
SEQ = 4096
NT = 32
NPAIR = 16
EPS = 1e-6
SCALE = 0.125
CQ, CKV, CGN, CZN, CXR, CZR, CGM = 0, 512, 1280, 1304, 1816, 2328, 2840
D_IN = 4888
ROPE_THETA = 500000.0
PAST = 8192


class StopBuild(Exception):
    pass


def ck(name):
    import os
    if os.environ.get("STOP", "") == name:
        raise StopBuild(name)


class Buf:
    __slots__ = ("name", "w", "r")

    def __init__(self, name):
        self.name = name
        self.w = None
        self.r = []


class Sched:
    def __init__(self, nc, ndma=40):
        self.nc = nc
        self.eng = {"pe": nc.tensor, "act": nc.scalar, "dve": nc.vector,
                    "pool": nc.gpsimd, "sp": nc.sync}
        self.sem, self.cnt, self._cms = {}, {}, []
        for k in self.eng:
            cm = nc.semaphore("s_" + k)
            self._cms.append(cm)
            self.sem[k] = cm.__enter__()
            self.cnt[k] = 0
        self.dsem, self.dcnt = [], []
        for i in range(ndma):
            cm = nc.semaphore("d%d" % i)
            self._cms.append(cm)
            self.dsem.append(cm.__enter__())
            self.dcnt.append(0)
        self.dnext = 0
        self.seen = {k: {} for k in self.eng}
        self.pend = {}
        self.ninst = 0

    def close(self):
        for cm in reversed(self._cms):
            cm.__exit__(None, None, None)

    def _wait(self, e, ev):
        if ev is None:
            return
        key, val = ev
        if key == "pe" and e == "pe":
            return
        if isinstance(key, str) and val == self.cnt[key] + 1:
            self._materialize(key)
        if self.seen[e].get(key, 0) >= val:
            return
        sem = self.sem[key] if isinstance(key, str) else self.dsem[key]
        self.eng[e].wait_ge(sem, val)
        self.seen[e][key] = val

    def _deps(self, e, reads, writes):
        for b in reads:
            self._wait(e, b.w)
        for b in writes:
            self._wait(e, b.w)
            for ev in b.r:
                self._wait(e, ev)

    def _commit(self, ev, reads, writes):
        for b in reads:
            b.r.append(ev)
            if len(b.r) > 8:
                best = {}
                for k, v in b.r:
                    if best.get(k, 0) < v:
                        best[k] = v
                b.r = list(best.items())
        for b in writes:
            b.w = ev
            b.r = []

    def _materialize(self, key):
        ins = self.pend.get(key)
        if ins is not None:
            self.cnt[key] += 1
            ins.then_inc(self.sem[key], 1)
            self.pend[key] = None

    def op(self, e, fn, reads=(), writes=(), inc=True):
        self._deps(e, reads, writes)
        ins = fn()
        self.pend[e] = ins
        self._commit((e, self.cnt[e] + 1), reads, writes)
        self.ninst += 1
        return ins

    def dma(self, q, out, in_, reads=(), writes=(), fn=None):
        self._deps(q, reads, writes)
        slot = self.dnext
        self.dnext = (self.dnext + 1) % len(self.dsem)
        if self.dcnt[slot] > 0:
            self._wait(q, (slot, self.dcnt[slot]))
        ins = self.eng[q].dma_start(out=out, in_=in_) if fn is None else fn()
        self.dcnt[slot] += 16
        ins.then_inc(self.dsem[slot], 16)
        ev = (slot, self.dcnt[slot])
        self._commit(ev, reads, writes)
        self.ninst += 1
        return ev

    def _all_events(self):
        for k in self.eng:
            self._materialize(k)
        evs = [(k, self.cnt[k]) for k in self.eng if self.cnt[k] > 0]
        evs += [(i, self.dcnt[i]) for i in range(len(self.dsem)) if self.dcnt[i] > 0]
        return evs

    def barrier(self):
        evs = self._all_events()
        for e in self.eng:
            for ev in evs:
                self._wait(e, ev)

    def finish(self, e="sp"):
        for ev in self._all_events():
            self._wait(e, ev)


def _rope_tab(pos):
    half = 8
    inv = ROPE_THETA ** (-np.arange(half, dtype=np.float32) / half)
    ang = np.asarray(pos, np.float32)[:, None] * inv[None, :].astype(np.float32)
    return np.concatenate([np.cos(ang), np.sin(ang)], axis=1).astype(np.float32)


def _core_tables(p):
    t = {}
    own_tiles = [2 * i + p for i in range(NPAIR)]
    pos_own = np.concatenate([np.arange(g * 128, g * 128 + 128) for g in own_tiles])
    t["ropeO"] = _rope_tab(pos_own)
    c = np.arange(256)
    cend = 16 * c + 31
    cm = np.zeros((NPAIR, 2, 128, 128), np.float32)
    fb = np.zeros((NPAIR, 128, 64), np.float32)
    for i, g in enumerate(own_tiles):
        pos = g * 128 + np.arange(128)
        m = (cend[:, None] <= pos[None, :]) & (c[:, None] < 255)
        cm[i] = m.reshape(2, 128, 128)
        cur = pos // 64
        blk = np.arange(64)
        f = (blk[None, :] == 0) | (blk[None, :] == cur[:, None]) | (blk[None, :] == cur[:, None] - 1)
        fb[i] = np.where(f, 1.0e4, 0.0)
    t["cmaskT"] = cm
    t["forcedB"] = fb
    r = np.arange(128)
    lower = (r[:, None] <= r[None, :]).astype(np.float32)
    upper = (r[:, None] >= r[None, :]).astype(np.float32)
    ones = np.ones((128, 128), np.float32)
    zeros = np.zeros((128, 128), np.float32)
    if p == 0:
        t["dmask"] = np.stack([lower, zeros])
        t["wmask"] = np.stack([upper, ones, ones, ones, lower, zeros])
    else:
        t["dmask"] = np.stack([ones, lower])
        t["wmask"] = np.stack([zeros, upper, ones, ones, ones, lower])
    sc = np.zeros((128, 2), np.float32)
    sc[:, p] = 1.0
    t["selcol"] = sc
    return t


def _shared_tables():
    t = {}
    t["ident"] = np.eye(128, dtype=np.float32)
    t["ropeA"] = _rope_tab(np.arange(SEQ))
    cpos = 16 * (np.arange(513) - 1) + 31
    t["ropeC"] = _rope_tab(cpos)
    t["ropeS"] = np.repeat(_rope_tab(np.array([PAST])), 16, axis=0)
    E = np.zeros((64, 32, 128), np.float32)
    for j in range(32):
        E[2 * j, j, 0:64] = 1.0
        E[2 * j + 1, j, 64:128] = 1.0
    t["Eall"] = E
    A = np.zeros((256, 64), np.float32)
    for j in range(64):
        for c in range(4 * j - 1, 4 * j + 4):
            if 0 <= c < 255:
                A[c, j] = 1.0
    t["Aimp"] = A
    return t


def build_program(npair=NPAIR, ns_seq=16, pool_rows=81920):
    nc = bass.Bass("TRN2", target_bir_lowering=False)

    def din(name, shape, dt=F32):
        return nc.dram_tensor(name, list(shape), dt, kind="ExternalInput").ap()

    def dout(name, shape, dt=F32):
        return nc.dram_tensor(name, list(shape), dt, kind="ExternalOutput").ap()

    xb_d = din("xb", [SEQ, D_MODEL])
    xown_d = din("xown", [SEQ // 2, D_MODEL])
    w_in_d = din("w_in", [D_MODEL, D_IN])
    g_norm_d = din("g_norm", [D_MODEL])
    g_q_d, g_kc_d, g_ks_d, g_kw_d = (din(n, [64]) for n in ("g_q", "g_kc", "g_ks", "g_kw"))
    pe_k_d, pe_v_d = din("pe_k", [32, 64]), din("pe_v", [32, 64])
    w1_k_d, w1_v_d = din("w1_k", [32, 64, 64]), din("w1_v", [32, 64, 64])
    w2_k_d, w2_v_d = din("w2_k", [64, 64]), din("w2_v", [64, 64])
    conv_w_d, conv_b_d = din("conv_w", [4, 512]), din("conv_b", [512])
    w_ra_d, b_ra_d = din("w_ra", [8, 64, 64]), din("b_ra", [8, 64])
    w_rx_d, b_rx_d = din("w_rx", [8, 64, 64]), din("b_rx", [8, 64])
    lam_d = din("lam", [512])
    w_pa_d, w_pb_d, w_out_d = din("w_pa", [512, 1024]), din("w_pb", [512, 1024]), din("w_out", [1024, 1024])
    ident_d = din("ident", [128, 128])
    ropeA_d, ropeO_d, ropeC_d, ropeS_d = din("ropeA", [SEQ, 16]), din("ropeO", [SEQ // 2, 16]), din("ropeC", [513, 16]), din("ropeS", [16, 16])
    cmaskT_d = din("cmaskT", [NPAIR, 2, 128, 128])
    forcedB_d = din("forcedB", [NPAIR, 128, 64])
    dmask_d = din("dmask", [2, 128, 128])
    wmask_d = din("wmask", [6, 128, 128])
    selcol_d = din("selcol", [128, 2])
    Eall_d = din("Eall", [64, 32, 128])
    Aimp_d = din("Aimp", [256, 64])
    xs_d = din("xs", [16, D_MODEL])
    ptrep_d = din("ptrep", [128, 64], I32)
    pm8_d = din("pm8", [128, 1])
    poolc_d = din("poolc", [pool_rows, 4096])
    pools_d = din("pools", [pool_rows, 4096])
    ckw_d = din("ckw", [16, 512, 256])
    sconv_d = din("sconv", [16, 3, 512])
    sh_d = din("sh", [16, 512])
    As_d = din("As", [512, 129])
    cmS_d = din("cmS", [128, 4])
    fS_d = din("fS", [32, 129])
    oys_d = dout("oys", [16, D_MODEL])
    okvcs_d = dout("okvcs", [16, 256])
    okvss_d = dout("okvss", [16, 256])
    okvws_d = dout("okvws", [16, 512, 256])
    oconvs_d = dout("oconvs", [16, 3, 512])
    ohs_d = dout("ohs", [16, 512])
    oy_d = dout("oy", [SEQ // 2, D_MODEL])
    okvc_d = dout("okvc", [SEQ, 256])
    okvs_d = dout("okvs", [SEQ, 256])
    okvw_d = dout("okvw", [512, 256])
    oconv_d = dout("oconv", [3, 512])
    oh_d = dout("oh", [512])

    WbD = nc.dram_tensor("WbD", [128, 8, D_IN], BF16, kind="Internal").ap(); bWbD = Buf("WbD")
    WpaD = nc.dram_tensor("WpaD", [128, 4, 1024], BF16, kind="Internal").ap()
    WpbD = nc.dram_tensor("WpbD", [128, 4, 1024], BF16, kind="Internal").ap()
    WoutD = nc.dram_tensor("WoutD", [128, 8, 1024], BF16, kind="Internal").ap()
    scrOc = nc.dram_tensor("scrOc", [4, 32, 194], F32, kind="Internal").ap()
    scrOs = nc.dram_tensor("scrOs", [4, 32, 65], F32, kind="Internal").ap()
    scrOw = nc.dram_tensor("scrOw", [4, 32, 65], F32, kind="Internal").ap()
    scrSel = nc.dram_tensor("scrSel", [32, 512], F32, kind="Internal").ap()
    scrG = nc.dram_tensor("scrG", [16, 24], F32, kind="Internal").ap()
    scrZ = nc.dram_tensor("scrZ", [16, 512], F32, kind="Internal").ap()
    scrA = nc.dram_tensor("scrA", [32, 256], F32, kind="Internal").ap()

    S = Sched(nc)
    V, G, A_, T = nc.vector, nc.gpsimd, nc.scalar, nc.tensor
    import os as _os
    DBG = int(_os.environ.get("DBG_PAIR", "-1"))
    dbg_out = {}
    if DBG >= 0:
        for nm, shp in (("d_obr", [3, 128, 512]), ("d_acc", [128, 512]), ("d_hs", [128, 512]), ("d_sel", [2, 128, 64]), ("d_imp", [2, 128, 64])):
            dbg_out[nm] = dout(nm, shp)

    with ExitStack() as es:
        cur_es = [es]

        def sb(name, shape, dt=F32):
            return cur_es[0].enter_context(nc.sbuf_tensor("s_" + name, list(shape), dt))

        def ps(name, shape, dt=F32):
            return es.enter_context(nc.psum_tensor("p_" + name, list(shape), dt))

        wu = [sb("wu%d" % i, [128, 4096], BF16) for i in range(4)]; bwu = [Buf("wu%d" % i) for i in range(4)]
        identf = sb("identf", [128, 128], F32); bidf = Buf("identf")
        identb = sb("identb", [128, 128], BF16); bidb = Buf("identb")
        gnc = sb("gnc", [128, 8], F32); bgnc = Buf("gnc")
        gqb = sb("gqb", [128, 64], F32); gkcb = sb("gkcb", [128, 64], F32)
        gksb = sb("gksb", [128, 64], F32); gkwb = sb("gkwb", [128, 64], F32)
        bgq, bgkc, bgks, bgkw = Buf("gq"), Buf("gkc"), Buf("gks"), Buf("gkw")
        cwT = sb("cwT", [128, 4, 4], F32); bcw = Buf("cwT")
        cbT = sb("cbT", [128, 4], F32); bcb = Buf("cbT")
        braT = sb("braT", [128, 4], F32); brxT = sb("brxT", [128, 4], F32); bbr = Buf("brT")
        clT = sb("clT", [128, 4], F32); cl2T = sb("cl2T", [128, 4], F32); bcl = Buf("clT")
        WraB = sb("WraB", [128, 4, 128], BF16); WrxB = sb("WrxB", [128, 4, 128], BF16); bWr = Buf("WrB")
        W2k = sb("W2k", [64, 64], BF16); W2v = sb("W2v", [64, 64], BF16); bW2 = Buf("W2")
        pbk = sb("pbk", [64, 1], F32); pbv = sb("pbv", [64, 1], F32); bpb = Buf("pb")
        nr_sq = sb("nr_sq", [128, 512]); nr_ss = sb("nr_ss", [128, 8]); nr_t = sb("nr_t", [128, 4, 8, 8]); bnr = Buf("nr")
        esP = ExitStack()
        cur_es[0] = esP
        WbA = sb("WbA", [128, 8, 1280], BF16); bWbA = Buf("WbA")
        Wgn = sb("Wgn", [128, 8, 24], BF16); bWgn = Buf("Wgn")
        KST = sb("KST", [64, 2, SEQ], BF16); bKST = [Buf("KST%d" % t) for t in range(NT)]
        VSa = sb("VSa", [128, NT, 2, 65], BF16); bVSa = [Buf("VSa%d" % t) for t in range(NT)]
        KWT = sb("KWT", [64, 2, 8 * 128], BF16); bKWT = [Buf("KWT%d" % t) for t in range(8)]
        VWa = sb("VWa", [128, 8, 2, 65], BF16); bVWa = [Buf("VWa%d" % t) for t in range(8)]
        KCT = sb("KCT", [64, 2, 256], BF16); bKCT = Buf("KCT")
        VCT = sb("VCT", [64, 2, 256], BF16); bVCT = Buf("VCT")
        VCA = sb("VCA", [128, 2, 2, 129], BF16); bVCA = Buf("VCA")
        Eall = sb("Eall", [64, 32, 128], BF16); bEall = Buf("Eall")
        XTp = sb("XTp", [64, 4, 272], BF16); bXTp = Buf("XTp")
        xr = sb("xr", [128, 4, 259], F32); bxr = Buf("xr")
        hprev = sb("hprev", [128, 4], F32); bhprev = Buf("hprev")
        W1k = sb("W1k", [64, 32, 64], BF16); W1v = sb("W1v", [64, 32, 64], BF16); bW1 = Buf("W1")
        selcol = sb("selcol", [128, 2], F32); bselcol = Buf("selcol")
        dmask = sb("dmask", [128, 2, 128], F32); bdmask = Buf("dmask")
        wmask = sb("wmask", [128, 6, 128], F32); bwmask = Buf("wmask")
        ropeCt = sb("ropeCt", [16, NPAIR, 16], F32); bropeC = Buf("ropeCt")

        pmm = [ps("pmm0", [128, 512]), ps("pmm1", [128, 512])]; bpmm = [Buf("pmm0"), Buf("pmm1")]
        ptr = ps("ptr", [128, 1024], BF16); bptr = Buf("ptr")
        pst = [ps("pst0", [128, 512]), ps("pst1", [128, 512])]; bpst = [Buf("pst0"), Buf("pst1")]
        pmk = ps("pmk", [128, 2, 256]); _bp = Buf("pmk"); bpmk = [_bp, _bp]
        pO = ps("pO", [128, 4, 256]); bpO = Buf("pO")
        st = {"pmm": 0, "pst": 0, "pmk": 0}

        def nxt(kind):
            st[kind] ^= 1
            return st[kind]

        def bc_mid(ap, P, n, w):
            return ap.unsqueeze(1).to_broadcast([P, n, w])

        def bc_last(ap, P, n, w):
            return ap.unsqueeze(2).to_broadcast([P, n, w])

        with ExitStack() as es2:
            stg = [es2.enter_context(nc.sbuf_tensor("stg%d" % i, [128, 2444], F32)) for i in range(2)]
            bstg = [Buf("stg0"), Buf("stg1")]
            tmpc = es2.enter_context(nc.sbuf_tensor("tmpc", [128, 16, 64], F32)); btmpc = Buf("tmpc")
            S.dma("sp", identf[:], ident_d, writes=[bidf])
            S.op("act", lambda: A_.copy(out=identb[:], in_=identf[:]), [bidf], [bidb])
            S.dma("sp", None, None, writes=[bgnc], fn=lambda: nc.sync.dma_start(
                out=gnc[:], in_=g_norm_d.rearrange("(k p) -> p k", p=128), allow_slow_non_contiguous=True))
            n = 0
            engs = ["dve", "pool"]
            stgb = [es2.enter_context(nc.sbuf_tensor("stgb%d" % i, [128, 2444], BF16)) for i in range(2)]
            bstgb = [Buf("stgb0"), Buf("stgb1")]
            for k in range(8):
                for hf in range(2):
                    sl = n % 2
                    S.dma("sp", stg[sl][:], w_in_d[k * 128:(k + 1) * 128, hf * 2444:(hf + 1) * 2444], writes=[bstg[sl]])
                    e = engs[n % 2]
                    E_ = V if e == "dve" else G
                    S.op(e, lambda: E_.tensor_scalar(out=stgb[sl][:], in0=stg[sl][:], scalar1=gnc[:, k:k + 1], scalar2=None,
                                                     op0=ALU.mult), [bstg[sl], bgnc], [bstgb[sl]])
                    S.dma("sp", WbD[:, k, hf * 2444:(hf + 1) * 2444], stgb[sl][:], reads=[bstgb[sl]], writes=[bWbD])
                    n += 1
            for (wd, wD, nk) in ((w_pa_d, WpaD, 4), (w_pb_d, WpbD, 4), (w_out_d, WoutD, 8)):
                for k in range(nk):
                    sl = n % 2
                    S.dma("sp", stg[sl][:, 0:1024], wd[k * 128:(k + 1) * 128, :], writes=[bstg[sl]])
                    e = engs[n % 2]
                    E_ = V if e == "dve" else G
                    S.op(e, lambda: E_.tensor_copy(out=stgb[sl][:, 0:1024], in_=stg[sl][:, 0:1024]), [bstg[sl]], [bstgb[sl]])
                    S.dma("sp", wD[:, k, :], stgb[sl][:, 0:1024], reads=[bstgb[sl]], writes=[bWbD])
                    n += 1
            S.dma("sp", WbA[:, :, 0:768], WbD[:, :, CKV:CKV + 768], reads=[bWbD], writes=[bWbA])
            S.dma("sp", WbA[:, :, 768:1280], WbD[:, :, CXR:CXR + 512], reads=[bWbD], writes=[bWbA])
            S.dma("sp", None, None, reads=[bWbD], writes=[bWgn], fn=lambda: nc.sync.dma_start(out=Wgn[:], in_=WbD[:, :, CGN:CGN + 24], allow_slow_non_contiguous=True))
            for hf in range(2):
                sl = n % 2
                S.dma("sp", stg[sl][0:64, 0:2048], Eall_d[:, hf * 16:(hf + 1) * 16, :].rearrange("b j k -> b (j k)"), writes=[bstg[sl]])
                S.op("dve", lambda sl=sl, hf=hf: V.tensor_copy(
                    out=Eall[:, hf * 16:(hf + 1) * 16, :].rearrange("b j k -> b (j k)"), in_=stg[sl][0:64, 0:2048]), [bstg[sl]], [bEall])
                n += 1
            for (gd, gt_, bg) in ((g_q_d, gqb, bgq), (g_kc_d, gkcb, bgkc), (g_ks_d, gksb, bgks), (g_kw_d, gkwb, bgkw)):
                S.dma("sp", None, None, writes=[bg], fn=lambda gd=gd, gt_=gt_: nc.sync.dma_start(out=gt_[:], in_=gd.partition_broadcast(128)))
            for tap in range(4):
                S.dma("sp", None, None, writes=[bcw], fn=lambda: nc.sync.dma_start(
                    out=cwT[:, :, tap], in_=conv_w_d[tap].rearrange("(g c) -> c g", c=128), allow_slow_non_contiguous=True))
            S.dma("sp", None, None, writes=[bcb], fn=lambda: nc.sync.dma_start(
                out=cbT[:], in_=conv_b_d.rearrange("(g c) -> c g", c=128), allow_slow_non_contiguous=True))
            S.dma("sp", None, None, writes=[bbr], fn=lambda: nc.sync.dma_start(
                out=braT[:], in_=b_ra_d.rearrange("(g a) c -> (a c) g", a=2), allow_slow_non_contiguous=True))
            S.dma("sp", None, None, writes=[bbr], fn=lambda: nc.sync.dma_start(
                out=brxT[:], in_=b_rx_d.rearrange("(g a) c -> (a c) g", a=2), allow_slow_non_contiguous=True))
            S.dma("sp", None, None, writes=[bcl], fn=lambda: nc.sync.dma_start(
                out=clT[:], in_=lam_d.rearrange("(g c) -> c g", c=128), allow_slow_non_contiguous=True))
            S.op("act", lambda: A_.activation(out=clT[:], in_=clT[:], func=AF.Exp, scale=-1.0), [bcl], [bcl])
            S.op("act", lambda: A_.activation(out=clT[:], in_=clT[:], func=AF.Ln, bias=1.0), [bcl], [bcl])
            S.op("dve", lambda: V.tensor_scalar(out=cl2T[:], in0=clT[:], scalar1=-16.0, scalar2=None, op0=ALU.mult), [bcl], [bcl])
            S.op("dve", lambda: V.tensor_scalar(out=clT[:], in0=clT[:], scalar1=-8.0, scalar2=None, op0=ALU.mult), [bcl], [bcl])
            for (wd, wsb) in ((w_ra_d, WraB), (w_rx_d, WrxB)):
                S.op("pool", lambda: G.memset(stg[0][:, 0:512], 0.0), [], [bstg[0]])
                for g in range(4):
                    for a in range(2):
                        S.dma("sp", stg[0][a * 64:(a + 1) * 64, g * 128 + a * 64: g * 128 + a * 64 + 64], wd[2 * g + a], writes=[bstg[0]])
                S.op("dve", lambda wsb=wsb: V.tensor_copy(out=wsb[:].rearrange("p g c -> p (g c)"), in_=stg[0][:, 0:512]), [bstg[0]], [bWr])
            for (wd, wsb) in ((w1_k_d, W1k), (w1_v_d, W1v)):
                S.dma("sp", None, None, writes=[bstg[1]], fn=lambda wd=wd: nc.sync.dma_start(
                    out=stg[1][0:64, 0:2048].rearrange("d (j e) -> d j e", j=32), in_=wd.rearrange("j d e -> d j e")))
                S.op("dve", lambda wsb=wsb: V.tensor_copy(out=wsb[:].rearrange("d j e -> d (j e)"), in_=stg[1][0:64, 0:2048]), [bstg[1]], [bW1])
            for (wd, wsb) in ((w2_k_d, W2k), (w2_v_d, W2v)):
                S.dma("sp", stg[1][0:64, 0:64], wd, writes=[bstg[1]])
                S.op("dve", lambda wsb=wsb: V.tensor_copy(out=wsb[:], in_=stg[1][0:64, 0:64]), [bstg[1]], [bW2])
            peT = es2.enter_context(nc.sbuf_tensor("peT", [64, 2, 32], F32)); bpe = Buf("peT")
            peTb = es2.enter_context(nc.sbuf_tensor("peTb", [64, 2, 32], BF16))
            S.dma("sp", None, None, writes=[bpe], fn=lambda: nc.sync.dma_start(out=peT[:, 0, :], in_=pe_k_d.rearrange("j d -> d j"), allow_slow_non_contiguous=True))
            S.dma("sp", None, None, writes=[bpe], fn=lambda: nc.sync.dma_start(out=peT[:, 1, :], in_=pe_v_d.rearrange("j d -> d j"), allow_slow_non_contiguous=True))
            S.op("dve", lambda: V.tensor_copy(out=peTb[:], in_=peT[:]), [bpe], [bpe])
            for kind, (w1s, pbt) in enumerate(((W1k, pbk), (W1v, pbv))):
                for j in range(32):
                    S.op("pe", lambda kind=kind, w1s=w1s, j=j: T.matmul(pmm[0][0:64, kind:kind + 1], lhsT=w1s[:, j, :], rhs=peTb[:, kind, j:j + 1],
                                                                       start=(j == 0), stop=(j == 31)), [bW1, bpe], [bpmm[0]])
                S.op("dve", lambda kind=kind, pbt=pbt: V.tensor_copy(out=pbt[:], in_=pmm[0][0:64, kind:kind + 1]), [bpmm[0]], [bpb])
            S.dma("sp", selcol[:], selcol_d, writes=[bselcol])
            S.dma("sp", dmask[:], dmask_d.rearrange("m k q -> k m q"), writes=[bdmask])
            S.dma("sp", wmask[:], wmask_d.rearrange("m k q -> k m q"), writes=[bwmask])
            S.dma("sp", None, None, writes=[bropeC], fn=lambda: nc.sync.dma_start(
                out=ropeCt[:], in_=ropeC_d[0:256, :].rearrange("(i m) e -> m i e", m=16)))
            S.op("pool", lambda: G.memset(VCA[:], 1.0), [], [bVCA])
            for ch in range(2):
                S.dma("sp", stg[0][:, 0:64], Aimp_d[ch * 128:(ch + 1) * 128, :], writes=[bstg[0]])
                for hk in range(2):
                    S.op("dve", lambda ch=ch, hk=hk: V.tensor_copy(out=VCA[:, ch, hk, 65:129], in_=stg[0][:, 0:64]), [bstg[0]], [bVCA])
            S.op("pool", lambda: G.memset(VSa[:], 1.0), [], bVSa)
            S.op("pool", lambda: G.memset(VWa[:], 1.0), [], bVWa)
            S.op("pool", lambda: G.memset(KCT[:], 0.0), [], [bKCT])
            S.op("pool", lambda: G.memset(VCT[:], 0.0), [], [bVCT])
            S.op("pool", lambda: G.memset(XTp[:], 0.0), [], [bXTp])
            S.op("pool", lambda: G.memset(xr[:], 0.0), [], [bxr])
            S.op("pool", lambda: G.memset(hprev[:], 0.0), [], [bhprev])
            S.barrier()

        xt = [sb("xt%d" % i, [128, 1024]) for i in range(2)]; bxt = [Buf("xt%d" % i) for i in range(2)]
        xo = [sb("xo%d" % i, [128, 1024]) for i in range(2)]; bxo = [Buf("xo%d" % i) for i in range(2)]
        rpA = [sb("rpA%d" % i, [128, 16]) for i in range(2)]; brpA = [Buf("rpA%d" % i) for i in range(2)]
        rpO = [sb("rpO%d" % i, [128, 16]) for i in range(2)]; brpO = [Buf("rpO%d" % i) for i in range(2)]
        cmk = [sb("cmk%d" % i, [128, 2, 128]) for i in range(2)]; bcmk = [Buf("cmk%d" % i) for i in range(2)]
        fbt = [sb("fbt%d" % i, [128, 64]) for i in range(2)]; bfbt = [Buf("fbt%d" % i) for i in range(2)]
        ss = sb("ss", [128, 2]); bss = Buf("ss")
        xn = sb("xn", [128, 1024], BF16); bxn = Buf("xn")
        uT = sb("uT", [128, 8, 256], BF16); buT = Buf("uT")
        uTo = sb("uTo", [128, 8, 128], BF16); buTo = Buf("uTo")
        uTt = sb("uTt", [128, 8, 128], BF16); buTt = Buf("uTt")
        kvr = [sb("kvr%d" % i, [128, 768]) for i in range(2)]; bkvr = [Buf("kvr%d" % i) for i in range(2)]
        kvb = sb("kvb", [128, 768], BF16); bkvb = Buf("kvb")
        xc = sb("xc", [128, 256]); bxc = Buf("xc")
        xcb = sb("xcb", [128, 256], BF16); bxcb = Buf("xcb")
        gr = sb("gr", [128, 256]); gi = sb("gi", [128, 256]); ga = sb("ga", [128, 256]); ga2 = sb("ga2", [128, 256]); bg_ = Buf("gates")
        hsf = sb("hsf", [128, 256]); bhsf = Buf("hsf")
        hsb = [sb("hsb%d" % q, [128, 4, 256], BF16) for q in range(2)]; bhsb = [Buf("hsb0"), Buf("hsb1")]
        hidk = sb("hidk", [64, 32], BF16); hidv = sb("hidv", [64, 32], BF16); bhid = Buf("hid")
        kcn = sb("kcn", [16, 2, 64]); kcnb = sb("kcnb", [16, 2, 64], BF16); bkcn = Buf("kcn")
        qf = sb("qf", [128, 512]); bqf = Buf("qf")
        qb = sb("qb", [128, 512], BF16); bqb = Buf("qb")
        QT = sb("QT", [64, 8, 128], BF16); bQT = Buf("QT")
        gns = sb("gns", [128, 24]); bgns = Buf("gns")
        zs = sb("zs", [128, 512]); bzs = Buf("zs")
        zr = sb("zr", [128, 4, 128]); bzr = Buf("zr")
        gm = sb("gm", [128, 2, 512]); bgm = Buf("gm")
        Eb = [sb("Eb%d" % i, [128, 4, 128], BF16) for i in range(2)]; bEb = [Buf("Eb%d" % i) for i in range(2)]
        Pt = [sb("Pt%d" % i, [128, 4, 128], BF16) for i in range(2)]; bPt = [Buf("Pt%d" % i) for i in range(2)]
        mk2 = sb("mk2", [128, 128]); bmk2 = Buf("mk2")
        rden = sb("rden", [128, 4]); coef = sb("coef", [128, 4]); brd = Buf("rden")
        acc = sb("acc", [128, 8, 64]); bacc = Buf("acc")
        tmpo = sb("tmpo", [128, 4, 64]); btmpo = Buf("tmpo")
        impq = sb("impq", [128, 64]); impw = sb("impw", [128, 64]); m8 = sb("m8", [128, 16]); bimp = Buf("imp")
        selb = sb("selb", [128, 64], BF16); bselb = Buf("selb")
        selT = sb("selT", [64, 128], BF16); bselT = Buf("selT")
        ab = sb("ab", [128, 512], BF16); bab = Buf("ab")
        aT = sb("aT", [128, 4, 128], BF16); baT = Buf("aT")
        rt1 = sb("rt1", [128, 4, 128]); brt1 = Buf("rt1")
        rT = sb("rT", [128, 4, 128], BF16); brT = Buf("rT")
        mt1 = sb("mt1", [128, 512]); mt2 = sb("mt2", [128, 512]); bmt = Buf("mt")
        mT = sb("mT", [128, 8, 128], BF16); bmT = Buf("mT")

        ust = {"n": 0}
        cur = {"i": -1}
        dbgt = sb("dbgt", [128, 4, 64]); bdbgt = Buf("dbgt")

        def unit(src_ap, nk):
            sl = ust["n"] % 4
            ust["n"] += 1
            W = 4096 // nk
            view = wu[sl][:].rearrange("p (k c) -> p k c", k=nk)
            S.dma("sp", view, src_ap, reads=[bWbD], writes=[bwu[sl]])
            return view, bwu[sl]

        def norm_rope(x3, P, nh, gb, bg, rope, brope, bx):
            sq = nr_sq[0:P, 0:nh * 64].rearrange("p (h d) -> p h d", h=nh)
            S.op("dve", lambda: V.tensor_tensor(out=sq, in0=x3, in1=x3, op=ALU.mult), [bx], [bnr])
            S.op("dve", lambda: V.tensor_reduce(out=nr_ss[0:P, 0:nh], in_=sq, axis=AX.X, op=ALU.add), [bnr], [bnr])
            S.op("act", lambda: A_.activation(out=nr_ss[0:P, 0:nh], in_=nr_ss[0:P, 0:nh], func=AF.Sqrt, scale=1.0 / 64, bias=EPS), [bnr], [bnr])
            S.op("dve", lambda: V.reciprocal(out=nr_ss[0:P, 0:nh], in_=nr_ss[0:P, 0:nh]), [bnr], [bnr])
            S.op("dve", lambda: V.tensor_tensor(out=x3, in0=x3, in1=bc_last(nr_ss[0:P, 0:nh], P, nh, 64), op=ALU.mult), [bx, bnr], [bx])
            S.op("dve", lambda: V.tensor_tensor(out=x3, in0=x3, in1=bc_mid(gb[0:P, :], P, nh, 64), op=ALU.mult), [bx, bg], [bx])
            x1, x2 = x3[:, :, 0:8], x3[:, :, 8:16]
            cosb, sinb = bc_mid(rope[0:P, 0:8], P, nh, 8), bc_mid(rope[0:P, 8:16], P, nh, 8)
            ta, tb, tc, td = (nr_t[0:P, i, 0:nh, :] for i in range(4))
            S.op("dve", lambda: V.tensor_tensor(out=ta, in0=x1, in1=cosb, op=ALU.mult), [bx, brope], [bnr])
            S.op("dve", lambda: V.tensor_tensor(out=tb, in0=x2, in1=sinb, op=ALU.mult), [bx, brope], [bnr])
            S.op("dve", lambda: V.tensor_tensor(out=tc, in0=x2, in1=cosb, op=ALU.mult), [bx, brope], [bnr])
            S.op("dve", lambda: V.tensor_tensor(out=td, in0=x1, in1=sinb, op=ALU.mult), [bx, brope], [bnr])
            S.op("dve", lambda: V.tensor_tensor(out=x1, in0=ta, in1=tb, op=ALU.subtract), [bnr], [bx])
            S.op("dve", lambda: V.tensor_tensor(out=x2, in0=tc, in1=td, op=ALU.add), [bnr], [bx])

        def load_x(i):
            for tt_ in range(2):
                t = 2 * i + tt_
                S.dma("sp", xt[tt_][:], xb_d[t * 128:(t + 1) * 128, :], writes=[bxt[tt_]])
                S.dma("sp", rpA[tt_][:], ropeA_d[t * 128:(t + 1) * 128, :], writes=[brpA[tt_]])

        def load_own(i):
            sl = i % 2
            S.dma("sp", xo[sl][:], xown_d[i * 128:(i + 1) * 128, :], writes=[bxo[sl]])
            S.dma("sp", rpO[sl][:], ropeO_d[i * 128:(i + 1) * 128, :], writes=[brpO[sl]])
            S.dma("sp", cmk[sl][:], cmaskT_d[i].rearrange("c k q -> k c q"), writes=[bcmk[sl]])
            S.dma("sp", fbt[sl][:], forcedB_d[i], writes=[bfbt[sl]])

        def attn_branch(hk, tiles, ncol, step):
            n = len(tiles)
            qrhs = QT[:, 4 * hk:4 * hk + 4, :].rearrange("d h q -> d (h q)")
            masks = [None] * n

            def front(x):
                tl = tiles[x]
                a = x % 2
                S.op("pe", lambda: T.matmul(pst[a][:], lhsT=tl["KT"], rhs=qrhs, start=True, stop=True), [tl["bK"], bQT], [bpst[a]])
                S.op("act", lambda: A_.activation(out=Eb[a][:].rearrange("k h q -> k (h q)"), in_=pst[a][:], func=AF.Exp, scale=SCALE),
                     [bpst[a]], [bEb[a]])

            def mask(x):
                tl = tiles[x]
                if tl["kind"] != "sel":
                    masks[x] = (tl["mk"], tl["bm"])
                    return
                m = x % 2
                S.op("pe", lambda: T.matmul(pmk[:, m, 0:128], lhsT=Eall[:, tl["j"], :], rhs=selT[:], start=True, stop=True),
                     [bEall, bselT], [bpmk[m]])
                if tl["dm"] is None:
                    masks[x] = (pmk[:, m, 0:128], bpmk[m])
                else:
                    S.op("dve", lambda: V.tensor_tensor(out=mk2[:], in0=pmk[:, m, 0:128], in1=dmask[:, tl["dm"], :], op=ALU.mult),
                         [bpmk[m], bdmask], [bmk2])
                    masks[x] = (mk2[:], bmk2)

            def back(x):
                tl = tiles[x]
                a = x % 2
                mk_ap, bm = masks[x]
                S.op("dve", lambda: V.tensor_tensor(out=Pt[a][:], in0=Eb[a][:], in1=bc_mid(mk_ap, 128, 4, 128), op=ALU.mult),
                     [bEb[a], bm], [bPt[a]])
                for h in range(4):
                    S.op("pe", lambda: T.matmul(pO[:, h, 0:ncol], lhsT=Pt[a][:, h, :], rhs=tl["V"], start=(x == 0 and h % 2 == 0), stop=(x == n - 1),
                                                skip_group_check=True), [bPt[a], tl["bV"]], [bpO])

            front(0)
            mask(0)
            for x in range(n):
                if x + 1 < n:
                    front(x + 1)
                back(x)
                if x + 1 < n:
                    mask(x + 1)
                step()

        def finish_branch(hk, br, first_branch):
            S.op("dve", lambda: V.tensor_scalar(out=rden[:], in0=pO[:, :, 64], scalar1=1e-30, scalar2=None, op0=ALU.max), [bpO], [brd])
            S.op("dve", lambda: V.reciprocal(out=rden[:], in_=rden[:]), [brd], [brd])
            gate = gns[:, hk * 12:(hk + 1) * 12].rearrange("p (h b) -> p h b", b=3)[:, :, br]
            S.op("dve", lambda: V.tensor_tensor(out=coef[:], in0=rden[:], in1=gate, op=ALU.mult), [brd, bgns], [brd])
            dst = acc[:, 4 * hk:4 * hk + 4, :]
            if DBG >= 0 and cur["i"] == DBG:
                S.op("dve", lambda: V.tensor_tensor(out=dbgt[:], in0=pO[:, :, 0:64], in1=bc_last(rden[:], 128, 4, 64), op=ALU.mult), [bpO, brd], [bdbgt])
                S.dma("pool", dbg_out["d_obr"][br, :, hk * 256:(hk + 1) * 256], dbgt[:].rearrange("p h d -> p (h d)"), reads=[bdbgt])
            if first_branch:
                S.op("dve", lambda: V.tensor_tensor(out=dst, in0=pO[:, :, 0:64], in1=bc_last(coef[:], 128, 4, 64), op=ALU.mult),
                     [bpO, brd], [bacc])
            else:
                S.op("dve", lambda: V.tensor_tensor(out=tmpo[:], in0=pO[:, :, 0:64], in1=bc_last(coef[:], 128, 4, 64), op=ALU.mult),
                     [bpO, brd], [btmpo])
                S.op("pool", lambda: G.tensor_tensor(out=dst, in0=dst, in1=tmpo[:], op=ALU.add), [btmpo, bacc], [bacc])

        def A_gen(i):
            hp = i % 2
            for tt_ in range(2):
                S.op("act", lambda: A_.activation(out=xn[:], in_=xt[tt_][:], func=AF.Square, accum_out=ss[:, 0:1]), [bxt[tt_]], [bxn, bss])
                S.op("act", lambda: A_.activation(out=ss[:, 1:2], in_=ss[:, 0:1], func=AF.Sqrt, scale=1.0 / D_MODEL, bias=EPS), [bss], [bss])
                S.op("dve", lambda: V.reciprocal(out=ss[:, 1:2], in_=ss[:, 1:2]), [bss], [bss])
                S.op("pool", lambda: G.tensor_scalar(out=xn[:], in0=xt[tt_][:], scalar1=ss[:, 1:2], scalar2=None, op0=ALU.mult), [bxt[tt_], bss], [bxn])
                for k in range(8):
                    S.op("pe", lambda: T.transpose(out=ptr[:, k * 128:(k + 1) * 128], in_=xn[:, k * 128:(k + 1) * 128], identity=identb[:]), [bxn, bidb], [bptr])
                S.op("act", lambda: A_.copy(out=uT[:, :, tt_ * 128:(tt_ + 1) * 128], in_=ptr[:].rearrange("p (k t) -> p k t", k=8)), [bptr], [buT])
            yield
            ck("A2")
            for tt_ in range(2):
                t = 2 * i + tt_
                ks_ = tt_
                a = nxt("pmm")
                for k in range(8):
                    S.op("pe", lambda: T.matmul(pmm[a][:], lhsT=uT[:, k, tt_ * 128:(tt_ + 1) * 128], rhs=WbA[:, k, 0:512],
                                                start=(k == 0), stop=(k == 7)), [buT, bWbA], [bpmm[a]])
                S.op("act", lambda: A_.copy(out=kvr[ks_][:, 0:512], in_=pmm[a][:]), [bpmm[a]], [bkvr[ks_]])
                a = nxt("pmm")
                for k in range(8):
                    S.op("pe", lambda: T.matmul(pmm[a][:, 0:256], lhsT=uT[:, k, tt_ * 128:(tt_ + 1) * 128], rhs=WbA[:, k, 512:768],
                                                start=(k == 0), stop=(k == 7)), [buT, bWbA], [bpmm[a]])
                S.op("act", lambda: A_.copy(out=kvr[ks_][:, 512:768], in_=pmm[a][:, 0:256]), [bpmm[a]], [bkvr[ks_]])
                ck("A2a")
                norm_rope(kvr[ks_][:, 256:384].rearrange("p (h d) -> p h d", h=2), 128, 2, gksb, bgks, rpA[tt_], brpA[tt_], bkvr[ks_])
                norm_rope(kvr[ks_][:, 512:640].rearrange("p (h d) -> p h d", h=2), 128, 2, gkwb, bgkw, rpA[tt_], brpA[tt_], bkvr[ks_])
                yield
                ck("A2b")
                S.dma("pool", okvc_d[t * 128:(t + 1) * 128, :], kvr[ks_][:, 0:256], reads=[bkvr[ks_]])
                S.dma("pool", okvs_d[t * 128:(t + 1) * 128, :], kvr[ks_][:, 256:512], reads=[bkvr[ks_]])
                if t >= NT - 4:
                    S.dma("pool", okvw_d[(t - (NT - 4)) * 128:(t - (NT - 4) + 1) * 128, :], kvr[ks_][:, 512:768], reads=[bkvr[ks_]])
                ck("A2c")
                S.op("pool", lambda: G.tensor_copy(out=kvb[:], in_=kvr[ks_][:]), [bkvr[ks_]], [bkvb])
                wsl = t % 8
                S.op("pool", lambda: G.tensor_copy(out=VSa[:, t, :, 0:64], in_=kvb[:, 384:512].rearrange("p (h d) -> p h d", h=2)), [bkvb], [bVSa[t]])
                S.op("pool", lambda: G.tensor_copy(out=VWa[:, wsl, :, 0:64], in_=kvb[:, 640:768].rearrange("p (h d) -> p h d", h=2)), [bkvb], [bVWa[wsl]])
                ck("A2d")
                srcs = [0, 64, 128, 192, 256, 320, 512, 576]
                for n_, c0 in enumerate(srcs):
                    S.op("pe", lambda: T.transpose(out=ptr[0:64, n_ * 128:(n_ + 1) * 128], in_=kvb[:, c0:c0 + 64], identity=identb[:]), [bkvb, bidb], [bptr])
                ck("A2t")
                S.op("dve", lambda: V.tensor_copy(out=XTp[:, :, 16 + tt_ * 128:16 + (tt_ + 1) * 128], in_=ptr[0:64, 0:512].rearrange("p (k t) -> p k t", k=4)), [bptr], [bXTp])
                ck("A2e1")
                S.op("dve", lambda: V.tensor_copy(out=KST[:, :, t * 128:(t + 1) * 128], in_=ptr[0:64, 512:768].rearrange("p (k t) -> p k t", k=2)), [bptr], [bKST[t]])
                ck("A2e2")
                S.op("dve", lambda: V.tensor_copy(out=KWT[:, :, wsl * 128:(wsl + 1) * 128], in_=ptr[0:64, 768:1024].rearrange("p (k t) -> p k t", k=2)), [bptr], [bKWT[wsl]])
                ck("A2e")
                yield
            yield
            ck("A3")
            for g in range(4):
                a = nxt("pmm")
                for k in range(8):
                    S.op("pe", lambda: T.matmul(pmm[a][:, 0:256], lhsT=WbA[:, k, 768 + g * 128:768 + (g + 1) * 128], rhs=uT[:, k, :],
                                                start=(k == 0), stop=(k == 7)), [buT, bWbA], [bpmm[a]])
                S.op("act", lambda: A_.copy(out=xr[:, g, 3:259], in_=pmm[a][:, 0:256]), [bpmm[a]], [bxr])
            if i + 1 < npair:
                load_x(i + 1)
            for g in range(4):
                S.op("dve", lambda: V.tensor_scalar(out=xc[:], in0=xr[:, g, 3:259], scalar1=cwT[:, g, 3:4], scalar2=cbT[:, g:g + 1],
                                                    op0=ALU.mult, op1=ALU.add), [bxr, bcw, bcb], [bxc])
                for tap in range(3):
                    S.op("dve", lambda: V.scalar_tensor_tensor(out=xc[:], in0=xr[:, g, tap:tap + 256], scalar=cwT[:, g, tap:tap + 1],
                                                               in1=xc[:], op0=ALU.mult, op1=ALU.add), [bxr, bcw, bxc], [bxc])
                S.op("pool", lambda: G.tensor_copy(out=xcb[:], in_=xc[:]), [bxc], [bxcb])
                a = nxt("pmm")
                S.op("pe", lambda: T.matmul(pmm[a][:, 0:256], lhsT=WraB[:, g, :], rhs=xcb[:], start=True, stop=True), [bWr, bxcb], [bpmm[a]])
                S.op("pe", lambda: T.matmul(pmm[a][:, 256:512], lhsT=WrxB[:, g, :], rhs=xcb[:], start=True, stop=True), [bWr, bxcb], [bpmm[a]])
                S.op("act", lambda: A_.activation(out=gr[:], in_=pmm[a][:, 0:256], func=AF.Sigmoid, bias=braT[:, g:g + 1]), [bpmm[a], bbr], [bg_])
                S.op("act", lambda: A_.activation(out=gi[:], in_=pmm[a][:, 256:512], func=AF.Sigmoid, bias=brxT[:, g:g + 1]), [bpmm[a], bbr], [bg_])
                S.op("act", lambda: A_.activation(out=ga[:], in_=gr[:], func=AF.Exp, scale=clT[:, g:g + 1]), [bg_, bcl], [bg_])
                S.op("act", lambda: A_.activation(out=ga2[:], in_=gr[:], func=AF.Exp, scale=cl2T[:, g:g + 1]), [bg_, bcl], [bg_])
                S.op("act", lambda: A_.activation(out=ga2[:], in_=ga2[:], func=AF.Sqrt, scale=-1.0, bias=1.0), [bg_], [bg_])
                S.op("dve", lambda: V.tensor_tensor(out=gi[:], in0=gi[:], in1=xc[:], op=ALU.mult), [bg_, bxc], [bg_])
                S.op("dve", lambda: V.tensor_tensor(out=gi[:], in0=gi[:], in1=ga2[:], op=ALU.mult), [bg_], [bg_])
                S.op("dve", lambda: V.tensor_tensor_scan(out=hsf[:], data0=ga[:], data1=gi[:], initial=hprev[:, g:g + 1], op0=ALU.mult, op1=ALU.add),
                     [bg_, bhprev], [bhsf])
                S.op("dve", lambda: V.tensor_copy(out=hprev[:, g:g + 1], in_=hsf[:, 255:256]), [bhsf], [bhprev])
                S.op("pool", lambda: G.tensor_copy(out=hsb[hp][:, g, :], in_=hsf[:]), [bhsf], [bhsb[hp]])
                yield
            if i == npair - 1:
                for t3 in range(3):
                    S.dma("pool", None, None, reads=[bxr], fn=lambda: nc.gpsimd.dma_start(
                        out=oconv_d[t3].rearrange("(g c) -> c g", c=128), in_=xr[:, :, 256 + t3], allow_slow_non_contiguous=True))
                S.dma("pool", None, None, reads=[bhprev], fn=lambda: nc.gpsimd.dma_start(
                    out=oh_d.rearrange("(g c) -> c g", c=128), in_=hprev[:], allow_slow_non_contiguous=True))
            S.op("pool", lambda: G.tensor_copy(out=xr[:, :, 0:3], in_=xr[:, :, 256:259]), [bxr], [bxr])
            yield
            ck("A4")
            m0 = 1 if i == 0 else 0
            for kind, (w1s, hid, pbt) in enumerate(((W1k, hidk, pbk), (W1v, hidv, pbv))):
                a = nxt("pmm")
                for h in range(2):
                    for j in range(32):
                        S.op("pe", lambda: T.matmul(pmm[a][0:64, h * 16:(h + 1) * 16], lhsT=w1s[:, j, :],
                                                    rhs=XTp[:, kind * 2 + h, j:j + 241:16], start=(j == 0), stop=(j == 31)),
                             [bW1, bXTp], [bpmm[a]])
                S.op("act", lambda: A_.activation(out=hid[:], in_=pmm[a][0:64, 0:32], func=AF.Silu, bias=pbt[:, 0:1]), [bpmm[a], bpb], [bhid])
            S.op("pool", lambda: G.tensor_copy(out=XTp[:, :, 0:16], in_=XTp[:, :, 256:272]), [bXTp], [bXTp])
            yield
            a = nxt("pmm")
            for h in range(2):
                S.op("pe", lambda: T.matmul(pmm[a][0:16, h * 64:(h + 1) * 64], lhsT=hidk[:, h * 16:(h + 1) * 16], rhs=W2k[:], start=True, stop=True),
                     [bhid, bW2], [bpmm[a]])
            S.op("act", lambda: A_.copy(out=kcn[:].rearrange("p h d -> p (h d)"), in_=pmm[a][0:16, 0:128]), [bpmm[a]], [bkcn])
            norm_rope(kcn[:], 16, 2, gkcb, bgkc, ropeCt[:, i, :], bropeC, bkcn)
            S.op("pool", lambda: G.tensor_copy(out=kcnb[:], in_=kcn[:]), [bkcn], [bkcn])
            for h in range(2):
                S.op("pe", lambda: T.transpose(out=ptr[0:64, h * 16:(h + 1) * 16], in_=kcnb[:, h, :], identity=identb[0:16, 0:16]), [bkcn, bidb], [bptr])
            c0 = 16 * i - 1 + m0
            S.op("dve", lambda: V.tensor_copy(out=KCT[:, :, c0:16 * i + 15], in_=ptr[0:64, 0:32].rearrange("p (h m) -> p h m", h=2)[:, :, m0:16]), [bptr], [bKCT])
            a = nxt("pmm")
            S.op("pe", lambda: T.matmul(pmm[a][0:64, 0:32], lhsT=W2v[:], rhs=hidv[:], start=True, stop=True), [bhid, bW2], [bpmm[a]])
            S.op("act", lambda: A_.copy(out=VCT[:, :, c0:16 * i + 15], in_=pmm[a][0:64, 0:32].rearrange("p (h m) -> p h m", h=2)[:, :, m0:16]), [bpmm[a]], [bVCT])
            for ch in range(2):
                for h in range(2):
                    n_ = ch * 2 + h
                    S.op("pe", lambda: T.transpose(out=ptr[:, n_ * 64:(n_ + 1) * 64], in_=VCT[:, h, ch * 128:(ch + 1) * 128], identity=identb[0:64, 0:64]),
                         [bVCT, bidb], [bptr])
            S.op("dve", lambda: V.tensor_copy(out=VCA[:, :, :, 0:64], in_=ptr[:, 0:256].rearrange("p (c h d) -> p c h d", c=2, h=2)), [bptr], [bVCA])

        try:
          ck("setup")
          load_x(0)
          for _ in A_gen(0):
              pass
          for i in range(npair):
              load_own(i)
              cur["i"] = i
              hp = i % 2
              agen = A_gen(i + 1) if i + 1 < npair else iter(())

              def step():
                  next(agen, None)
              ck("B1")
              so = i % 2
              S.op("dve", lambda: V.tensor_scalar(out=uTt[:], in0=uT[:, :, 0:128], scalar1=selcol[:, 0:1], scalar2=None, op0=ALU.mult), [buT, bselcol], [buTt])
              S.op("dve", lambda: V.scalar_tensor_tensor(out=uTo[:], in0=uT[:, :, 128:256], scalar=selcol[:, 1:2], in1=uTt[:], op0=ALU.mult, op1=ALU.add),
                   [buT, bselcol, buTt], [buTo])
              ck("B2")
              wq, bwq = unit(WbD[:, :, CQ:CQ + 512], 8)
              a = nxt("pmm")
              for k in range(8):
                  S.op("pe", lambda: T.matmul(pmm[a][:], lhsT=uTo[:, k, :], rhs=wq[:, k, :], start=(k == 0), stop=(k == 7)), [buTo, bwq], [bpmm[a]])
              S.op("act", lambda: A_.copy(out=qf[:], in_=pmm[a][:]), [bpmm[a]], [bqf])
              norm_rope(qf[:].rearrange("p (h d) -> p h d", h=8), 128, 8, gqb, bgq, rpO[so], brpO[so], bqf)
              S.op("pool", lambda: G.tensor_copy(out=qb[:], in_=qf[:]), [bqf], [bqb])
              for h in range(8):
                  S.op("pe", lambda: T.transpose(out=ptr[0:64, h * 128:(h + 1) * 128], in_=qb[:, h * 64:(h + 1) * 64], identity=identb[:]), [bqb, bidb], [bptr])
              S.op("act", lambda: A_.copy(out=QT[:].rearrange("d h q -> d (h q)"), in_=ptr[0:64, :]), [bptr], [bQT])
              a = nxt("pmm")
              for k in range(8):
                  S.op("pe", lambda: T.matmul(pmm[a][:, 0:24], lhsT=uTo[:, k, :], rhs=Wgn[:, k, :], start=(k == 0), stop=(k == 7)), [buTo, bWgn], [bpmm[a]])
              S.op("act", lambda: A_.activation(out=gns[:], in_=pmm[a][:, 0:24], func=AF.Sigmoid), [bpmm[a]], [bgns])
              wz, bwz = unit(WbD[:, :, CZN:CZN + 512], 8)
              a = nxt("pmm")
              for k in range(8):
                  S.op("pe", lambda: T.matmul(pmm[a][:], lhsT=uTo[:, k, :], rhs=wz[:, k, :], start=(k == 0), stop=(k == 7)), [buTo, bwz], [bpmm[a]])
              S.op("act", lambda: A_.activation(out=zs[:], in_=pmm[a][:], func=AF.Silu), [bpmm[a]], [bzs])
              wzr, bwzr = unit(WbD[:, :, CZR:CZR + 512], 8)
              a = nxt("pmm")
              for g in range(4):
                  for k in range(8):
                      S.op("pe", lambda: T.matmul(pmm[a][:, g * 128:(g + 1) * 128], lhsT=wzr[:, k, g * 128:(g + 1) * 128], rhs=uTo[:, k, :],
                                                  start=(k == 0), stop=(k == 7)), [buTo, bwzr], [bpmm[a]])
              S.op("act", lambda: A_.activation(out=zr[:].rearrange("p g t -> p (g t)"), in_=pmm[a][:], func=AF.Silu), [bpmm[a]], [bzr])
              ck("B4B5")
              for hk in range(2):
                  attn_branch(hk, [dict(KT=KCT[:, hk, ch * 128:(ch + 1) * 128], bK=bKCT, V=VCA[:, ch, hk, :], bV=bVCA, kind="tab", mk=cmk[so][:, ch, :], bm=bcmk[so])
                                   for ch in range(2)], 129, step)
                  S.op("dve", lambda: V.tensor_scalar(out=rden[:], in0=pO[:, :, 64], scalar1=1e-30, scalar2=None, op0=ALU.max), [bpO], [brd])
                  S.op("dve", lambda: V.reciprocal(out=rden[:], in_=rden[:]), [brd], [brd])
                  for h in range(4):
                      src1 = fbt[so][:] if h == 0 else impq[:]
                      S.op("dve", lambda: V.scalar_tensor_tensor(out=impq[:], in0=pO[:, h, 65:129], scalar=rden[:, h:h + 1], in1=src1, op0=ALU.mult, op1=ALU.add),
                           [bpO, brd, bfbt[so], bimp], [bimp])
                  finish_branch(hk, 0, True)
                  S.op("dve", lambda: V.max(out=m8[:, 0:8], in_=impq[:]), [bimp], [bimp])
                  S.op("dve", lambda: V.match_replace(out=impw[:], in_to_replace=m8[:, 0:8], in_values=impq[:], imm_value=-1e30), [bimp], [bimp])
                  S.op("dve", lambda: V.max(out=m8[:, 8:16], in_=impw[:]), [bimp], [bimp])
                  S.op("dve", lambda: V.tensor_scalar(out=selb[:], in0=impq[:], scalar1=m8[:, 15:16], scalar2=None, op0=ALU.is_ge), [bimp], [bselb])
                  if DBG == i:
                      S.op("dve", lambda: V.tensor_copy(out=dbgt[:, 0, :], in_=selb[:]), [bselb], [bdbgt])
                      S.dma("pool", dbg_out["d_sel"][hk], dbgt[:, 0, :], reads=[bdbgt])
                      S.dma("pool", dbg_out["d_imp"][hk], impq[:], reads=[bimp])
                  S.op("pe", lambda: T.transpose(out=ptr[0:64, 0:128], in_=selb[:], identity=identb[:]), [bselb, bidb], [bptr])
                  S.op("act", lambda: A_.copy(out=selT[:], in_=ptr[0:64, 0:128]), [bptr], [bselT])
                  nj = 2 * i + 2
                  attn_branch(hk, [dict(KT=KST[:, hk, j * 128:(j + 1) * 128], bK=bKST[j], V=VSa[:, j, hk, :], bV=bVSa[j], kind="sel", j=j,
                                        dm=(None if j < 2 * i else j - 2 * i)) for j in range(nj)], 65, step)
                  finish_branch(hk, 1, False)
                  js = [(m, 2 * i - 4 + m) for m in range(6) if 2 * i - 4 + m >= 0]
                  attn_branch(hk, [dict(KT=KWT[:, hk, (j % 8) * 128:(j % 8 + 1) * 128], bK=bKWT[j % 8], V=VWa[:, j % 8, hk, :], bV=bVWa[j % 8], kind="tab",
                                        mk=wmask[:, m, :], bm=bwmask) for (m, j) in js], 65, step)
                  finish_branch(hk, 2, False)
              for _ in agen:
                  pass
              ck("B6")
              S.op("dve", lambda: V.tensor_tensor(out=ab[:], in0=acc[:].rearrange("p h d -> p (h d)"), in1=zs[:], op=ALU.mult), [bacc, bzs], [bab])
              for k in range(4):
                  S.op("pe", lambda: T.transpose(out=ptr[:, k * 128:(k + 1) * 128], in_=ab[:, k * 128:(k + 1) * 128], identity=identb[:]), [bab, bidb], [bptr])
              S.op("act", lambda: A_.copy(out=aT[:].rearrange("p k t -> p (k t)"), in_=ptr[:, 0:512]), [bptr], [baT])
              S.op("pool", lambda: G.tensor_scalar(out=rt1[:], in0=hsb[hp][:, :, 0:128], scalar1=selcol[:, 0:1], scalar2=None, op0=ALU.mult), [bhsb[hp], bselcol], [brt1])
              S.op("dve", lambda: V.scalar_tensor_tensor(out=rt1[:], in0=hsb[hp][:, :, 128:256], scalar=selcol[:, 1:2], in1=rt1[:], op0=ALU.mult, op1=ALU.add),
                   [bhsb[hp], bselcol, brt1], [brt1])
              if DBG == i:
                  S.dma("pool", dbg_out["d_acc"], acc[:].rearrange("p h d -> p (h d)"), reads=[bacc])
                  S.dma("pool", dbg_out["d_hs"], rt1[:].rearrange("p g t -> p (g t)"), reads=[brt1])
              S.op("pool", lambda: G.tensor_tensor(out=rT[:], in0=rt1[:], in1=zr[:], op=ALU.mult), [brt1, bzr], [brT])
              ck("B7")
              wpa, bwpa = None, None
              for rnd in range(2):
                  for ab_ in range(2):
                      wg, bwg = unit(WbD[:, :, CGM + ab_ * 1024 + rnd * 512:CGM + ab_ * 1024 + (rnd + 1) * 512], 8)
                      a = nxt("pmm")
                      for g in range(4):
                          for k in range(8):
                              S.op("pe", lambda: T.matmul(pmm[a][:, g * 128:(g + 1) * 128], lhsT=wg[:, k, g * 128:(g + 1) * 128], rhs=uTo[:, k, :],
                                                          start=(k == 0), stop=(k == 7)), [buTo, bwg], [bpmm[a]])
                      S.op("act", lambda: A_.activation(out=gm[:, ab_, :], in_=pmm[a][:], func=AF.Sigmoid), [bpmm[a]], [bgm])
                  if rnd == 0:
                      wpa, bwpa = unit(WpaD, 4)
                      wpb, bwpb = unit(WpbD, 4)
                  a = nxt("pmm")
                  for g in range(4):
                      fc = rnd * 4 + g
                      for k in range(4):
                          S.op("pe", lambda: T.matmul(pmm[a][:, g * 128:(g + 1) * 128], lhsT=wpa[:, k, fc * 128:(fc + 1) * 128], rhs=aT[:, k, :],
                                                      start=(k == 0), stop=(k == 3)), [baT, bwpa], [bpmm[a]])
                  S.op("dve", lambda: V.tensor_tensor(out=mt1[:], in0=pmm[a][:], in1=gm[:, 0, :], op=ALU.mult), [bpmm[a], bgm], [bmt])
                  a = nxt("pmm")
                  for g in range(4):
                      fc = rnd * 4 + g
                      for k in range(4):
                          S.op("pe", lambda: T.matmul(pmm[a][:, g * 128:(g + 1) * 128], lhsT=wpb[:, k, fc * 128:(fc + 1) * 128], rhs=rT[:, k, :],
                                                      start=(k == 0), stop=(k == 3)), [brT, bwpb], [bpmm[a]])
                  S.op("dve", lambda: V.tensor_tensor(out=mt2[:], in0=pmm[a][:], in1=gm[:, 1, :], op=ALU.mult), [bpmm[a], bgm], [bmt])
                  S.op("pool", lambda: G.tensor_tensor(out=mT[:, rnd * 4:(rnd + 1) * 4, :].rearrange("p g t -> p (g t)"), in0=mt1[:], in1=mt2[:], op=ALU.add), [bmt], [bmT])
              wo0, bwo0 = unit(WoutD[:, 0:4, :], 4)
              wo1, bwo1 = unit(WoutD[:, 4:8, :], 4)
              for hf in range(2):
                  a = nxt("pmm")
                  for k in range(8):
                      wsrc = wo0 if k < 4 else wo1
                      S.op("pe", lambda: T.matmul(pmm[a][:], lhsT=mT[:, k, :], rhs=wsrc[:, k % 4, hf * 512:(hf + 1) * 512], start=(k == 0), stop=(k == 7)),
                           [bmT, bwo0, bwo1], [bpmm[a]])
                  S.op("dve", lambda: V.tensor_tensor(out=xo[so][:, hf * 512:(hf + 1) * 512], in0=pmm[a][:], in1=xo[so][:, hf * 512:(hf + 1) * 512], op=ALU.add),
                       [bpmm[a], bxo[so]], [bxo[so]])
              S.dma("pool", oy_d[i * 128:(i + 1) * 128, :], xo[so][:], reads=[bxo[so]])

        except StopBuild:
            pass
        S.barrier()
        esP.close()
        esS = ExitStack()
        cur_es[0] = esS
        NS = 16

        def unitw(src_ap, nk, w):
            sl = ust["n"] % 4
            ust["n"] += 1
            view = wu[sl][:].rearrange("p (k c) -> p k c", k=nk)[:, :, 0:w]
            S.dma("sp", view, src_ap, reads=[bWbD], writes=[bwu[sl]])
            return view, bwu[sl]

        def norm_rope2(x3, P, nh, gb, bg, cosb, sinb, brope, bx):
            sq = nr_sq[0:P, 0:nh * 64].rearrange("p (h d) -> p h d", h=nh)
            S.op("dve", lambda: V.tensor_tensor(out=sq, in0=x3, in1=x3, op=ALU.mult), [bx], [bnr])
            S.op("dve", lambda: V.tensor_reduce(out=nr_ss[0:P, 0:nh], in_=sq, axis=AX.X, op=ALU.add), [bnr], [bnr])
            S.op("act", lambda: A_.activation(out=nr_ss[0:P, 0:nh], in_=nr_ss[0:P, 0:nh], func=AF.Sqrt, scale=1.0 / 64, bias=EPS), [bnr], [bnr])
            S.op("dve", lambda: V.reciprocal(out=nr_ss[0:P, 0:nh], in_=nr_ss[0:P, 0:nh]), [bnr], [bnr])
            S.op("dve", lambda: V.tensor_tensor(out=x3, in0=x3, in1=bc_last(nr_ss[0:P, 0:nh], P, nh, 64), op=ALU.mult), [bx, bnr], [bx])
            S.op("dve", lambda: V.tensor_tensor(out=x3, in0=x3, in1=bc_mid(gb[0:P, :], P, nh, 64), op=ALU.mult), [bx, bg], [bx])
            x1, x2 = x3[:, :, 0:8], x3[:, :, 8:16]
            ta, tb, tc, td = (nr_t[0:P, i, 0:nh, :] for i in range(4))
            S.op("dve", lambda: V.tensor_tensor(out=ta, in0=x1, in1=cosb, op=ALU.mult), [bx, brope], [bnr])
            S.op("dve", lambda: V.tensor_tensor(out=tb, in0=x2, in1=sinb, op=ALU.mult), [bx, brope], [bnr])
            S.op("dve", lambda: V.tensor_tensor(out=tc, in0=x2, in1=cosb, op=ALU.mult), [bx, brope], [bnr])
            S.op("dve", lambda: V.tensor_tensor(out=td, in0=x1, in1=sinb, op=ALU.mult), [bx, brope], [bnr])
            S.op("dve", lambda: V.tensor_tensor(out=x1, in0=ta, in1=tb, op=ALU.subtract), [bnr], [bx])
            S.op("dve", lambda: V.tensor_tensor(out=x2, in0=tc, in1=td, op=ALU.add), [bnr], [bx])

        pS = sb("pS", [16, D_IN]); bpS = Buf("pS")
        xs = sb("xs", [16, 1024]); bxs = Buf("xs")
        xsn = sb("xsn", [16, 1024], BF16); bxsn = Buf("xsn")
        ss2 = sb("ss2", [16, 2]); bss2 = Buf("ss2")
        uTs = sb("uTs", [128, 8, 16], BF16); buTs = Buf("uTs")
        rpS = sb("rpS", [16, 16]); brpS = Buf("rpS")
        qTs = sb("qTs", [128, 8, 16], BF16); bqTs = Buf("qTs")
        knT = sb("knT", [64, 4, 16], BF16); bknT = Buf("knT")
        Vsn = sb("Vsn", [16, 2, 65], BF16); Vwn = sb("Vwn", [16, 2, 65], BF16); bVn = Buf("Vn")
        tb16 = sb("tb16", [16, 1024], BF16); btb16 = Buf("tb16")
        tf16 = sb("tf16", [16, 2048]); btf16 = Buf("tf16")
        idxi = sb("idxi", [128, 64], I32); idxf = sb("idxf", [128, 64]); pm8 = sb("pm8", [128, 1]); bidx = Buf("idx")
        Gt = [sb("Gt%d" % i, [128, 4096]) for i in range(2)]; bGt = [Buf("Gt%d" % i) for i in range(2)]
        Es = sb("Es", [128, 256]); bEs = Buf("Es")
        Pts = sb("Pts", [128, 256], BF16); bPts = Buf("Pts")
        Oc = sb("Oc", [4, 2, 194]); bOc = Buf("Oc")
        Os = sb("Os", [4, 2, 65]); bOs = Buf("Os")
        Ow = sb("Ow", [4, 2, 65]); bOw = Buf("Ow")
        Oc32 = sb("Oc32", [32, 4, 194]); Os32 = sb("Os32", [32, 4, 65]); Ow32 = sb("Ow32", [32, 4, 65]); bO32 = Buf("O32")
        fS = sb("fS", [32, 129]); bfS = Buf("fS")
        imp32 = sb("imp32", [32, 136]); impw32 = sb("impw32", [32, 136]); m8s = sb("m8s", [32, 16]); bimp32 = Buf("imp32")
        rd32 = sb("rd32", [32, 3, 4]); brd32 = Buf("rd32")
        sel4 = sb("sel4", [32, 128, 4]); bsel4 = Buf("sel4")
        selM = sb("selM", [128, 32, 4]); bselM = Buf("selM")
        RQ = sb("RQ", [128, 32, 8], BF16); bRQ = Buf("RQ")
        Pn = sb("Pn", [16, 4], BF16); Pnf = sb("Pnf", [16, 4]); bPn = Buf("Pn")
        g32 = sb("g32", [32, 12]); z32 = sb("z32", [32, 256]); acc32 = sb("acc32", [32, 4, 64]); tmp32 = sb("tmp32", [32, 4, 64]); coef32 = sb("coef32", [32, 4]); b32 = Buf("b32")
        a16 = sb("a16", [16, 512]); ba16 = Buf("a16")
        aTs = sb("aTs", [128, 4, 16], BF16); baTs = Buf("aTs")
        rTs = sb("rTs", [128, 4, 16], BF16); brTs = Buf("rTs")
        mTs = sb("mTs", [128, 8, 16], BF16); bmTs = Buf("mTs")
        cwB = sb("cwB", [16, 4, 512]); cbB = sb("cbB", [16, 512]); scv = sb("scv", [16, 3, 512]); bcwB = Buf("cwB")
        xcs = sb("xcs", [16, 512]); bxcs = Buf("xcs")
        xcT = sb("xcT", [128, 4, 16]); xcTb = sb("xcTb", [128, 4, 16], BF16); bxcT = Buf("xcT")
        h0T = sb("h0T", [128, 4, 16]); hnT = sb("hnT", [128, 4, 16]); bhT = Buf("hT")
        zrT = sb("zrT", [128, 4, 16]); bzrT = Buf("zrT")
        gs = [sb("gs%d" % i, [128, 16]) for i in range(4)]; bgs = Buf("gs")
        bscr = {n: Buf(n) for n in ("Oc", "Os", "Ow", "Sel", "G", "Z", "A")}

        esC = ExitStack()
        cur_es[0] = esC
        Gb = sb("Gb", [128, 4, 16, 64], BF16); bGb = Buf("Gb")
        YT = sb("YT", [128, 4, 8, 512], BF16); bYT = Buf("YT")
        W1p = [sb("W1p%d" % i, [128, 16, 64], BF16) for i in range(2)]; bW1p = Buf("W1p")
        hidS = [sb("hidS%d" % i, [64, 2, 512], BF16) for i in range(2)]; bhidS = Buf("hidS")
        kcs = sb("kcs", [128, 8, 64]); bkcs = Buf("kcs")
        kcsb = sb("kcsb", [128, 8, 64], BF16); bkcsb = Buf("kcsb")
        KCTs = sb("KCTs", [64, 8, 128], BF16); bKCTs = Buf("KCTs")
        VCAs = sb("VCAs", [128, 4, 2, 194], BF16); bVCAs = Buf("VCAs")
        ropeCs = sb("ropeCs", [128, 4, 16]); bropeCs = Buf("ropeCs")
        cmS = sb("cmS", [128, 4]); bcmS = Buf("cmS")
        S.dma("sp", xs[:], xs_d, writes=[bxs])
        S.dma("sp", rpS[:], ropeS_d, writes=[brpS])
        S.op("act", lambda: A_.activation(out=xsn[:], in_=xs[:], func=AF.Square, accum_out=ss2[:, 0:1]), [bxs], [bxsn, bss2])
        S.op("act", lambda: A_.activation(out=ss2[:, 1:2], in_=ss2[:, 0:1], func=AF.Sqrt, scale=1.0 / D_MODEL, bias=EPS), [bss2], [bss2])
        S.op("dve", lambda: V.reciprocal(out=ss2[:, 1:2], in_=ss2[:, 1:2]), [bss2], [bss2])
        S.op("dve", lambda: V.tensor_scalar(out=xsn[:], in0=xs[:], scalar1=ss2[:, 1:2], scalar2=None, op0=ALU.mult), [bxs, bss2], [bxsn])
        for k in range(8):
            S.op("pe", lambda: T.transpose(out=ptr[:, k * 16:(k + 1) * 16], in_=xsn[:, k * 128:(k + 1) * 128], identity=identb[0:16, 0:16]), [bxsn, bidb], [bptr])
        S.op("act", lambda: A_.copy(out=uTs[:].rearrange("p k s -> p (k s)"), in_=ptr[:, 0:128]), [bptr], [buTs])
        for u in range(10):
            c0 = u * 512
            w = min(512, D_IN - c0)
            wv, bwv = unitw(WbD[:, :, c0:c0 + w], 8, w)
            a = nxt("pmm")
            for k in range(8):
                S.op("pe", lambda: T.matmul(pmm[a][0:16, 0:w], lhsT=uTs[:, k, :], rhs=wv[:, k, :], start=(k == 0), stop=(k == 7)), [buTs, bwv], [bpmm[a]])
            S.op("act", lambda: A_.copy(out=pS[:, c0:c0 + w], in_=pmm[a][0:16, 0:w]), [bpmm[a]], [bpS])
        cosS, sinS = rpS[:, 0:8], rpS[:, 8:16]
        norm_rope2(pS[:, 0:512].rearrange("p (h d) -> p h d", h=8), 16, 8, gqb, bgq, bc_mid(cosS, 16, 8, 8), bc_mid(sinS, 16, 8, 8), brpS, bpS)
        norm_rope2(pS[:, CKV + 256:CKV + 384].rearrange("p (h d) -> p h d", h=2), 16, 2, gksb, bgks, bc_mid(cosS, 16, 2, 8), bc_mid(sinS, 16, 2, 8), brpS, bpS)
        norm_rope2(pS[:, CKV + 512:CKV + 640].rearrange("p (h d) -> p h d", h=2), 16, 2, gkwb, bgkw, bc_mid(cosS, 16, 2, 8), bc_mid(sinS, 16, 2, 8), brpS, bpS)
        S.dma("pool", okvcs_d, pS[:, CKV:CKV + 256], reads=[bpS])
        S.dma("pool", okvss_d, pS[:, CKV + 256:CKV + 512], reads=[bpS])
        S.dma("pool", okvws_d[:, 511, :], pS[:, CKV + 512:CKV + 768], reads=[bpS])
        S.dma("pool", okvws_d[:, 0:511, :], ckw_d[:, 1:512, :])
        S.dma("pool", oconvs_d[:, 0:2, :], sconv_d[:, 1:3, :])
        S.dma("pool", oconvs_d[:, 2, :], pS[:, CXR:CXR + 512], reads=[bpS])
        S.op("pool", lambda: G.tensor_copy(out=tb16[:, 0:512], in_=pS[:, 0:512]), [bpS], [btb16])
        for h in range(8):
            S.op("pe", lambda: T.transpose(out=ptr[0:64, h * 16:(h + 1) * 16], in_=tb16[:, h * 64:(h + 1) * 64], identity=identb[0:16, 0:16]), [btb16, bidb], [bptr])
        S.op("act", lambda: A_.copy(out=qTs[0:64, :, :].rearrange("p h s -> p (h s)"), in_=ptr[0:64, 0:128]), [bptr], [bqTs])
        S.op("dve", lambda: V.tensor_copy(out=qTs[64:128, :, :], in_=qTs[0:64, :, :]), [bqTs], [bqTs])
        S.op("pool", lambda: G.tensor_copy(out=tb16[:, 512:640], in_=pS[:, CKV + 256:CKV + 384]), [bpS], [btb16])
        S.op("pool", lambda: G.tensor_copy(out=tb16[:, 640:768], in_=pS[:, CKV + 512:CKV + 640]), [bpS], [btb16])
        for n_ in range(4):
            S.op("pe", lambda: T.transpose(out=ptr[0:64, n_ * 16:(n_ + 1) * 16], in_=tb16[:, 512 + n_ * 64:512 + (n_ + 1) * 64], identity=identb[0:16, 0:16]), [btb16, bidb], [bptr])
        S.op("act", lambda: A_.copy(out=knT[:].rearrange("p n s -> p (n s)"), in_=ptr[0:64, 0:64]), [bptr], [bknT])
        S.op("pool", lambda: G.memset(Vsn[:], 1.0), [], [bVn])
        S.op("pool", lambda: G.memset(Vwn[:], 1.0), [], [bVn])
        S.op("pool", lambda: G.tensor_copy(out=Vsn[:, :, 0:64], in_=pS[:, CKV + 384:CKV + 512].rearrange("p (h d) -> p h d", h=2)), [bpS], [bVn])
        S.op("pool", lambda: G.tensor_copy(out=Vwn[:, :, 0:64], in_=pS[:, CKV + 640:CKV + 768].rearrange("p (h d) -> p h d", h=2)), [bpS], [bVn])
        S.dma("sp", None, None, writes=[bcwB], fn=lambda: nc.sync.dma_start(out=cwB[:].rearrange("p t c -> p (t c)"), in_=conv_w_d.rearrange("t c -> (t c)").partition_broadcast(16)))
        S.dma("sp", None, None, writes=[bcwB], fn=lambda: nc.sync.dma_start(out=cbB[:], in_=conv_b_d.partition_broadcast(16)))
        S.dma("sp", scv[:], sconv_d, writes=[bcwB])
        S.op("dve", lambda: V.tensor_tensor(out=xcs[:], in0=pS[:, CXR:CXR + 512], in1=cwB[:, 3, :], op=ALU.mult), [bpS, bcwB], [bxcs])
        S.op("dve", lambda: V.tensor_tensor(out=xcs[:], in0=xcs[:], in1=cbB[:], op=ALU.add), [bxcs, bcwB], [bxcs])
        for tap in range(3):
            S.op("dve", lambda: V.tensor_tensor(out=tf16[:, 0:512], in0=scv[:, tap, :], in1=cwB[:, tap, :], op=ALU.mult), [bcwB], [btf16])
            S.op("dve", lambda: V.tensor_tensor(out=xcs[:], in0=xcs[:], in1=tf16[:, 0:512], op=ALU.add), [bxcs, btf16], [bxcs])
        a = nxt("pmm")
        for g in range(4):
            S.op("pe", lambda: T.transpose(out=pmm[a][:, g * 16:(g + 1) * 16], in_=xcs[:, g * 128:(g + 1) * 128], identity=identf[0:16, 0:16]), [bxcs, bidf], [bpmm[a]])
        S.op("act", lambda: A_.copy(out=xcT[:].rearrange("p g s -> p (g s)"), in_=pmm[a][:, 0:64]), [bpmm[a]], [bxcT])
        S.op("dve", lambda: V.tensor_copy(out=xcTb[:], in_=xcT[:]), [bxcT], [bxcT])
        for g in range(4):
            S.dma("sp", None, None, writes=[bhT], fn=lambda: nc.sync.dma_start(out=h0T[:, g, :], in_=sh_d[:, g * 128:(g + 1) * 128].rearrange("s c -> c s"), allow_slow_non_contiguous=True))
        for g in range(4):
            a = nxt("pmm")
            S.op("pe", lambda: T.matmul(pmm[a][:, 0:16], lhsT=WraB[:, g, :], rhs=xcTb[:, g, :], start=True, stop=True), [bWr, bxcT], [bpmm[a]])
            S.op("pe", lambda: T.matmul(pmm[a][:, 16:32], lhsT=WrxB[:, g, :], rhs=xcTb[:, g, :], start=True, stop=True), [bWr, bxcT], [bpmm[a]])
            S.op("act", lambda: A_.activation(out=gs[0][:], in_=pmm[a][:, 0:16], func=AF.Sigmoid, bias=braT[:, g:g + 1]), [bpmm[a], bbr], [bgs])
            S.op("act", lambda: A_.activation(out=gs[1][:], in_=pmm[a][:, 16:32], func=AF.Sigmoid, bias=brxT[:, g:g + 1]), [bpmm[a], bbr], [bgs])
            S.op("act", lambda: A_.activation(out=gs[2][:], in_=gs[0][:], func=AF.Exp, scale=clT[:, g:g + 1]), [bgs, bcl], [bgs])
            S.op("act", lambda: A_.activation(out=gs[3][:], in_=gs[0][:], func=AF.Exp, scale=cl2T[:, g:g + 1]), [bgs, bcl], [bgs])
            S.op("act", lambda: A_.activation(out=gs[3][:], in_=gs[3][:], func=AF.Sqrt, scale=-1.0, bias=1.0), [bgs], [bgs])
            S.op("dve", lambda: V.tensor_tensor(out=gs[1][:], in0=gs[1][:], in1=xcT[:, g, :], op=ALU.mult), [bgs, bxcT], [bgs])
            S.op("dve", lambda: V.tensor_tensor(out=gs[1][:], in0=gs[1][:], in1=gs[3][:], op=ALU.mult), [bgs], [bgs])
            S.op("dve", lambda: V.tensor_tensor(out=gs[2][:], in0=gs[2][:], in1=h0T[:, g, :], op=ALU.mult), [bgs, bhT], [bgs])
            S.op("dve", lambda: V.tensor_tensor(out=hnT[:, g, :], in0=gs[2][:], in1=gs[1][:], op=ALU.add), [bgs], [bhT])
        for g in range(4):
            S.dma("pool", None, None, reads=[bhT], fn=lambda: nc.gpsimd.dma_start(out=ohs_d[:, g * 128:(g + 1) * 128].rearrange("s c -> c s"), in_=hnT[:, g, :], allow_slow_non_contiguous=True))
        S.op("act", lambda: A_.activation(out=tf16[:, 0:512], in_=pS[:, CZR:CZR + 512], func=AF.Silu), [bpS], [btf16])
        a = nxt("pmm")
        for g in range(4):
            S.op("pe", lambda: T.transpose(out=pmm[a][:, g * 16:(g + 1) * 16], in_=tf16[:, g * 128:(g + 1) * 128], identity=identf[0:16, 0:16]), [btf16, bidf], [bpmm[a]])
        S.op("dve", lambda: V.tensor_tensor(out=rTs[:].rearrange("p g s -> p (g s)"), in0=pmm[a][:, 0:64], in1=hnT[:].rearrange("p g s -> p (g s)"), op=ALU.mult), [bpmm[a], bhT], [brTs])
        for kind, wd in enumerate((w1_k_d, w1_v_d)):
            S.dma("sp", Gt[0][:, 0:1024].rearrange("p (c e) -> p c e", c=16), wd.rearrange("(c a) d e -> (a d) c e", a=2), writes=[bGt[0]])
            S.op("dve", lambda: V.tensor_copy(out=W1p[kind][:].rearrange("p c e -> p (c e)"), in_=Gt[0][:, 0:1024]), [bGt[0]], [bW1p])
        S.op("pool", lambda: G.memset(VCAs[:], 1.0), [], [bVCAs])
        for ch in range(4):
            S.dma("sp", Gt[1][:, 0:129], As_d[ch * 128:(ch + 1) * 128, :], writes=[bGt[1]])
            for hk in range(2):
                S.op("dve", lambda: V.tensor_copy(out=VCAs[:, ch, hk, 65:194], in_=Gt[1][:, 0:129]), [bGt[1]], [bVCAs])
        S.dma("sp", ropeCs[:], ropeC_d[1:513, :].rearrange("(c p) e -> p c e", p=128), writes=[bropeCs])
        S.dma("sp", cmS[:], cmS_d, writes=[bcmS])
        S.dma("sp", fS[:], fS_d, writes=[bfS])
        S.dma("sp", idxi[:], ptrep_d, writes=[bidx])
        S.dma("sp", pm8[:], pm8_d, writes=[bidx])
        S.op("dve", lambda: V.tensor_copy(out=idxf[:], in_=idxi[:]), [bidx], [bidx])
        S.op("dve", lambda: V.tensor_scalar(out=idxf[:], in0=idxf[:], scalar1=8.0, scalar2=None, op0=ALU.mult), [bidx], [bidx])
        S.op("dve", lambda: V.tensor_scalar(out=idxf[:], in0=idxf[:], scalar1=pm8[:, 0:1], scalar2=None, op0=ALU.add), [bidx], [bidx])
        S.op("dve", lambda: V.tensor_copy(out=idxi[:], in_=idxf[:]), [bidx], [bidx])
        S.op("pool", lambda: G.memset(hidS[0][:], 0.0), [], [bhidS])
        S.op("pool", lambda: G.memset(hidS[1][:], 0.0), [], [bhidS])
        gcount = {"n": 0}

        def gather(pool_d, s, t):
            sl = gcount["n"] % 2
            gcount["n"] += 1
            col = s * 4 + t
            S.dma("pool", None, None, reads=[bidx], writes=[bGt[sl]], fn=lambda: nc.gpsimd.indirect_dma_start(
                out=Gt[sl][:], out_offset=None, in_=pool_d, in_offset=bass.IndirectOffsetOnAxis(ap=idxi[:, col:col + 1], axis=0)))
            return Gt[sl], bGt[sl]

        for s in range(ns_seq):
            for t in range(4):
                gt_, bgt_ = gather(poolc_d, s, t)
                src = gt_[:].rearrange("p (j k d) -> p k j d", j=16, k=4)
                S.op("dve", lambda: V.tensor_copy(out=Gb[:, 0:2, :, :], in_=src[:, 0:2, :, :]), [bgt_], [bGb])
                S.op("act", lambda: A_.copy(out=Gb[:, 2:4, :, :], in_=src[:, 2:4, :, :]), [bgt_], [bGb])
                for kvh in range(4):
                    for jc in range(8):
                        S.op("pe", lambda: T.transpose(out=ptr[:, jc * 128:(jc + 1) * 128], in_=Gb[:, kvh, 2 * jc:2 * jc + 2, :].rearrange("p j d -> p (j d)"), identity=identb[:]), [bGb, bidb], [bptr])
                    eng = "act" if kvh % 2 == 0 else "dve"
                    if eng == "act":
                        S.op("act", lambda: A_.copy(out=YT[:, kvh, :, t * 128:(t + 1) * 128], in_=ptr[:].rearrange("p (c s) -> p c s", c=8)), [bptr], [bYT])
                    else:
                        S.op("dve", lambda: V.tensor_copy(out=YT[:, kvh, :, t * 128:(t + 1) * 128], in_=ptr[:].rearrange("p (c s) -> p c s", c=8)), [bptr], [bYT])
            for kind in range(2):
                for h in range(2):
                    kvh = kind * 2 + h
                    a = nxt("pst")
                    for jc in range(8):
                        S.op("pe", lambda: T.matmul(pst[a][0:64, 0:511], lhsT=W1p[kind][:, jc, :], rhs=YT[:, kvh, jc, 0:511], start=(jc == 0), stop=False), [bW1p, bYT], [bpst[a]])
                    for jc in range(8):
                        S.op("pe", lambda: T.matmul(pst[a][0:64, 0:511], lhsT=W1p[kind][:, 8 + jc, :], rhs=YT[:, kvh, jc, 1:512], start=False, stop=(jc == 7)), [bW1p, bYT], [bpst[a]])
                    pbt = pbk if kind == 0 else pbv
                    S.op("act", lambda: A_.activation(out=hidS[kind][:, h, 0:511], in_=pst[a][0:64, 0:511], func=AF.Silu, bias=pbt[:, 0:1]), [bpst[a], bpb], [bhidS])
            a = nxt("pmm")
            for h in range(2):
                for ch in range(4):
                    n_ = h * 4 + ch
                    S.op("pe", lambda: T.matmul(pmm[a][:, n_ * 64:(n_ + 1) * 64], lhsT=hidS[0][:, h, ch * 128:(ch + 1) * 128], rhs=W2k[:], start=True, stop=True), [bhidS, bW2], [bpmm[a]])
            S.op("act", lambda: A_.copy(out=kcs[:].rearrange("p n d -> p (n d)"), in_=pmm[a][:]), [bpmm[a]], [bkcs])
            for h in range(2):
                norm_rope2(kcs[:, h * 4:(h + 1) * 4, :], 128, 4, gkcb, bgkc, ropeCs[:, :, 0:8], ropeCs[:, :, 8:16], bropeCs, bkcs)
            S.op("pool", lambda: G.tensor_copy(out=kcsb[:], in_=kcs[:]), [bkcs], [bkcsb])
            for n_ in range(8):
                S.op("pe", lambda: T.transpose(out=ptr[0:64, n_ * 128:(n_ + 1) * 128], in_=kcsb[:, n_, :], identity=identb[:]), [bkcsb, bidb], [bptr])
            S.op("act", lambda: A_.copy(out=KCTs[:].rearrange("p n c -> p (n c)"), in_=ptr[0:64, :]), [bptr], [bKCTs])
            a = nxt("pmm")
            for h in range(2):
                for ch in range(4):
                    n_ = h * 4 + ch
                    S.op("pe", lambda: T.matmul(pmm[a][:, n_ * 64:(n_ + 1) * 64], lhsT=hidS[1][:, h, ch * 128:(ch + 1) * 128], rhs=W2v[:], start=True, stop=True), [bhidS, bW2], [bpmm[a]])
            S.op("dve", lambda: V.tensor_copy(out=VCAs[:, :, :, 0:64], in_=pmm[a][:].rearrange("p (h c d) -> p c h d", h=2, c=4)), [bpmm[a]], [bVCAs])
            for hk in range(2):
                for ch in range(4):
                    S.op("pe", lambda: T.matmul(pmk[:, 0, ch * 4:(ch + 1) * 4], lhsT=KCTs[:, hk * 4 + ch, :], rhs=qTs[0:64, 4 * hk:4 * hk + 4, s], start=True, stop=True), [bKCTs, bqTs], [bpmk[0]])
                S.op("act", lambda: A_.activation(out=Es[:, 0:16], in_=pmk[:, 0, 0:16], func=AF.Exp, scale=SCALE), [bpmk[0]], [bEs])
                S.op("dve", lambda: V.tensor_tensor(out=Pts[:, 0:16].rearrange("p (c h) -> p c h", c=4), in0=Es[:, 0:16].rearrange("p (c h) -> p c h", c=4),
                                                    in1=bc_last(cmS[:], 128, 4, 4), op=ALU.mult), [bEs, bcmS], [bPts])
                for ch in range(4):
                    S.op("pe", lambda: T.matmul(pO[0:4, 0, 0:194], lhsT=Pts[:, ch * 4:(ch + 1) * 4], rhs=VCAs[:, ch, hk, :], start=(ch == 0), stop=(ch == 3)), [bPts, bVCAs], [bpO])
                S.op("act", lambda: A_.copy(out=Oc[:, hk, :], in_=pO[0:4, 0, 0:194]), [bpO], [bOc])
            S.dma("sp", scrOc[:, 2 * s:2 * s + 2, :], Oc[:], reads=[bOc], writes=[bscr["Oc"]])
        S.dma("sp", Oc32[:], scrOc.rearrange("h q e -> q h e"), reads=[bscr["Oc"]], writes=[bO32])
        S.op("dve", lambda: V.tensor_scalar(out=rd32[:, 0, :], in0=Oc32[:, :, 64], scalar1=1e-30, scalar2=None, op0=ALU.max), [bO32], [brd32])
        S.op("dve", lambda: V.reciprocal(out=rd32[:, 0, :], in_=rd32[:, 0, :]), [brd32], [brd32])
        S.op("pool", lambda: G.memset(imp32[:], -1e30), [], [bimp32])
        for h in range(4):
            src1 = fS[:] if h == 0 else imp32[:, 0:129]
            S.op("dve", lambda: V.scalar_tensor_tensor(out=imp32[:, 0:129], in0=Oc32[:, h, 65:194], scalar=rd32[:, 0, h:h + 1], in1=src1, op0=ALU.mult, op1=ALU.add),
                 [bO32, brd32, bfS, bimp32], [bimp32])
        S.op("dve", lambda: V.max(out=m8s[:, 0:8], in_=imp32[:]), [bimp32], [bimp32])
        S.op("dve", lambda: V.match_replace(out=impw32[:], in_to_replace=m8s[:, 0:8], in_values=imp32[:], imm_value=-1e30), [bimp32], [bimp32])
        S.op("dve", lambda: V.max(out=m8s[:, 8:16], in_=impw32[:]), [bimp32], [bimp32])
        S.op("dve", lambda: V.tensor_scalar(out=impw32[:, 0:128], in0=imp32[:, 0:128], scalar1=m8s[:, 15:16], scalar2=None, op0=ALU.is_ge), [bimp32], [bimp32])
        S.op("dve", lambda: V.tensor_copy(out=sel4[:], in_=bc_last(impw32[:, 0:128], 32, 128, 4)), [bimp32], [bsel4])
        S.dma("sp", scrSel, sel4[:].rearrange("q b r -> q (b r)"), reads=[bsel4], writes=[bscr["Sel"]])
        for hf in range(2):
            S.dma("sp", None, None, reads=[bscr["Sel"]], writes=[bselM], fn=lambda: nc.sync.dma_start(
                out=selM[:, hf * 16:(hf + 1) * 16, :], in_=scrSel[hf * 16:(hf + 1) * 16, :].rearrange("q (t p) -> p q t", p=128), allow_slow_non_contiguous=True))
        S.barrier()
        esC.close()
        cur_es[0] = esS
        Kb = sb("Kb", [128, 2, 16, 64], BF16); bKb = Buf("Kb")
        Vs = sb("Vs", [128, 4, 16, 2, 65], BF16); bVs = Buf("Vs")
        KTs = sb("KTs", [128, 8, 128], BF16); bKTs = Buf("KTs")
        Wt = sb("Wt", [128, 4, 256]); bWt = Buf("Wt")
        Kwb = sb("Kwb", [128, 4, 2, 64], BF16); bKwb = Buf("Kwb")
        Vw = sb("Vw", [128, 4, 2, 65], BF16); bVw = Buf("Vw")
        KwT = sb("KwT", [64, 8, 128], BF16); bKwT = Buf("KwT")
        S.op("pool", lambda: G.memset(RQ[:], 0.0), [], [bRQ])
        for hk in range(2):
            S.op("dve", lambda: V.tensor_copy(out=RQ[0:64, :, :].rearrange("p (s k) e -> p s k e", k=2)[:, :, hk, 0:4], in_=qTs[0:64, 4 * hk:4 * hk + 4, :].rearrange("p h s -> p s h")), [bqTs], [bRQ])
            S.op("dve", lambda: V.tensor_copy(out=RQ[64:128, :, :].rearrange("p (s k) e -> p s k e", k=2)[:, :, hk, 4:8], in_=qTs[64:128, 4 * hk:4 * hk + 4, :].rearrange("p h s -> p s h")), [bqTs], [bRQ])
        S.op("pool", lambda: G.memset(Vs[:], 1.0), [], [bVs])
        for s in range(ns_seq):
            for t in range(4):
                gt_, bgt_ = gather(pools_d, s, t)
                src = gt_[:].rearrange("p (j k d) -> p k j d", j=16, k=4)
                S.op("dve", lambda: V.tensor_copy(out=Kb[:], in_=src[:, 0:2, :, :]), [bgt_], [bKb])
                S.op("act", lambda: A_.copy(out=Vs[:, t, :, :, 0:64], in_=gt_[:].rearrange("p (j k d) -> p j k d", j=16, k=4)[:, :, 2:4, :]), [bgt_], [bVs])
                for hk in range(2):
                    for jp in range(8):
                        S.op("pe", lambda: T.transpose(out=ptr[:, jp * 128:(jp + 1) * 128], in_=Kb[:, hk, 2 * jp:2 * jp + 2, :].rearrange("p j d -> p (j d)"), identity=identb[:]), [bKb, bidb], [bptr])
                    S.op("act", lambda: A_.copy(out=KTs[:].rearrange("p c s -> p (c s)"), in_=ptr[:]), [bptr], [bKTs])
                    for jp in range(8):
                        c0 = (t * 8 + jp) * 8
                        S.op("pe", lambda: T.matmul(pst[hk][:, c0:c0 + 8], lhsT=KTs[:, jp, :], rhs=RQ[:, s * 2 + hk, :], start=True, stop=True), [bKTs, bRQ], [bpst[hk]])
            for hk in range(2):
                q_ = s * 2 + hk
                S.op("act", lambda: A_.activation(out=Es[:], in_=pst[hk][:, 0:256], func=AF.Exp, scale=SCALE), [bpst[hk]], [bEs])
                S.op("dve", lambda: V.tensor_tensor(out=Pts[:].rearrange("p (t c) -> p t c", t=4), in0=Es[:].rearrange("p (t c) -> p t c", t=4),
                                                    in1=bc_last(selM[:, q_, :], 128, 4, 64), op=ALU.mult), [bEs, bselM], [bPts])
                first = True
                for t in range(4):
                    for jp in range(8):
                        for a2 in range(2):
                            c0 = ((t * 8 + jp) * 2 + a2) * 4
                            S.op("pe", lambda: T.matmul(pO[0:4, 0, 0:65], lhsT=Pts[:, c0:c0 + 4], rhs=Vs[:, t, 2 * jp + a2, hk, :], start=first, stop=False,
                                                        skip_group_check=True), [bPts, bVs], [bpO])
                            first = False
                S.op("pe", lambda: T.matmul(pmk[0:16, 0, 0:4], lhsT=knT[:, hk, :], rhs=qTs[0:64, 4 * hk:4 * hk + 4, s], start=True, stop=True), [bknT, bqTs], [bpmk[0]])
                S.op("act", lambda: A_.activation(out=Pnf[:], in_=pmk[0:16, 0, 0:4], func=AF.Exp, scale=SCALE), [bpmk[0]], [bPn])
                S.op("dve", lambda: V.tensor_scalar(out=Pn[:], in0=Pnf[:], scalar1=identf[0:16, s:s + 1], scalar2=None, op0=ALU.mult), [bPn, bidf], [bPn])
                S.op("pe", lambda: T.matmul(pO[0:4, 0, 0:65], lhsT=Pn[:], rhs=Vsn[:, hk, :], start=False, stop=True, skip_group_check=True), [bPn, bVn], [bpO])
                S.op("act", lambda: A_.copy(out=Os[:, hk, :], in_=pO[0:4, 0, 0:65]), [bpO], [bOs])
            S.dma("sp", scrOs[:, 2 * s:2 * s + 2, :], Os[:], reads=[bOs], writes=[bscr["Os"]])
        S.op("pool", lambda: G.memset(Vw[:], 1.0), [], [bVw])
        for s in range(ns_seq):
            S.dma("sp", Wt[:], ckw_d[s].rearrange("(t p) e -> p t e", p=128), writes=[bWt])
            S.op("dve", lambda: V.tensor_copy(out=Kwb[:], in_=Wt[:, :, 0:128].rearrange("p t (h d) -> p t h d", h=2)), [bWt], [bKwb])
            S.op("pool", lambda: G.tensor_copy(out=Vw[:, :, :, 0:64], in_=Wt[:, :, 128:256].rearrange("p t (h d) -> p t h d", h=2)), [bWt], [bVw])
            for t in range(4):
                for hk in range(2):
                    n_ = t * 2 + hk
                    S.op("pe", lambda: T.transpose(out=ptr[0:64, n_ * 128:(n_ + 1) * 128], in_=Kwb[:, t, hk, :], identity=identb[:]), [bKwb, bidb], [bptr])
            S.op("act", lambda: A_.copy(out=KwT[:].rearrange("p n c -> p (n c)"), in_=ptr[0:64, :]), [bptr], [bKwT])
            for hk in range(2):
                for t in range(4):
                    c0 = (hk * 4 + t) * 4
                    S.op("pe", lambda: T.matmul(pmk[:, 1, c0:c0 + 4], lhsT=KwT[:, t * 2 + hk, :], rhs=qTs[0:64, 4 * hk:4 * hk + 4, s], start=True, stop=True), [bKwT, bqTs], [bpmk[1]])
            S.op("act", lambda: A_.activation(out=Pts[:, 0:32], in_=pmk[:, 1, 0:32], func=AF.Exp, scale=SCALE), [bpmk[1]], [bPts])
            for hk in range(2):
                q_ = s * 2 + hk
                for t in range(4):
                    c0 = (hk * 4 + t) * 4
                    S.op("pe", lambda: T.matmul(pO[0:4, 0, 0:65], lhsT=Pts[:, c0:c0 + 4], rhs=Vw[:, t, hk, :], start=(t == 0), stop=False, skip_group_check=True), [bPts, bVw], [bpO])
                S.op("pe", lambda: T.matmul(pmk[0:16, 0, 0:4], lhsT=knT[:, 2 + hk, :], rhs=qTs[0:64, 4 * hk:4 * hk + 4, s], start=True, stop=True), [bknT, bqTs], [bpmk[0]])
                S.op("act", lambda: A_.activation(out=Pnf[:], in_=pmk[0:16, 0, 0:4], func=AF.Exp, scale=SCALE), [bpmk[0]], [bPn])
                S.op("dve", lambda: V.tensor_scalar(out=Pn[:], in0=Pnf[:], scalar1=identf[0:16, s:s + 1], scalar2=None, op0=ALU.mult), [bPn, bidf], [bPn])
                S.op("pe", lambda: T.matmul(pO[0:4, 0, 0:65], lhsT=Pn[:], rhs=Vwn[:, hk, :], start=False, stop=True, skip_group_check=True), [bPn, bVn], [bpO])
                S.op("act", lambda: A_.copy(out=Ow[:, hk, :], in_=pO[0:4, 0, 0:65]), [bpO], [bOw])
            S.dma("sp", scrOw[:, 2 * s:2 * s + 2, :], Ow[:], reads=[bOw], writes=[bscr["Ow"]])
        S.dma("sp", Os32[:], scrOs.rearrange("h q e -> q h e"), reads=[bscr["Os"]], writes=[bO32])
        S.dma("sp", Ow32[:], scrOw.rearrange("h q e -> q h e"), reads=[bscr["Ow"]], writes=[bO32])
        S.op("act", lambda: A_.activation(out=tf16[:, 0:24], in_=pS[:, CGN:CGN + 24], func=AF.Sigmoid), [bpS], [btf16])
        S.dma("sp", scrG, tf16[:, 0:24], reads=[btf16], writes=[bscr["G"]])
        S.dma("sp", g32[:], scrG.rearrange("s (k e) -> (s k) e", k=2), reads=[bscr["G"]], writes=[b32])
        S.op("act", lambda: A_.activation(out=tf16[:, 512:1024], in_=pS[:, CZN:CZN + 512], func=AF.Silu), [bpS], [btf16])
        S.dma("sp", scrZ, tf16[:, 512:1024], reads=[btf16], writes=[bscr["Z"]])
        S.dma("sp", z32[:], scrZ.rearrange("s (k e) -> (s k) e", k=2), reads=[bscr["Z"]], writes=[b32])
        for br, O32 in enumerate((Oc32, Os32, Ow32)):
            if br > 0:
                S.op("dve", lambda: V.tensor_scalar(out=rd32[:, br, :], in0=O32[:, :, 64], scalar1=1e-30, scalar2=None, op0=ALU.max), [bO32], [brd32])
                S.op("dve", lambda: V.reciprocal(out=rd32[:, br, :], in_=rd32[:, br, :]), [brd32], [brd32])
            S.op("dve", lambda: V.tensor_tensor(out=coef32[:], in0=rd32[:, br, :], in1=g32[:].rearrange("q (h b) -> q h b", b=3)[:, :, br], op=ALU.mult), [brd32, b32], [b32])
            if br == 0:
                S.op("dve", lambda: V.tensor_tensor(out=acc32[:], in0=O32[:, :, 0:64], in1=bc_last(coef32[:], 32, 4, 64), op=ALU.mult), [bO32, b32], [b32])
            else:
                S.op("dve", lambda: V.tensor_tensor(out=tmp32[:], in0=O32[:, :, 0:64], in1=bc_last(coef32[:], 32, 4, 64), op=ALU.mult), [bO32, b32], [b32])
                S.op("dve", lambda: V.tensor_tensor(out=acc32[:], in0=acc32[:], in1=tmp32[:], op=ALU.add), [b32], [b32])
        S.op("dve", lambda: V.tensor_tensor(out=acc32[:].rearrange("q h d -> q (h d)"), in0=acc32[:].rearrange("q h d -> q (h d)"), in1=z32[:], op=ALU.mult), [b32], [b32])
        S.dma("sp", scrA, acc32[:].rearrange("q h d -> q (h d)"), reads=[b32], writes=[bscr["A"]])
        S.dma("sp", a16[:], scrA.rearrange("(s k) e -> s (k e)", k=2), reads=[bscr["A"]], writes=[ba16])
        S.op("dve", lambda: V.tensor_copy(out=tb16[:, 0:512], in_=a16[:]), [ba16], [btb16])
        for k in range(4):
            S.op("pe", lambda: T.transpose(out=ptr[:, k * 16:(k + 1) * 16], in_=tb16[:, k * 128:(k + 1) * 128], identity=identb[0:16, 0:16]), [btb16, bidb], [bptr])
        S.op("act", lambda: A_.copy(out=aTs[:].rearrange("p k s -> p (k s)"), in_=ptr[:, 0:64]), [bptr], [baTs])
        S.op("act", lambda: A_.activation(out=tf16[:], in_=pS[:, CGM:CGM + 2048], func=AF.Sigmoid), [bpS], [btf16])
        wpa, bwpa = unit(WpaD, 4)
        wpb, bwpb = unit(WpbD, 4)
        mS = xs_m = sb("mS", [16, 1024]); bmS = Buf("mS")
        t1 = sb("t1s", [16, 512]); bt1 = Buf("t1s")
        for hf in range(2):
            a = nxt("pmm")
            for k in range(4):
                S.op("pe", lambda: T.matmul(pmm[a][0:16, :], lhsT=aTs[:, k, :], rhs=wpa[:, k, hf * 512:(hf + 1) * 512], start=(k == 0), stop=(k == 3)), [baTs, bwpa], [bpmm[a]])
            S.op("dve", lambda: V.tensor_tensor(out=t1[:], in0=pmm[a][0:16, :], in1=tf16[:, hf * 512:(hf + 1) * 512], op=ALU.mult), [bpmm[a], btf16], [bt1])
            a = nxt("pmm")
            for k in range(4):
                S.op("pe", lambda: T.matmul(pmm[a][0:16, :], lhsT=rTs[:, k, :], rhs=wpb[:, k, hf * 512:(hf + 1) * 512], start=(k == 0), stop=(k == 3)), [brTs, bwpb], [bpmm[a]])
            S.op("dve", lambda: V.tensor_tensor(out=mS[:, hf * 512:(hf + 1) * 512], in0=pmm[a][0:16, :], in1=tf16[:, 1024 + hf * 512:1024 + (hf + 1) * 512], op=ALU.mult), [bpmm[a], btf16], [bmS])
            S.op("dve", lambda: V.tensor_tensor(out=mS[:, hf * 512:(hf + 1) * 512], in0=mS[:, hf * 512:(hf + 1) * 512], in1=t1[:], op=ALU.add), [bmS, bt1], [bmS])
        S.op("dve", lambda: V.tensor_copy(out=tb16[:], in_=mS[:]), [bmS], [btb16])
        for k in range(8):
            S.op("pe", lambda: T.transpose(out=ptr[:, k * 16:(k + 1) * 16], in_=tb16[:, k * 128:(k + 1) * 128], identity=identb[0:16, 0:16]), [btb16, bidb], [bptr])
        S.op("act", lambda: A_.copy(out=mTs[:].rearrange("p k s -> p (k s)"), in_=ptr[:, 0:128]), [bptr], [bmTs])
        wo0, bwo0 = unit(WoutD[:, 0:4, :], 4)
        wo1, bwo1 = unit(WoutD[:, 4:8, :], 4)
        for hf in range(2):
            a = nxt("pmm")
            for k in range(8):
                wsrc = wo0 if k < 4 else wo1
                S.op("pe", lambda: T.matmul(pmm[a][0:16, :], lhsT=mTs[:, k, :], rhs=wsrc[:, k % 4, hf * 512:(hf + 1) * 512], start=(k == 0), stop=(k == 7)), [bmTs, bwo0, bwo1], [bpmm[a]])
            S.op("dve", lambda: V.tensor_tensor(out=xs[:, hf * 512:(hf + 1) * 512], in0=pmm[a][0:16, :], in1=xs[:, hf * 512:(hf + 1) * 512], op=ALU.add), [bpmm[a], bxs], [bxs])
        S.dma("pool", oys_d, xs[:], reads=[bxs])
        S.finish("sp")
        esS.close()
        S.finish("sp")
    S.close()
    return nc, S


_WEIGHT_NAMES = ["w_in", "g_norm", "g_q", "g_kc", "g_ks", "g_kw", "pe_k", "w1_k", "w2_k", "pe_v", "w1_v", "w2_v",
                 "conv_w", "conv_b", "w_ra", "b_ra", "w_rx", "b_rx", "lam", "w_pa", "w_pb", "w_out"]


def kernel(**inputs):
    inp = {k: np.asarray(v) for k, v in inputs.items()}
    x_prompt = inp["x_prompt"]
    shared = _shared_tables()
    nc, S = build_program()
    print('ninst', S.ninst, flush=True)
    in_maps = []
    poolc = np.ascontiguousarray(inp["cache_kv_cmp"], dtype=np.float32).reshape(81920, 4096)
    pools = np.ascontiguousarray(inp["cache_kv_sel"], dtype=np.float32).reshape(81920, 4096)
    pt = np.asarray(inp["page_table"]).astype(np.int32)
    As = np.zeros((512, 129), np.float32)
    for j in range(129):
        for cc in range(4 * j - 1, 4 * j + 4):
            if 0 <= cc <= 510:
                As[cc, j] = 1.0
    cmS = (np.arange(4)[None, :] * 128 + np.arange(128)[:, None] <= 510).astype(np.float32)
    fS = np.zeros((32, 129), np.float32)
    fS[:, [0, 127, 128]] = 1.0e4
    pm8 = (np.arange(128) % 8).astype(np.float32).reshape(128, 1)
    for c in range(8):
        b, p = c // 2, c % 2
        m = {}
        sl = slice(16 * c, 16 * c + 16)
        m["xs"] = np.ascontiguousarray(inp["x_sample"][sl, 0, :], dtype=np.float32)
        ptc = pt[sl]
        m["ptrep"] = np.ascontiguousarray(np.repeat(ptc.reshape(16, 4, 16), 8, axis=2).transpose(2, 0, 1).reshape(128, 64))
        m["pm8"] = pm8
        m["poolc"] = poolc
        m["pools"] = pools
        m["ckw"] = np.ascontiguousarray(inp["cache_kv_win"][sl], dtype=np.float32).reshape(16, 512, 256)
        m["sconv"] = np.ascontiguousarray(inp["state_conv"][sl], dtype=np.float32)
        m["sh"] = np.ascontiguousarray(inp["state_h"][sl], dtype=np.float32)
        m["As"] = As
        m["cmS"] = cmS
        m["fS"] = fS
        xb = np.ascontiguousarray(x_prompt[b])
        m["xb"] = xb
        m["xown"] = np.ascontiguousarray(xb.reshape(NT, 128, D_MODEL)[p::2].reshape(SEQ // 2, D_MODEL))
        for n in _WEIGHT_NAMES:
            m[n] = np.ascontiguousarray(inp[n], dtype=np.float32)
        m.update(shared)
        m.update(_core_tables(p))
        in_maps.append(m)
    res = run_bass_kernel_spmd(nc, in_maps, core_ids=list(range(8)))
    R = res.results
    B = 4
    y_prompt = np.zeros((B, SEQ, D_MODEL), np.float32)
    kv_cmp_p = np.zeros((B, SEQ, 2, 2, 64), np.float32)
    kv_sel_p = np.zeros((B, SEQ, 2, 2, 64), np.float32)
    kv_win_p = np.zeros((B, 512, 2, 2, 64), np.float32)
    conv_p = np.zeros((B, 3, 512), np.float32)
    h_p = np.zeros((B, 512), np.float32)
    for c in range(8):
        b, p = c // 2, c % 2
        y_prompt[b].reshape(NT, 128, D_MODEL)[p::2] = R[c]["oy"].reshape(NPAIR, 128, D_MODEL)
        if p == 0:
            kv_cmp_p[b] = R[c]["okvc"].reshape(SEQ, 2, 2, 64)
            kv_sel_p[b] = R[c]["okvs"].reshape(SEQ, 2, 2, 64)
            kv_win_p[b] = R[c]["okvw"].reshape(512, 2, 2, 64)
            conv_p[b] = R[c]["oconv"]
            h_p[b] = R[c]["oh"]
    DB = 128
    y_sample = np.concatenate([R[c]["oys"] for c in range(8)], 0).reshape(DB, 1, D_MODEL)
    kv_cmp_s = np.concatenate([R[c]["okvcs"] for c in range(8)], 0).reshape(DB, 1, 2, 2, 64)
    kv_sel_s = np.concatenate([R[c]["okvss"] for c in range(8)], 0).reshape(DB, 1, 2, 2, 64)
    kv_win_s = np.concatenate([R[c]["okvws"] for c in range(8)], 0).reshape(DB, 512, 2, 2, 64)
    conv_s = np.concatenate([R[c]["oconvs"] for c in range(8)], 0).reshape(DB, 3, 512)
    h_s = np.concatenate([R[c]["ohs"] for c in range(8)], 0).reshape(DB, 512)
    return (y_prompt, y_sample, kv_cmp_p, kv_cmp_s, kv_sel_p, kv_sel_s, kv_win_p, kv_win_s, conv_p, conv_s, h_p, h_s)
```

```python
import numpy as np
from contextlib import ExitStack
import concourse.bass as bass
import concourse.mybir as mybir
from concourse.bass_utils import run_bass_kernel_spmd

F32 = mybir.dt.float32
BF16 = mybir.dt.bfloat16
I32 = mybir.dt.int32
AF = mybir.ActivationFunctionType
ALU = mybir.AluOpType
AX = mybir.AxisListType

D_MODEL = 1024
SEQ = 4096
NT = 32
NPAIR = 16
EPS = 1e-6
SCALE = 0.125
CQ, CKV, CGN, CZN, CXR, CZR, CGM = 0, 512, 1280, 1304, 1816, 2328, 2840
D_IN = 4888
ROPE_THETA = 500000.0
PAST = 8192


class StopBuild(Exception):
    pass


def ck(name):
    import os
    if os.environ.get("STOP", "") == name:
        raise StopBuild(name)


class Buf:
    __slots__ = ("name", "w", "r")

    def __init__(self, name):
        self.name = name
        self.w = None
        self.r = []


class Sched:
    def __init__(self, nc, ndma=40):
        self.nc = nc
        self.eng = {"pe": nc.tensor, "act": nc.scalar, "dve": nc.vector,
                    "pool": nc.gpsimd, "sp": nc.sync}
        self.sem, self.cnt, self._cms = {}, {}, []
        for k in self.eng:
            cm = nc.semaphore("s_" + k)
            self._cms.append(cm)
            self.sem[k] = cm.__enter__()
            self.cnt[k] = 0
        self.dsem, self.dcnt = [], []
        for i in range(ndma):
            cm = nc.semaphore("d%d" % i)
            self._cms.append(cm)
            self.dsem.append(cm.__enter__())
            self.dcnt.append(0)
        self.dnext = 0
        self.seen = {k: {} for k in self.eng}
        self.pend = {}
        self.ninst = 0

    def close(self):
        for cm in reversed(self._cms):
            cm.__exit__(None, None, None)

    def _wait(self, e, ev):
        if ev is None:
            return
        key, val = ev
        if key == "pe" and e == "pe":
            return
        if isinstance(key, str) and val == self.cnt[key] + 1:
            self._materialize(key)
        if self.seen[e].get(key, 0) >= val:
            return
        sem = self.sem[key] if isinstance(key, str) else self.dsem[key]
        self.eng[e].wait_ge(sem, val)
        self.seen[e][key] = val

    def _deps(self, e, reads, writes):
        for b in reads:
            self._wait(e, b.w)
        for b in writes:
            self._wait(e, b.w)
            for ev in b.r:
                self._wait(e, ev)

    def _commit(self, ev, reads, writes):
        for b in reads:
            b.r.append(ev)
            if len(b.r) > 8:
                best = {}
                for k, v in b.r:
                    if best.get(k, 0) < v:
                        best[k] = v
                b.r = list(best.items())
        for b in writes:
            b.w = ev
            b.r = []

    def _materialize(self, key):
        ins = self.pend.get(key)
        if ins is not None:
            self.cnt[key] += 1
            ins.then_inc(self.sem[key], 1)
            self.pend[key] = None

    def op(self, e, fn, reads=(), writes=(), inc=True):
        self._deps(e, reads, writes)
        ins = fn()
        self.pend[e] = ins
        self._commit((e, self.cnt[e] + 1), reads, writes)
        self.ninst += 1
        return ins

    def dma(self, q, out, in_, reads=(), writes=(), fn=None):
        self._deps(q, reads, writes)
        slot = self.dnext
        self.dnext = (self.dnext + 1) % len(self.dsem)
        if self.dcnt[slot] > 0:
            self._wait(q, (slot, self.dcnt[slot]))
        ins = self.eng[q].dma_start(out=out, in_=in_) if fn is None else fn()
        self.dcnt[slot] += 16
        ins.then_inc(self.dsem[slot], 16)
        ev = (slot, self.dcnt[slot])
        self._commit(ev, reads, writes)
        self.ninst += 1
        return ev

    def _all_events(self):
        for k in self.eng:
            self._materialize(k)
        evs = [(k, self.cnt[k]) for k in self.eng if self.cnt[k] > 0]
        evs += [(i, self.dcnt[i]) for i in range(len(self.dsem)) if self.dcnt[i] > 0]
        return evs

    def barrier(self):
        evs = self._all_events()
        for e in self.eng:
            for ev in evs:
                self._wait(e, ev)

    def finish(self, e="sp"):
        for ev in self._all_events():
            self._wait(e, ev)


def _rope_tab(pos):
    half = 8
    inv = ROPE_THETA ** (-np.arange(half, dtype=np.float32) / half)
    ang = np.asarray(pos, np.float32)[:, None] * inv[None, :].astype(np.float32)
    return np.concatenate([np.cos(ang), np.sin(ang)], axis=1).astype(np.float32)


def _core_tables(p):
    t = {}
    own_tiles = [2 * i + p for i in range(NPAIR)]
    pos_own = np.concatenate([np.arange(g * 128, g * 128 + 128) for g in own_tiles])
    t["ropeO"] = _rope_tab(pos_own)
    c = np.arange(256)
    cend = 16 * c + 31
    cm = np.zeros((NPAIR, 2, 128, 128), np.float32)
    fb = np.zeros((NPAIR, 128, 64), np.float32)
    for i, g in enumerate(own_tiles):
        pos = g * 128 + np.arange(128)
        m = (cend[:, None] <= pos[None, :]) & (c[:, None] < 255)
        cm[i] = m.reshape(2, 128, 128)
        cur = pos // 64
        blk = np.arange(64)
        f = (blk[None, :] == 0) | (blk[None, :] == cur[:, None]) | (blk[None, :] == cur[:, None] - 1)
        fb[i] = np.where(f, 1.0e4, 0.0)
    t["cmaskT"] = cm
    t["forcedB"] = fb
    r = np.arange(128)
    lower = (r[:, None] <= r[None, :]).astype(np.float32)
    upper = (r[:, None] >= r[None, :]).astype(np.float32)
    ones = np.ones((128, 128), np.float32)
    zeros = np.zeros((128, 128), np.float32)
    if p == 0:
        t["dmask"] = np.stack([lower, zeros])
        t["wmask"] = np.stack([upper, ones, ones, ones, lower, zeros])
    else:
        t["dmask"] = np.stack([ones, lower])
        t["wmask"] = np.stack([zeros, upper, ones, ones, ones, lower])
    sc = np.zeros((128, 2), np.float32)
    sc[:, p] = 1.0
    t["selcol"] = sc
    return t


def _shared_tables():
    t = {}
    t["ident"] = np.eye(128, dtype=np.float32)
    t["ropeA"] = _rope_tab(np.arange(SEQ))
    cpos = 16 * (np.arange(513) - 1) + 31
    t["ropeC"] = _rope_tab(cpos)
    t["ropeS"] = np.repeat(_rope_tab(np.array([PAST])), 16, axis=0)
    E = np.zeros((64, 32, 128), np.float32)
    for j in range(32):
        E[2 * j, j, 0:64] = 1.0
        E[2 * j + 1, j, 64:128] = 1.0
    t["Eall"] = E
    A = np.zeros((256, 64), np.float32)
    for j in range(64):
        for c in range(4 * j - 1, 4 * j + 4):
            if 0 <= c < 255:
                A[c, j] = 1.0
    t["Aimp"] = A
    return t


def build_program(npair=NPAIR, ns_seq=16, pool_rows=81920):
    nc = bass.Bass("TRN2", target_bir_lowering=False)

    def din(name, shape, dt=F32):
        return nc.dram_tensor(name, list(shape), dt, kind="ExternalInput").ap()

    def dout(name, shape, dt=F32):
        return nc.dram_tensor(name, list(shape), dt, kind="ExternalOutput").ap()

    xb_d = din("xb", [SEQ, D_MODEL])
    xown_d = din("xown", [SEQ // 2, D_MODEL])
    w_in_d = din("w_in", [D_MODEL, D_IN])
    g_norm_d = din("g_norm", [D_MODEL])
    g_q_d, g_kc_d, g_ks_d, g_kw_d = (din(n, [64]) for n in ("g_q", "g_kc", "g_ks", "g_kw"))
    pe_k_d, pe_v_d = din("pe_k", [32, 64]), din("pe_v", [32, 64])
    w1_k_d, w1_v_d = din("w1_k", [32, 64, 64]), din("w1_v", [32, 64, 64])
    w2_k_d, w2_v_d = din("w2_k", [64, 64]), din("w2_v", [64, 64])
    conv_w_d, conv_b_d = din("conv_w", [4, 512]), din("conv_b", [512])
    w_ra_d, b_ra_d = din("w_ra", [8, 64, 64]), din("b_ra", [8, 64])
    w_rx_d, b_rx_d = din("w_rx", [8, 64, 64]), din("b_rx", [8, 64])
    lam_d = din("lam", [512])
    w_pa_d, w_pb_d, w_out_d = din("w_pa", [512, 1024]), din("w_pb", [512, 1024]), din("w_out", [1024, 1024])
    ident_d = din("ident", [128, 128])
    ropeA_d, ropeO_d, ropeC_d, ropeS_d = din("ropeA", [SEQ, 16]), din("ropeO", [SEQ // 2, 16]), din("ropeC", [513, 16]), din("ropeS", [16, 16])
    cmaskT_d = din("cmaskT", [NPAIR, 2, 128, 128])
    forcedB_d = din("forcedB", [NPAIR, 128, 64])
    dmask_d = din("dmask", [2, 128, 128])
    wmask_d = din("wmask", [6, 128, 128])
    selcol_d = din("selcol", [128, 2])
    Eall_d = din("Eall", [64, 32, 128])
    Aimp_d = din("Aimp", [256, 64])
    xs_d = din("xs", [16, D_MODEL])
    ptrep_d = din("ptrep", [128, 64], I32)
    pm8_d = din("pm8", [128, 1])
    poolc_d = din("poolc", [pool_rows, 4096])
    pools_d = din("pools", [pool_rows, 4096])
    ckw_d = din("ckw", [16, 512, 256])
    sconv_d = din("sconv", [16, 3, 512])
    sh_d = din("sh", [16, 512])
    As_d = din("As", [512, 129])
    cmS_d = din("cmS", [128, 4])
    fS_d = din("fS", [32, 129])
    oys_d = dout("oys", [16, D_MODEL])
    okvcs_d = dout("okvcs", [16, 256])
    okvss_d = dout("okvss", [16, 256])
    okvws_d = dout("okvws", [16, 512, 256])
    oconvs_d = dout("oconvs", [16, 3, 512])
    ohs_d = dout("ohs", [16, 512])
    oy_d = dout("oy", [SEQ // 2, D_MODEL])
    okvc_d = dout("okvc", [SEQ, 256])
    okvs_d = dout("okvs", [SEQ, 256])
    okvw_d = dout("okvw", [512, 256])
    oconv_d = dout("oconv", [3, 512])
    oh_d = dout("oh", [512])

    WbD = nc.dram_tensor("WbD", [128, 8, D_IN], BF16, kind="Internal").ap(); bWbD = Buf("WbD")
    WpaD = nc.dram_tensor("WpaD", [128, 4, 1024], BF16, kind="Internal").ap()
    WpbD = nc.dram_tensor("WpbD", [128, 4, 1024], BF16, kind="Internal").ap()
    WoutD = nc.dram_tensor("WoutD", [128, 8, 1024], BF16, kind="Internal").ap()
    scrOc = nc.dram_tensor("scrOc", [4, 32, 194], F32, kind="Internal").ap()
    scrOs = nc.dram_tensor("scrOs", [4, 32, 65], F32, kind="Internal").ap()
    scrOw = nc.dram_tensor("scrOw", [4, 32, 65], F32, kind="Internal").ap()
    scrSel = nc.dram_tensor("scrSel", [32, 512], F32, kind="Internal").ap()
    scrG = nc.dram_tensor("scrG", [16, 24], F32, kind="Internal").ap()
    scrZ = nc.dram_tensor("scrZ", [16, 512], F32, kind="Internal").ap()
    scrA = nc.dram_tensor("scrA", [32, 256], F32, kind="Internal").ap()

    S = Sched(nc)
    V, G, A_, T = nc.vector, nc.gpsimd, nc.scalar, nc.tensor
    import os as _os
    DBG = int(_os.environ.get("DBG_PAIR", "-1"))
    dbg_out = {}
    if DBG >= 0:
        for nm, shp in (("d_obr", [3, 128, 512]), ("d_acc", [128, 512]), ("d_hs", [128, 512]), ("d_sel", [2, 128, 64]), ("d_imp", [2, 128, 64])):
            dbg_out[nm] = dout(nm, shp)

    with ExitStack() as es:
        cur_es = [es]

        def sb(name, shape, dt=F32):
            return cur_es[0].enter_context(nc.sbuf_tensor("s_" + name, list(shape), dt))

        def ps(name, shape, dt=F32):
            return es.enter_context(nc.psum_tensor("p_" + name, list(shape), dt))

        wu = [sb("wu%d" % i, [128, 4096], BF16) for i in range(5)]; bwu = [Buf("wu%d" % i) for i in range(5)]
        identf = sb("identf", [128, 128], F32); bidf = Buf("identf")
        identb = sb("identb", [128, 128], BF16); bidb = Buf("identb")
        gnc = sb("gnc", [128, 8], F32); bgnc = Buf("gnc")
        gqb = sb("gqb", [128, 64], F32); gkcb = sb("gkcb", [128, 64], F32)
        gksb = sb("gksb", [128, 64], F32); gkwb = sb("gkwb", [128, 64], F32)
        bgq, bgkc, bgks, bgkw = Buf("gq"), Buf("gkc"), Buf("gks"), Buf("gkw")
        cwT = sb("cwT", [128, 4, 4], F32); bcw = Buf("cwT")
        cbT = sb("cbT", [128, 4], F32); bcb = Buf("cbT")
        braT = sb("braT", [128, 4], F32); brxT = sb("brxT", [128, 4], F32); bbr = Buf("brT")
        clT = sb("clT", [128, 4], F32); cl2T = sb("cl2T", [128, 4], F32); bcl = Buf("clT")
        WraB = sb("WraB", [128, 4, 128], BF16); WrxB = sb("WrxB", [128, 4, 128], BF16); bWr = Buf("WrB")
        W2k = sb("W2k", [64, 64], BF16); W2v = sb("W2v", [64, 64], BF16); bW2 = Buf("W2")
        pbk = sb("pbk", [64, 1], F32); pbv = sb("pbv", [64, 1], F32); bpb = Buf("pb")
        nr_sq = sb("nr_sq", [128, 512]); nr_ss = sb("nr_ss", [128, 8]); nr_t = sb("nr_t", [128, 4, 8, 8]); bnr = Buf("nr")
        esP = ExitStack()
        cur_es[0] = esP
        WbA = sb("WbA", [128, 8, 1280], BF16); bWbA = Buf("WbA")
        Wgn = sb("Wgn", [128, 8, 24], BF16); bWgn = Buf("Wgn")
        KST = sb("KST", [64, 2, SEQ], BF16); bKST = [Buf("KST%d" % t) for t in range(NT)]
        VSa = sb("VSa", [128, NT, 2, 65], BF16); bVSa = [Buf("VSa%d" % t) for t in range(NT)]
        KWT = sb("KWT", [64, 2, 8 * 128], BF16); bKWT = [Buf("KWT%d" % t) for t in range(8)]
        VWa = sb("VWa", [128, 8, 2, 65], BF16); bVWa = [Buf("VWa%d" % t) for t in range(8)]
        KCT = sb("KCT", [64, 2, 256], BF16); bKCT = Buf("KCT")
        VCT = sb("VCT", [64, 2, 256], BF16); bVCT = Buf("VCT")
        VCA = sb("VCA", [128, 2, 2, 129], BF16); bVCA = Buf("VCA")
        Eall = sb("Eall", [64, 32, 128], BF16); bEall = Buf("Eall")
        XTp = sb("XTp", [64, 4, 272], BF16); bXTp = Buf("XTp")
        xr = sb("xr", [128, 4, 259], F32); bxr = Buf("xr")
        hprev = sb("hprev", [128, 4], F32); bhprev = Buf("hprev")
        W1k = sb("W1k", [64, 32, 64], BF16); W1v = sb("W1v", [64, 32, 64], BF16); bW1 = Buf("W1")
        selcol = sb("selcol", [128, 2], F32); bselcol = Buf("selcol")
        dmask = sb("dmask", [128, 2, 128], F32); bdmask = Buf("dmask")
        wmask = sb("wmask", [128, 6, 128], F32); bwmask = Buf("wmask")
        ropeCt = sb("ropeCt", [16, NPAIR, 16], F32); bropeC = Buf("ropeCt")

        pmm = [ps("pmm0", [128, 512]), ps("pmm1", [128, 512])]; bpmm = [Buf("pmm0"), Buf("pmm1")]
        ptr = ps("ptr", [128, 1024], BF16); bptr = Buf("ptr")
        pst = [ps("pst0", [128, 512]), ps("pst1", [128, 512])]; bpst = [Buf("pst0"), Buf("pst1")]
        pmk = ps("pmk", [128, 2, 256]); _bp = Buf("pmk"); bpmk = [_bp, _bp]
        pO = ps("pO", [128, 4, 256]); bpO = Buf("pO")
        st = {"pmm": 0, "pst": 0, "pmk": 0}

        def nxt(kind):
            st[kind] ^= 1
            return st[kind]

        def bc_mid(ap, P, n, w):
            return ap.unsqueeze(1).to_broadcast([P, n, w])

        def bc_last(ap, P, n, w):
            return ap.unsqueeze(2).to_broadcast([P, n, w])

        with ExitStack() as es2:
            stg = [es2.enter_context(nc.sbuf_tensor("stg%d" % i, [128, 2444], F32)) for i in range(2)]
            bstg = [Buf("stg0"), Buf("stg1")]
            tmpc = es2.enter_context(nc.sbuf_tensor("tmpc", [128, 16, 64], F32)); btmpc = Buf("tmpc")
            S.dma("sp", identf[:], ident_d, writes=[bidf])
            S.op("act", lambda: A_.copy(out=identb[:], in_=identf[:]), [bidf], [bidb])
            S.dma("sp", None, None, writes=[bgnc], fn=lambda: nc.sync.dma_start(
                out=gnc[:], in_=g_norm_d.rearrange("(k p) -> p k", p=128), allow_slow_non_contiguous=True))
            n = 0
            engs = ["dve", "pool"]
            stgb = [es2.enter_context(nc.sbuf_tensor("stgb%d" % i, [128, 2444], BF16)) for i in range(2)]
            bstgb = [Buf("stgb0"), Buf("stgb1")]
            for k in range(8):
                for hf in range(2):
                    sl = n % 2
                    S.dma("sp", stg[sl][:], w_in_d[k * 128:(k + 1) * 128, hf * 2444:(hf + 1) * 2444], writes=[bstg[sl]])
                    e = engs[n % 2]
                    E_ = V if e == "dve" else G
                    S.op(e, lambda: E_.tensor_scalar(out=stgb[sl][:], in0=stg[sl][:], scalar1=gnc[:, k:k + 1], scalar2=None,
                                                     op0=ALU.mult), [bstg[sl], bgnc], [bstgb[sl]])
                    S.dma("sp", WbD[:, k, hf * 2444:(hf + 1) * 2444], stgb[sl][:], reads=[bstgb[sl]], writes=[bWbD])
                    n += 1
            for (wd, wD, nk) in ((w_pa_d, WpaD, 4), (w_pb_d, WpbD, 4), (w_out_d, WoutD, 8)):
                for k in range(nk):
                    sl = n % 2
                    S.dma("sp", stg[sl][:, 0:1024], wd[k * 128:(k + 1) * 128, :], writes=[bstg[sl]])
                    e = engs[n % 2]
                    E_ = V if e == "dve" else G
                    S.op(e, lambda: E_.tensor_copy(out=stgb[sl][:, 0:1024], in_=stg[sl][:, 0:1024]), [bstg[sl]], [bstgb[sl]])
                    S.dma("sp", wD[:, k, :], stgb[sl][:, 0:1024], reads=[bstgb[sl]], writes=[bWbD])
                    n += 1
            S.dma("sp", WbA[:, :, 0:768], WbD[:, :, CKV:CKV + 768], reads=[bWbD], writes=[bWbA])
            S.dma("sp", WbA[:, :, 768:1280], WbD[:, :, CXR:CXR + 512], reads=[bWbD], writes=[bWbA])
            S.dma("sp", None, None, reads=[bWbD], writes=[bWgn], fn=lambda: nc.sync.dma_start(out=Wgn[:], in_=WbD[:, :, CGN:CGN + 24], allow_slow_non_contiguous=True))
            for hf in range(2):
                sl = n % 2
                S.dma("sp", stg[sl][0:64, 0:2048], Eall_d[:, hf * 16:(hf + 1) * 16, :].rearrange("b j k -> b (j k)"), writes=[bstg[sl]])
                S.op("dve", lambda sl=sl, hf=hf: V.tensor_copy(
                    out=Eall[:, hf * 16:(hf + 1) * 16, :].rearrange("b j k -> b (j k)"), in_=stg[sl][0:64, 0:2048]), [bstg[sl]], [bEall])
                n += 1
            for (gd, gt_, bg) in ((g_q_d, gqb, bgq), (g_kc_d, gkcb, bgkc), (g_ks_d, gksb, bgks), (g_kw_d, gkwb, bgkw)):
                S.dma("sp", None, None, writes=[bg], fn=lambda gd=gd, gt_=gt_: nc.sync.dma_start(out=gt_[:], in_=gd.partition_broadcast(128)))
            for tap in range(4):
                S.dma("sp", None, None, writes=[bcw], fn=lambda: nc.sync.dma_start(
                    out=cwT[:, :, tap], in_=conv_w_d[tap].rearrange("(g c) -> c g", c=128), allow_slow_non_contiguous=True))
            S.dma("sp", None, None, writes=[bcb], fn=lambda: nc.sync.dma_start(
                out=cbT[:], in_=conv_b_d.rearrange("(g c) -> c g", c=128), allow_slow_non_contiguous=True))
            S.dma("sp", None, None, writes=[bbr], fn=lambda: nc.sync.dma_start(
                out=braT[:], in_=b_ra_d.rearrange("(g a) c -> (a c) g", a=2), allow_slow_non_contiguous=True))
            S.dma("sp", None, None, writes=[bbr], fn=lambda: nc.sync.dma_start(
                out=brxT[:], in_=b_rx_d.rearrange("(g a) c -> (a c) g", a=2), allow_slow_non_contiguous=True))
            S.dma("sp", None, None, writes=[bcl], fn=lambda: nc.sync.dma_start(
                out=clT[:], in_=lam_d.rearrange("(g c) -> c g", c=128), allow_slow_non_contiguous=True))
            S.op("act", lambda: A_.activation(out=clT[:], in_=clT[:], func=AF.Exp, scale=-1.0), [bcl], [bcl])
            S.op("act", lambda: A_.activation(out=clT[:], in_=clT[:], func=AF.Ln, bias=1.0), [bcl], [bcl])
            S.op("dve", lambda: V.tensor_scalar(out=cl2T[:], in0=clT[:], scalar1=-16.0, scalar2=None, op0=ALU.mult), [bcl], [bcl])
            S.op("dve", lambda: V.tensor_scalar(out=clT[:], in0=clT[:], scalar1=-8.0, scalar2=None, op0=ALU.mult), [bcl], [bcl])
            for (wd, wsb) in ((w_ra_d, WraB), (w_rx_d, WrxB)):
                S.op("pool", lambda: G.memset(stg[0][:, 0:512], 0.0), [], [bstg[0]])
                for g in range(4):
                    for a in range(2):
                        S.dma("sp", stg[0][a * 64:(a + 1) * 64, g * 128 + a * 64: g * 128 + a * 64 + 64], wd[2 * g + a], writes=[bstg[0]])
                S.op("dve", lambda wsb=wsb: V.tensor_copy(out=wsb[:].rearrange("p g c -> p (g c)"), in_=stg[0][:, 0:512]), [bstg[0]], [bWr])
            for (wd, wsb) in ((w1_k_d, W1k), (w1_v_d, W1v)):
                S.dma("sp", None, None, writes=[bstg[1]], fn=lambda wd=wd: nc.sync.dma_start(
                    out=stg[1][0:64, 0:2048].rearrange("d (j e) -> d j e", j=32), in_=wd.rearrange("j d e -> d j e")))
                S.op("dve", lambda wsb=wsb: V.tensor_copy(out=wsb[:].rearrange("d j e -> d (j e)"), in_=stg[1][0:64, 0:2048]), [bstg[1]], [bW1])
            for (wd, wsb) in ((w2_k_d, W2k), (w2_v_d, W2v)):
                S.dma("sp", stg[1][0:64, 0:64], wd, writes=[bstg[1]])
                S.op("dve", lambda wsb=wsb: V.tensor_copy(out=wsb[:], in_=stg[1][0:64, 0:64]), [bstg[1]], [bW2])
            peT = es2.enter_context(nc.sbuf_tensor("peT", [64, 2, 32], F32)); bpe = Buf("peT")
            peTb = es2.enter_context(nc.sbuf_tensor("peTb", [64, 2, 32], BF16))
            S.dma("sp", None, None, writes=[bpe], fn=lambda: nc.sync.dma_start(out=peT[:, 0, :], in_=pe_k_d.rearrange("j d -> d j"), allow_slow_non_contiguous=True))
            S.dma("sp", None, None, writes=[bpe], fn=lambda: nc.sync.dma_start(out=peT[:, 1, :], in_=pe_v_d.rearrange("j d -> d j"), allow_slow_non_contiguous=True))
            S.op("dve", lambda: V.tensor_copy(out=peTb[:], in_=peT[:]), [bpe], [bpe])
            for kind, (w1s, pbt) in enumerate(((W1k, pbk), (W1v, pbv))):
                for j in range(32):
                    S.op("pe", lambda kind=kind, w1s=w1s, j=j: T.matmul(pmm[0][0:64, kind:kind + 1], lhsT=w1s[:, j, :], rhs=peTb[:, kind, j:j + 1],
                                                                       start=(j == 0), stop=(j == 31)), [bW1, bpe], [bpmm[0]])
                S.op("dve", lambda kind=kind, pbt=pbt: V.tensor_copy(out=pbt[:], in_=pmm[0][0:64, kind:kind + 1]), [bpmm[0]], [bpb])
            S.dma("sp", selcol[:], selcol_d, writes=[bselcol])
            S.dma("sp", dmask[:], dmask_d.rearrange("m k q -> k m q"), writes=[bdmask])
            S.dma("sp", wmask[:], wmask_d.rearrange("m k q -> k m q"), writes=[bwmask])
            S.dma("sp", None, None, writes=[bropeC], fn=lambda: nc.sync.dma_start(
                out=ropeCt[:], in_=ropeC_d[0:256, :].rearrange("(i m) e -> m i e", m=16)))
            S.op("pool", lambda: G.memset(VCA[:], 1.0), [], [bVCA])
            for ch in range(2):
                S.dma("sp", stg[0][:, 0:64], Aimp_d[ch * 128:(ch + 1) * 128, :], writes=[bstg[0]])
                for hk in range(2):
                    S.op("dve", lambda ch=ch, hk=hk: V.tensor_copy(out=VCA[:, ch, hk, 65:129], in_=stg[0][:, 0:64]), [bstg[0]], [bVCA])
            S.op("pool", lambda: G.memset(VSa[:], 1.0), [], bVSa)
            S.op("pool", lambda: G.memset(VWa[:], 1.0), [], bVWa)
            S.op("pool", lambda: G.memset(KCT[:], 0.0), [], [bKCT])
            S.op("pool", lambda: G.memset(VCT[:], 0.0), [], [bVCT])
            S.op("pool", lambda: G.memset(XTp[:], 0.0), [], [bXTp])
            S.op("pool", lambda: G.memset(xr[:], 0.0), [], [bxr])
            S.op("pool", lambda: G.memset(hprev[:], 0.0), [], [bhprev])
            S.barrier()

        xt = [sb("xt%d" % i, [128, 1024]) for i in range(2)]; bxt = [Buf("xt%d" % i) for i in range(2)]
        xo = [sb("xo%d" % i, [128, 1024]) for i in range(2)]; bxo = [Buf("xo%d" % i) for i in range(2)]
        rpA = [sb("rpA%d" % i, [128, 16]) for i in range(2)]; brpA = [Buf("rpA%d" % i) for i in range(2)]
        rpO = [sb("rpO%d" % i, [128, 16]) for i in range(2)]; brpO = [Buf("rpO%d" % i) for i in range(2)]
        cmk = [sb("cmk%d" % i, [128, 2, 128]) for i in range(2)]; bcmk = [Buf("cmk%d" % i) for i in range(2)]
        fbt = [sb("fbt%d" % i, [128, 64]) for i in range(2)]; bfbt = [Buf("fbt%d" % i) for i in range(2)]
        ss = sb("ss", [128, 2]); bss = Buf("ss")
        xn = sb("xn", [128, 1024], BF16); bxn = Buf("xn")
        uT = sb("uT", [128, 8, 256], BF16); buT = Buf("uT")
        uTo = sb("uTo", [128, 8, 128], BF16); buTo = Buf("uTo")
        uTt = xn[:].rearrange("p (k t) -> p k t", k=8); buTt = bxn
        kvr = [sb("kvr%d" % i, [128, 768]) for i in range(2)]; bkvr = [Buf("kvr%d" % i) for i in range(2)]
        kvb = sb("kvb", [128, 768], BF16); bkvb = Buf("kvb")
        xc = sb("xc", [128, 256]); bxc = Buf("xc")
        xcb = sb("xcb", [128, 256], BF16); bxcb = Buf("xcb")
        gr = sb("gr", [128, 256]); gi = sb("gi", [128, 256]); ga = sb("ga", [128, 256]); ga2 = sb("ga2", [128, 256]); bg_ = Buf("gates")
        hsf = sb("hsf", [128, 256]); bhsf = Buf("hsf")
        hsb = [sb("hsb%d" % q, [128, 4, 256], BF16) for q in range(2)]; bhsb = [Buf("hsb0"), Buf("hsb1")]
        hidk = sb("hidk", [64, 32], BF16); hidv = sb("hidv", [64, 32], BF16); bhid = Buf("hid")
        kcn = sb("kcn", [16, 2, 64]); kcnb = sb("kcnb", [16, 2, 64], BF16); bkcn = Buf("kcn")
        qf = sb("qf", [128, 512]); bqf = Buf("qf")
        qb = sb("qb", [128, 512], BF16); bqb = Buf("qb")
        QT = sb("QT", [64, 8, 128], BF16); bQT = Buf("QT")
        gns = sb("gns", [128, 24]); bgns = Buf("gns")
        zs = sb("zs", [128, 512]); bzs = Buf("zs")
        zr = sb("zr", [128, 4, 128]); bzr = Buf("zr")
        gm = sb("gm", [128, 2, 512]); bgm = Buf("gm")
        Eb = [sb("Eb%d" % i, [128, 4, 128], BF16) for i in range(2)]; bEb = [Buf("Eb%d" % i) for i in range(2)]
        Pt = [sb("Pt%d" % i, [128, 4, 128], BF16) for i in range(2)]; bPt = [Buf("Pt%d" % i) for i in range(2)]
        mk2 = sb("mk2", [128, 128]); bmk2 = Buf("mk2")
        rden = sb("rden", [128, 4]); coef = sb("coef", [128, 4]); brd = Buf("rden")
        acc = sb("acc", [128, 8, 64]); bacc = Buf("acc")
        tmpo = sb("tmpo", [128, 4, 64]); btmpo = Buf("tmpo")
        impq = sb("impq", [128, 64]); impw = sb("impw", [128, 64]); m8 = sb("m8", [128, 16]); bimp = Buf("imp")
        selb = sb("selb", [128, 64], BF16); bselb = Buf("selb")
        selT = sb("selT", [64, 128], BF16); bselT = Buf("selT")
        ab = sb("ab", [128, 512], BF16); bab = Buf("ab")
        aT = sb("aT", [128, 4, 128], BF16); baT = Buf("aT")
        rt1 = sb("rt1", [128, 4, 128]); brt1 = Buf("rt1")
        rT = sb("rT", [128, 4, 128], BF16); brT = Buf("rT")
        mt1 = sb("mt1", [128, 512]); mt2 = sb("mt2", [128, 512]); bmt = Buf("mt")
        mT = sb("mT", [128, 8, 128], BF16); bmT = Buf("mT")

        ust = {"n": 0}
        cur = {"i": -1}
        dbgt = tmpo; bdbgt = btmpo

        def unit(src_ap, nk):
            sl = ust["n"] % 5
            ust["n"] += 1
            W = 4096 // nk
            view = wu[sl][:].rearrange("p (k c) -> p k c", k=nk)
            S.dma("sp", view, src_ap, reads=[bWbD], writes=[bwu[sl]])
            return view, bwu[sl]

        def norm_rope(x3, P, nh, gb, bg, rope, brope, bx):
            sq = nr_sq[0:P, 0:nh * 64].rearrange("p (h d) -> p h d", h=nh)
            S.op("dve", lambda: V.tensor_tensor(out=sq, in0=x3, in1=x3, op=ALU.mult), [bx], [bnr])
            S.op("dve", lambda: V.tensor_reduce(out=nr_ss[0:P, 0:nh], in_=sq, axis=AX.X, op=ALU.add), [bnr], [bnr])
            S.op("act", lambda: A_.activation(out=nr_ss[0:P, 0:nh], in_=nr_ss[0:P, 0:nh], func=AF.Sqrt, scale=1.0 / 64, bias=EPS), [bnr], [bnr])
            S.op("dve", lambda: V.reciprocal(out=nr_ss[0:P, 0:nh], in_=nr_ss[0:P, 0:nh]), [bnr], [bnr])
            S.op("dve", lambda: V.tensor_tensor(out=x3, in0=x3, in1=bc_last(nr_ss[0:P, 0:nh], P, nh, 64), op=ALU.mult), [bx, bnr], [bx])
            S.op("dve", lambda: V.tensor_tensor(out=x3, in0=x3, in1=bc_mid(gb[0:P, :], P, nh, 64), op=ALU.mult), [bx, bg], [bx])
            x1, x2 = x3[:, :, 0:8], x3[:, :, 8:16]
            cosb, sinb = bc_mid(rope[0:P, 0:8], P, nh, 8), bc_mid(rope[0:P, 8:16], P, nh, 8)
            ta, tb, tc, td = (nr_t[0:P, i, 0:nh, :] for i in range(4))
            S.op("dve", lambda: V.tensor_tensor(out=ta, in0=x1, in1=cosb, op=ALU.mult), [bx, brope], [bnr])
            S.op("dve", lambda: V.tensor_tensor(out=tb, in0=x2, in1=sinb, op=ALU.mult), [bx, brope], [bnr])
            S.op("dve", lambda: V.tensor_tensor(out=tc, in0=x2, in1=cosb, op=ALU.mult), [bx, brope], [bnr])
            S.op("dve", lambda: V.tensor_tensor(out=td, in0=x1, in1=sinb, op=ALU.mult), [bx, brope], [bnr])
            S.op("dve", lambda: V.tensor_tensor(out=x1, in0=ta, in1=tb, op=ALU.subtract), [bnr], [bx])
            S.op("dve", lambda: V.tensor_tensor(out=x2, in0=tc, in1=td, op=ALU.add), [bnr], [bx])

        def load_x(i):
            for tt_ in range(2):
                t = 2 * i + tt_
                S.dma("sp", xt[tt_][:], xb_d[t * 128:(t + 1) * 128, :], writes=[bxt[tt_]])
                S.dma("sp", rpA[tt_][:], ropeA_d[t * 128:(t + 1) * 128, :], writes=[brpA[tt_]])

        def load_own(i):
            sl = i % 2
            S.dma("sp", xo[sl][:], xown_d[i * 128:(i + 1) * 128, :], writes=[bxo[sl]])
            S.dma("sp", rpO[sl][:], ropeO_d[i * 128:(i + 1) * 128, :], writes=[brpO[sl]])
            S.dma("sp", cmk[sl][:], cmaskT_d[i].rearrange("c k q -> k c q"), writes=[bcmk[sl]])
            S.dma("sp", fbt[sl][:], forcedB_d[i], writes=[bfbt[sl]])

        def attn_branch(hk, tiles, ncol, step):
            n = len(tiles)
            qrhs = QT[:, 4 * hk:4 * hk + 4, :].rearrange("d h q -> d (h q)")
            masks = [None] * n

            def front(x):
                tl = tiles[x]
                a = x % 2
                S.op("pe", lambda: T.matmul(pst[a][:], lhsT=tl["KT"], rhs=qrhs, start=True, stop=True), [tl["bK"], bQT], [bpst[a]])
                S.op("act", lambda: A_.activation(out=Eb[a][:].rearrange("k h q -> k (h q)"), in_=pst[a][:], func=AF.Exp, scale=SCALE),
                     [bpst[a]], [bEb[a]])

            def mask(x):
                tl = tiles[x]
                if tl["kind"] != "sel":
                    masks[x] = (tl["mk"], tl["bm"])
                    return
                m = x % 2
                S.op("pe", lambda: T.matmul(pmk[:, m, 0:128], lhsT=Eall[:, tl["j"], :], rhs=selT[:], start=True, stop=True),
                     [bEall, bselT], [bpmk[m]])
                if tl["dm"] is None:
                    masks[x] = (pmk[:, m, 0:128], bpmk[m])
                else:
                    S.op("dve", lambda: V.tensor_tensor(out=mk2[:], in0=pmk[:, m, 0:128], in1=dmask[:, tl["dm"], :], op=ALU.mult),
                         [bpmk[m], bdmask], [bmk2])
                    masks[x] = (mk2[:], bmk2)

            def back(x):
                tl = tiles[x]
                a = x % 2
                mk_ap, bm = masks[x]
                S.op("dve", lambda: V.tensor_tensor(out=Pt[a][:], in0=Eb[a][:], in1=bc_mid(mk_ap, 128, 4, 128), op=ALU.mult),
                     [bEb[a], bm], [bPt[a]])
                for h in range(4):
                    S.op("pe", lambda: T.matmul(pO[:, h, 0:ncol], lhsT=Pt[a][:, h, :], rhs=tl["V"], start=(x == 0 and h % 2 == 0), stop=(x == n - 1),
                                                skip_group_check=True), [bPt[a], tl["bV"]], [bpO])

            front(0)
            mask(0)
            for x in range(n):
                if x + 1 < n:
                    front(x + 1)
                back(x)
                if x + 1 < n:
                    mask(x + 1)
                step()

        def finish_branch(hk, br, first_branch):
            S.op("dve", lambda: V.tensor_scalar(out=rden[:], in0=pO[:, :, 64], scalar1=1e-30, scalar2=None, op0=ALU.max), [bpO], [brd])
            S.op("dve", lambda: V.reciprocal(out=rden[:], in_=rden[:]), [brd], [brd])
            gate = gns[:, hk * 12:(hk + 1) * 12].rearrange("p (h b) -> p h b", b=3)[:, :, br]
            S.op("dve", lambda: V.tensor_tensor(out=coef[:], in0=rden[:], in1=gate, op=ALU.mult), [brd, bgns], [brd])
            dst = acc[:, 4 * hk:4 * hk + 4, :]
            if DBG >= 0 and cur["i"] == DBG:
                S.op("dve", lambda: V.tensor_tensor(out=dbgt[:], in0=pO[:, :, 0:64], in1=bc_last(rden[:], 128, 4, 64), op=ALU.mult), [bpO, brd], [bdbgt])
                S.dma("pool", dbg_out["d_obr"][br, :, hk * 256:(hk + 1) * 256], dbgt[:].rearrange("p h d -> p (h d)"), reads=[bdbgt])
            if first_branch:
                S.op("dve", lambda: V.tensor_tensor(out=dst, in0=pO[:, :, 0:64], in1=bc_last(coef[:], 128, 4, 64), op=ALU.mult),
                     [bpO, brd], [bacc])
            else:
                S.op("dve", lambda: V.tensor_tensor(out=tmpo[:], in0=pO[:, :, 0:64], in1=bc_last(coef[:], 128, 4, 64), op=ALU.mult),
                     [bpO, brd], [btmpo])
                S.op("pool", lambda: G.tensor_tensor(out=dst, in0=dst, in1=tmpo[:], op=ALU.add), [btmpo, bacc], [bacc])

        def A_gen(i):
            hp = i % 2
            for tt_ in range(2):
                S.op("act", lambda: A_.activation(out=xn[:], in_=xt[tt_][:], func=AF.Square, accum_out=ss[:, 0:1]), [bxt[tt_]], [bxn, bss])
                S.op("act", lambda: A_.activation(out=ss[:, 1:2], in_=ss[:, 0:1], func=AF.Sqrt, scale=1.0 / D_MODEL, bias=EPS), [bss], [bss])
                S.op("dve", lambda: V.reciprocal(out=ss[:, 1:2], in_=ss[:, 1:2]), [bss], [bss])
                S.op("pool", lambda: G.tensor_scalar(out=xn[:], in0=xt[tt_][:], scalar1=ss[:, 1:2], scalar2=None, op0=ALU.mult), [bxt[tt_], bss], [bxn])
                for k in range(8):
                    S.op("pe", lambda: T.transpose(out=ptr[:, k * 128:(k + 1) * 128], in_=xn[:, k * 128:(k + 1) * 128], identity=identb[:]), [bxn, bidb], [bptr])
                S.op("act", lambda: A_.copy(out=uT[:, :, tt_ * 128:(tt_ + 1) * 128], in_=ptr[:].rearrange("p (k t) -> p k t", k=8)), [bptr], [buT])
            yield
            ck("A2")
            for tt_ in range(2):
                t = 2 * i + tt_
                ks_ = tt_
                a = nxt("pmm")
                for k in range(8):
                    S.op("pe", lambda: T.matmul(pmm[a][:], lhsT=uT[:, k, tt_ * 128:(tt_ + 1) * 128], rhs=WbA[:, k, 0:512],
                                                start=(k == 0), stop=(k == 7)), [buT, bWbA], [bpmm[a]])
                S.op("act", lambda: A_.copy(out=kvr[ks_][:, 0:512], in_=pmm[a][:]), [bpmm[a]], [bkvr[ks_]])
                a = nxt("pmm")
                for k in range(8):
                    S.op("pe", lambda: T.matmul(pmm[a][:, 0:256], lhsT=uT[:, k, tt_ * 128:(tt_ + 1) * 128], rhs=WbA[:, k, 512:768],
                                                start=(k == 0), stop=(k == 7)), [buT, bWbA], [bpmm[a]])
                S.op("act", lambda: A_.copy(out=kvr[ks_][:, 512:768], in_=pmm[a][:, 0:256]), [bpmm[a]], [bkvr[ks_]])
                ck("A2a")
                norm_rope(kvr[ks_][:, 256:384].rearrange("p (h d) -> p h d", h=2), 128, 2, gksb, bgks, rpA[tt_], brpA[tt_], bkvr[ks_])
                norm_rope(kvr[ks_][:, 512:640].rearrange("p (h d) -> p h d", h=2), 128, 2, gkwb, bgkw, rpA[tt_], brpA[tt_], bkvr[ks_])
                yield
                ck("A2b")
                S.dma("pool", okvc_d[t * 128:(t + 1) * 128, :], kvr[ks_][:, 0:256], reads=[bkvr[ks_]])
                S.dma("pool", okvs_d[t * 128:(t + 1) * 128, :], kvr[ks_][:, 256:512], reads=[bkvr[ks_]])
                if t >= NT - 4:
                    S.dma("pool", okvw_d[(t - (NT - 4)) * 128:(t - (NT - 4) + 1) * 128, :], kvr[ks_][:, 512:768], reads=[bkvr[ks_]])
                ck("A2c")
                S.op("pool", lambda: G.tensor_copy(out=kvb[:], in_=kvr[ks_][:]), [bkvr[ks_]], [bkvb])
                wsl = t % 8
                S.op("pool", lambda: G.tensor_copy(out=VSa[:, t, :, 0:64], in_=kvb[:, 384:512].rearrange("p (h d) -> p h d", h=2)), [bkvb], [bVSa[t]])
                S.op("pool", lambda: G.tensor_copy(out=VWa[:, wsl, :, 0:64], in_=kvb[:, 640:768].rearrange("p (h d) -> p h d", h=2)), [bkvb], [bVWa[wsl]])
                ck("A2d")
                srcs = [0, 64, 128, 192, 256, 320, 512, 576]
                for n_, c0 in enumerate(srcs):
                    S.op("pe", lambda: T.transpose(out=ptr[0:64, n_ * 128:(n_ + 1) * 128], in_=kvb[:, c0:c0 + 64], identity=identb[:]), [bkvb, bidb], [bptr])
                ck("A2t")
                S.op("dve", lambda: V.tensor_copy(out=XTp[:, :, 16 + tt_ * 128:16 + (tt_ + 1) * 128], in_=ptr[0:64, 0:512].rearrange("p (k t) -> p k t", k=4)), [bptr], [bXTp])
                ck("A2e1")
                S.op("dve", lambda: V.tensor_copy(out=KST[:, :, t * 128:(t + 1) * 128], in_=ptr[0:64, 512:768].rearrange("p (k t) -> p k t", k=2)), [bptr], [bKST[t]])
                ck("A2e2")
                S.op("dve", lambda: V.tensor_copy(out=KWT[:, :, wsl * 128:(wsl + 1) * 128], in_=ptr[0:64, 768:1024].rearrange("p (k t) -> p k t", k=2)), [bptr], [bKWT[wsl]])
                ck("A2e")
                yield
            yield
            ck("A3")
            for g in range(4):
                a = nxt("pmm")
                for k in range(8):
                    S.op("pe", lambda: T.matmul(pmm[a][:, 0:256], lhsT=WbA[:, k, 768 + g * 128:768 + (g + 1) * 128], rhs=uT[:, k, :],
                                                start=(k == 0), stop=(k == 7)), [buT, bWbA], [bpmm[a]])
                S.op("act", lambda: A_.copy(out=xr[:, g, 3:259], in_=pmm[a][:, 0:256]), [bpmm[a]], [bxr])
            if i + 1 < npair:
                load_x(i + 1)
            for g in range(4):
                S.op("dve", lambda: V.tensor_scalar(out=xc[:], in0=xr[:, g, 3:259], scalar1=cwT[:, g, 3:4], scalar2=cbT[:, g:g + 1],
                                                    op0=ALU.mult, op1=ALU.add), [bxr, bcw, bcb], [bxc])
                for tap in range(3):
                    S.op("dve", lambda: V.scalar_tensor_tensor(out=xc[:], in0=xr[:, g, tap:tap + 256], scalar=cwT[:, g, tap:tap + 1],
                                                               in1=xc[:], op0=ALU.mult, op1=ALU.add), [bxr, bcw, bxc], [bxc])
                S.op("pool", lambda: G.tensor_copy(out=xcb[:], in_=xc[:]), [bxc], [bxcb])
                a = nxt("pmm")
                S.op("pe", lambda: T.matmul(pmm[a][:, 0:256], lhsT=WraB[:, g, :], rhs=xcb[:], start=True, stop=True), [bWr, bxcb], [bpmm[a]])
                S.op("pe", lambda: T.matmul(pmm[a][:, 256:512], lhsT=WrxB[:, g, :], rhs=xcb[:], start=True, stop=True), [bWr, bxcb], [bpmm[a]])
                S.op("act", lambda: A_.activation(out=gr[:], in_=pmm[a][:, 0:256], func=AF.Sigmoid, bias=braT[:, g:g + 1]), [bpmm[a], bbr], [bg_])
                S.op("act", lambda: A_.activation(out=gi[:], in_=pmm[a][:, 256:512], func=AF.Sigmoid, bias=brxT[:, g:g + 1]), [bpmm[a], bbr], [bg_])
                S.op("act", lambda: A_.activation(out=ga[:], in_=gr[:], func=AF.Exp, scale=clT[:, g:g + 1]), [bg_, bcl], [bg_])
                S.op("act", lambda: A_.activation(out=ga2[:], in_=gr[:], func=AF.Exp, scale=cl2T[:, g:g + 1]), [bg_, bcl], [bg_])
                S.op("act", lambda: A_.activation(out=ga2[:], in_=ga2[:], func=AF.Sqrt, scale=-1.0, bias=1.0), [bg_], [bg_])
                S.op("dve", lambda: V.tensor_tensor(out=gi[:], in0=gi[:], in1=xc[:], op=ALU.mult), [bg_, bxc], [bg_])
                S.op("dve", lambda: V.tensor_tensor(out=gi[:], in0=gi[:], in1=ga2[:], op=ALU.mult), [bg_], [bg_])
                S.op("dve", lambda: V.tensor_tensor_scan(out=hsf[:], data0=ga[:], data1=gi[:], initial=hprev[:, g:g + 1], op0=ALU.mult, op1=ALU.add),
                     [bg_, bhprev], [bhsf])
                S.op("dve", lambda: V.tensor_copy(out=hprev[:, g:g + 1], in_=hsf[:, 255:256]), [bhsf], [bhprev])
                S.op("pool", lambda: G.tensor_copy(out=hsb[hp][:, g, :], in_=hsf[:]), [bhsf], [bhsb[hp]])
                yield
            if i == npair - 1:
                for t3 in range(3):
                    S.dma("pool", None, None, reads=[bxr], fn=lambda: nc.gpsimd.dma_start(
                        out=oconv_d[t3].rearrange("(g c) -> c g", c=128), in_=xr[:, :, 256 + t3], allow_slow_non_contiguous=True))
                S.dma("pool", None, None, reads=[bhprev], fn=lambda: nc.gpsimd.dma_start(
                    out=oh_d.rearrange("(g c) -> c g", c=128), in_=hprev[:], allow_slow_non_contiguous=True))
            S.op("pool", lambda: G.tensor_copy(out=xr[:, :, 0:3], in_=xr[:, :, 256:259]), [bxr], [bxr])
            yield
            ck("A4")
            m0 = 1 if i == 0 else 0
            for kind, (w1s, hid, pbt) in enumerate(((W1k, hidk, pbk), (W1v, hidv, pbv))):
                a = nxt("pmm")
                for h in range(2):
                    for j in range(32):
                        S.op("pe", lambda: T.matmul(pmm[a][0:64, h * 16:(h + 1) * 16], lhsT=w1s[:, j, :],
                                                    rhs=XTp[:, kind * 2 + h, j:j + 241:16], start=(j == 0), stop=(j == 31)),
                             [bW1, bXTp], [bpmm[a]])
                S.op("act", lambda: A_.activation(out=hid[:], in_=pmm[a][0:64, 0:32], func=AF.Silu, bias=pbt[:, 0:1]), [bpmm[a], bpb], [bhid])
            S.op("pool", lambda: G.tensor_copy(out=XTp[:, :, 0:16], in_=XTp[:, :, 256:272]), [bXTp], [bXTp])
            yield
            a = nxt("pmm")
            for h in range(2):
                S.op("pe", lambda: T.matmul(pmm[a][0:16, h * 64:(h + 1) * 64], lhsT=hidk[:, h * 16:(h + 1) * 16], rhs=W2k[:], start=True, stop=True),
                     [bhid, bW2], [bpmm[a]])
            S.op("act", lambda: A_.copy(out=kcn[:].rearrange("p h d -> p (h d)"), in_=pmm[a][0:16, 0:128]), [bpmm[a]], [bkcn])
            norm_rope(kcn[:], 16, 2, gkcb, bgkc, ropeCt[:, i, :], bropeC, bkcn)
            S.op("pool", lambda: G.tensor_copy(out=kcnb[:], in_=kcn[:]), [bkcn], [bkcn])
            for h in range(2):
                S.op("pe", lambda: T.transpose(out=ptr[0:64, h * 16:(h + 1) * 16], in_=kcnb[:, h, :], identity=identb[0:16, 0:16]), [bkcn, bidb], [bptr])
            c0 = 16 * i - 1 + m0
            S.op("dve", lambda: V.tensor_copy(out=KCT[:, :, c0:16 * i + 15], in_=ptr[0:64, 0:32].rearrange("p (h m) -> p h m", h=2)[:, :, m0:16]), [bptr], [bKCT])
            a = nxt("pmm")
            S.op("pe", lambda: T.matmul(pmm[a][0:64, 0:32], lhsT=W2v[:], rhs=hidv[:], start=True, stop=True), [bhid, bW2], [bpmm[a]])
            S.op("act", lambda: A_.copy(out=VCT[:, :, c0:16 * i + 15], in_=pmm[a][0:64, 0:32].rearrange("p (h m) -> p h m", h=2)[:, :, m0:16]), [bpmm[a]], [bVCT])
            for ch in range(2):
                for h in range(2):
                    n_ = ch * 2 + h
                    S.op("pe", lambda: T.transpose(out=ptr[:, n_ * 64:(n_ + 1) * 64], in_=VCT[:, h, ch * 128:(ch + 1) * 128], identity=identb[0:64, 0:64]),
                         [bVCT, bidb], [bptr])
            S.op("dve", lambda: V.tensor_copy(out=VCA[:, :, :, 0:64], in_=ptr[:, 0:256].rearrange("p (c h d) -> p c h d", c=2, h=2)), [bptr], [bVCA])

        try:
          ck("setup")
          load_x(0)
          for _ in A_gen(0):
              pass
          for i in range(npair):
              load_own(i)
              cur["i"] = i
              hp = i % 2
              agen = A_gen(i + 1) if i + 1 < npair else iter(())

              def step():
                  next(agen, None)
              ck("B1")
              so = i % 2
              S.op("dve", lambda: V.tensor_scalar(out=uTt[:], in0=uT[:, :, 0:128], scalar1=selcol[:, 0:1], scalar2=None, op0=ALU.mult), [buT, bselcol], [buTt])
              S.op("dve", lambda: V.scalar_tensor_tensor(out=uTo[:], in0=uT[:, :, 128:256], scalar=selcol[:, 1:2], in1=uTt[:], op0=ALU.mult, op1=ALU.add),
                   [buT, bselcol, buTt], [buTo])
              ck("B2")
              wq, bwq = unit(WbD[:, :, CQ:CQ + 512], 8)
              a = nxt("pmm")
              for k in range(8):
                  S.op("pe", lambda: T.matmul(pmm[a][:], lhsT=uTo[:, k, :], rhs=wq[:, k, :], start=(k == 0), stop=(k == 7)), [buTo, bwq], [bpmm[a]])
              S.op("act", lambda: A_.copy(out=qf[:], in_=pmm[a][:]), [bpmm[a]], [bqf])
              norm_rope(qf[:].rearrange("p (h d) -> p h d", h=8), 128, 8, gqb, bgq, rpO[so], brpO[so], bqf)
              S.op("pool", lambda: G.tensor_copy(out=qb[:], in_=qf[:]), [bqf], [bqb])
              for h in range(8):
                  S.op("pe", lambda: T.transpose(out=ptr[0:64, h * 128:(h + 1) * 128], in_=qb[:, h * 64:(h + 1) * 64], identity=identb[:]), [bqb, bidb], [bptr])
              S.op("act", lambda: A_.copy(out=QT[:].rearrange("d h q -> d (h q)"), in_=ptr[0:64, :]), [bptr], [bQT])
              a = nxt("pmm")
              for k in range(8):
                  S.op("pe", lambda: T.matmul(pmm[a][:, 0:24], lhsT=uTo[:, k, :], rhs=Wgn[:, k, :], start=(k == 0), stop=(k == 7)), [buTo, bWgn], [bpmm[a]])
              S.op("act", lambda: A_.activation(out=gns[:], in_=pmm[a][:, 0:24], func=AF.Sigmoid), [bpmm[a]], [bgns])
              wz, bwz = unit(WbD[:, :, CZN:CZN + 512], 8)
              a = nxt("pmm")
              for k in range(8):
                  S.op("pe", lambda: T.matmul(pmm[a][:], lhsT=uTo[:, k, :], rhs=wz[:, k, :], start=(k == 0), stop=(k == 7)), [buTo, bwz], [bpmm[a]])
              S.op("act", lambda: A_.activation(out=zs[:], in_=pmm[a][:], func=AF.Silu), [bpmm[a]], [bzs])
              wzr, bwzr = unit(WbD[:, :, CZR:CZR + 512], 8)
              a = nxt("pmm")
              for g in range(4):
                  for k in range(8):
                      S.op("pe", lambda: T.matmul(pmm[a][:, g * 128:(g + 1) * 128], lhsT=wzr[:, k, g * 128:(g + 1) * 128], rhs=uTo[:, k, :],
                                                  start=(k == 0), stop=(k == 7)), [buTo, bwzr], [bpmm[a]])
              S.op("act", lambda: A_.activation(out=zr[:].rearrange("p g t -> p (g t)"), in_=pmm[a][:], func=AF.Silu), [bpmm[a]], [bzr])
              ck("B4B5")
              for hk in range(2):
                  attn_branch(hk, [dict(KT=KCT[:, hk, ch * 128:(ch + 1) * 128], bK=bKCT, V=VCA[:, ch, hk, :], bV=bVCA, kind="tab", mk=cmk[so][:, ch, :], bm=bcmk[so])
                                   for ch in range(2)], 129, step)
                  S.op("dve", lambda: V.tensor_scalar(out=rden[:], in0=pO[:, :, 64], scalar1=1e-30, scalar2=None, op0=ALU.max), [bpO], [brd])
                  S.op("dve", lambda: V.reciprocal(out=rden[:], in_=rden[:]), [brd], [brd])
                  for h in range(4):
                      src1 = fbt[so][:] if h == 0 else impq[:]
                      S.op("dve", lambda: V.scalar_tensor_tensor(out=impq[:], in0=pO[:, h, 65:129], scalar=rden[:, h:h + 1], in1=src1, op0=ALU.mult, op1=ALU.add),
                           [bpO, brd, bfbt[so], bimp], [bimp])
                  finish_branch(hk, 0, True)
                  S.op("dve", lambda: V.max(out=m8[:, 0:8], in_=impq[:]), [bimp], [bimp])
                  S.op("dve", lambda: V.match_replace(out=impw[:], in_to_replace=m8[:, 0:8], in_values=impq[:], imm_value=-1e30), [bimp], [bimp])
                  S.op("dve", lambda: V.max(out=m8[:, 8:16], in_=impw[:]), [bimp], [bimp])
                  S.op("dve", lambda: V.tensor_scalar(out=selb[:], in0=impq[:], scalar1=m8[:, 15:16], scalar2=None, op0=ALU.is_ge), [bimp], [bselb])
                  if DBG == i:
                      S.op("dve", lambda: V.tensor_copy(out=dbgt[:, 0, :], in_=selb[:]), [bselb], [bdbgt])
                      S.dma("pool", dbg_out["d_sel"][hk], dbgt[:, 0, :], reads=[bdbgt])
                      S.dma("pool", dbg_out["d_imp"][hk], impq[:], reads=[bimp])
                  S.op("pe", lambda: T.transpose(out=ptr[0:64, 0:128], in_=selb[:], identity=identb[:]), [bselb, bidb], [bptr])
                  S.op("act", lambda: A_.copy(out=selT[:], in_=ptr[0:64, 0:128]), [bptr], [bselT])
                  nj = 2 * i + 2
                  attn_branch(hk, [dict(KT=KST[:, hk, j * 128:(j + 1) * 128], bK=bKST[j], V=VSa[:, j, hk, :], bV=bVSa[j], kind="sel", j=j,
                                        dm=(None if j < 2 * i else j - 2 * i)) for j in range(nj)], 65, step)
                  finish_branch(hk, 1, False)
                  js = [(m, 2 * i - 4 + m) for m in range(6) if 2 * i - 4 + m >= 0]
                  attn_branch(hk, [dict(KT=KWT[:, hk, (j % 8) * 128:(j % 8 + 1) * 128], bK=bKWT[j % 8], V=VWa[:, j % 8, hk, :], bV=bVWa[j % 8], kind="tab",
                                        mk=wmask[:, m, :], bm=bwmask) for (m, j) in js], 65, step)
                  finish_branch(hk, 2, False)
              for _ in agen:
                  pass
              ck("B6")
              S.op("dve", lambda: V.tensor_tensor(out=ab[:], in0=acc[:].rearrange("p h d -> p (h d)"), in1=zs[:], op=ALU.mult), [bacc, bzs], [bab])
              for k in range(4):
                  S.op("pe", lambda: T.transpose(out=ptr[:, k * 128:(k + 1) * 128], in_=ab[:, k * 128:(k + 1) * 128], identity=identb[:]), [bab, bidb], [bptr])
              S.op("act", lambda: A_.copy(out=aT[:].rearrange("p k t -> p (k t)"), in_=ptr[:, 0:512]), [bptr], [baT])
              S.op("pool", lambda: G.tensor_scalar(out=rt1[:], in0=hsb[hp][:, :, 0:128], scalar1=selcol[:, 0:1], scalar2=None, op0=ALU.mult), [bhsb[hp], bselcol], [brt1])
              S.op("dve", lambda: V.scalar_tensor_tensor(out=rt1[:], in0=hsb[hp][:, :, 128:256], scalar=selcol[:, 1:2], in1=rt1[:], op0=ALU.mult, op1=ALU.add),
                   [bhsb[hp], bselcol, brt1], [brt1])
              if DBG == i:
                  S.dma("pool", dbg_out["d_acc"], acc[:].rearrange("p h d -> p (h d)"), reads=[bacc])
                  S.dma("pool", dbg_out["d_hs"], rt1[:].rearrange("p g t -> p (g t)"), reads=[brt1])
              S.op("pool", lambda: G.tensor_tensor(out=rT[:], in0=rt1[:], in1=zr[:], op=ALU.mult), [brt1, bzr], [brT])
              ck("B7")
              wpa, bwpa = None, None
              for rnd in range(2):
                  for ab_ in range(2):
                      wg, bwg = unit(WbD[:, :, CGM + ab_ * 1024 + rnd * 512:CGM + ab_ * 1024 + (rnd + 1) * 512], 8)
                      a = nxt("pmm")
                      for g in range(4):
                          for k in range(8):
                              S.op("pe", lambda: T.matmul(pmm[a][:, g * 128:(g + 1) * 128], lhsT=wg[:, k, g * 128:(g + 1) * 128], rhs=uTo[:, k, :],
                                                          start=(k == 0), stop=(k == 7)), [buTo, bwg], [bpmm[a]])
                      S.op("act", lambda: A_.activation(out=gm[:, ab_, :], in_=pmm[a][:], func=AF.Sigmoid), [bpmm[a]], [bgm])
                  if rnd == 0:
                      wpa, bwpa = unit(WpaD, 4)
                      wpb, bwpb = unit(WpbD, 4)
                  a = nxt("pmm")
                  for g in range(4):
                      fc = rnd * 4 + g
                      for k in range(4):
                          S.op("pe", lambda: T.matmul(pmm[a][:, g * 128:(g + 1) * 128], lhsT=wpa[:, k, fc * 128:(fc + 1) * 128], rhs=aT[:, k, :],
                                                      start=(k == 0), stop=(k == 3)), [baT, bwpa], [bpmm[a]])
                  S.op("dve", lambda: V.tensor_tensor(out=mt1[:], in0=pmm[a][:], in1=gm[:, 0, :], op=ALU.mult), [bpmm[a], bgm], [bmt])
                  a = nxt("pmm")
                  for g in range(4):
                      fc = rnd * 4 + g
                      for k in range(4):
                          S.op("pe", lambda: T.matmul(pmm[a][:, g * 128:(g + 1) * 128], lhsT=wpb[:, k, fc * 128:(fc + 1) * 128], rhs=rT[:, k, :],
                                                      start=(k == 0), stop=(k == 3)), [brT, bwpb], [bpmm[a]])
                  S.op("dve", lambda: V.tensor_tensor(out=mt2[:], in0=pmm[a][:], in1=gm[:, 1, :], op=ALU.mult), [bpmm[a], bgm], [bmt])
                  S.op("pool", lambda: G.tensor_tensor(out=mT[:, rnd * 4:(rnd + 1) * 4, :].rearrange("p g t -> p (g t)"), in0=mt1[:], in1=mt2[:], op=ALU.add), [bmt], [bmT])
              wo0, bwo0 = unit(WoutD[:, 0:4, :], 4)
              wo1, bwo1 = unit(WoutD[:, 4:8, :], 4)
              for hf in range(2):
                  a = nxt("pmm")
                  for k in range(8):
                      wsrc = wo0 if k < 4 else wo1
                      S.op("pe", lambda: T.matmul(pmm[a][:], lhsT=mT[:, k, :], rhs=wsrc[:, k % 4, hf * 512:(hf + 1) * 512], start=(k == 0), stop=(k == 7)),
                           [bmT, bwo0, bwo1], [bpmm[a]])
                  S.op("dve", lambda: V.tensor_tensor(out=xo[so][:, hf * 512:(hf + 1) * 512], in0=pmm[a][:], in1=xo[so][:, hf * 512:(hf + 1) * 512], op=ALU.add),
                       [bpmm[a], bxo[so]], [bxo[so]])
              S.dma("pool", oy_d[i * 128:(i + 1) * 128, :], xo[so][:], reads=[bxo[so]])

        except StopBuild:
            pass
        S.barrier()
        esP.close()
        esS = ExitStack()
        cur_es[0] = esS
        NS = 16

        def unitw(src_ap, nk, w):
            sl = ust["n"] % 5
            ust["n"] += 1
            view = wu[sl][:].rearrange("p (k c) -> p k c", k=nk)[:, :, 0:w]
            S.dma("sp", view, src_ap, reads=[bWbD], writes=[bwu[sl]])
            return view, bwu[sl]

        def norm_rope2(x3, P, nh, gb, bg, cosb, sinb, brope, bx):
            sq = nr_sq[0:P, 0:nh * 64].rearrange("p (h d) -> p h d", h=nh)
            S.op("dve", lambda: V.tensor_tensor(out=sq, in0=x3, in1=x3, op=ALU.mult), [bx], [bnr])
            S.op("dve", lambda: V.tensor_reduce(out=nr_ss[0:P, 0:nh], in_=sq, axis=AX.X, op=ALU.add), [bnr], [bnr])
            S.op("act", lambda: A_.activation(out=nr_ss[0:P, 0:nh], in_=nr_ss[0:P, 0:nh], func=AF.Sqrt, scale=1.0 / 64, bias=EPS), [bnr], [bnr])
            S.op("dve", lambda: V.reciprocal(out=nr_ss[0:P, 0:nh], in_=nr_ss[0:P, 0:nh]), [bnr], [bnr])
            S.op("dve", lambda: V.tensor_tensor(out=x3, in0=x3, in1=bc_last(nr_ss[0:P, 0:nh], P, nh, 64), op=ALU.mult), [bx, bnr], [bx])
            S.op("dve", lambda: V.tensor_tensor(out=x3, in0=x3, in1=bc_mid(gb[0:P, :], P, nh, 64), op=ALU.mult), [bx, bg], [bx])
            x1, x2 = x3[:, :, 0:8], x3[:, :, 8:16]
            ta, tb, tc, td = (nr_t[0:P, i, 0:nh, :] for i in range(4))
            S.op("dve", lambda: V.tensor_tensor(out=ta, in0=x1, in1=cosb, op=ALU.mult), [bx, brope], [bnr])
            S.op("dve", lambda: V.tensor_tensor(out=tb, in0=x2, in1=sinb, op=ALU.mult), [bx, brope], [bnr])
            S.op("dve", lambda: V.tensor_tensor(out=tc, in0=x2, in1=cosb, op=ALU.mult), [bx, brope], [bnr])
            S.op("dve", lambda: V.tensor_tensor(out=td, in0=x1, in1=sinb, op=ALU.mult), [bx, brope], [bnr])
            S.op("dve", lambda: V.tensor_tensor(out=x1, in0=ta, in1=tb, op=ALU.subtract), [bnr], [bx])
            S.op("dve", lambda: V.tensor_tensor(out=x2, in0=tc, in1=td, op=ALU.add), [bnr], [bx])

        pS = sb("pS", [16, D_IN]); bpS = Buf("pS")
        xs = sb("xs", [16, 1024]); bxs = Buf("xs")
        xsn = sb("xsn", [16, 1024], BF16); bxsn = Buf("xsn")
        ss2 = sb("ss2", [16, 2]); bss2 = Buf("ss2")
        uTs = sb("uTs", [128, 8, 16], BF16); buTs = Buf("uTs")
        rpS = sb("rpS", [16, 16]); brpS = Buf("rpS")
        qTs = sb("qTs", [128, 8, 16], BF16); bqTs = Buf("qTs")
        knT = sb("knT", [64, 4, 16], BF16); bknT = Buf("knT")
        Vsn = sb("Vsn", [16, 2, 65], BF16); Vwn = sb("Vwn", [16, 2, 65], BF16); bVn = Buf("Vn")
        tb16 = sb("tb16", [16, 1024], BF16); btb16 = Buf("tb16")
        tf16 = sb("tf16", [16, 1024]); btf16 = Buf("tf16")
        idxi = sb("idxi", [128, 64], I32); idxf = sb("idxf", [128, 64]); pm8 = sb("pm8", [128, 1]); bidx = Buf("idx")
        Gt = [sb("Gt%d" % i, [128, 4096]) for i in range(2)]; bGt = [Buf("Gt%d" % i) for i in range(2)]
        Es = sb("Es", [128, 256]); bEs = Buf("Es")
        Pts = sb("Pts", [128, 256], BF16); bPts = Buf("Pts")
        Oc = sb("Oc", [4, 2, 194]); bOc = Buf("Oc")
        Os = sb("Os", [4, 2, 65]); bOs = Buf("Os")
        Ow = sb("Ow", [4, 2, 65]); bOw = Buf("Ow")
        Oc32 = sb("Oc32", [32, 4, 194]); Os32 = sb("Os32", [32, 4, 65]); Ow32 = sb("Ow32", [32, 4, 65]); bO32 = Buf("O32")
        fS = sb("fS", [32, 129]); bfS = Buf("fS")
        imp32 = sb("imp32", [32, 136]); impw32 = sb("impw32", [32, 136]); m8s = sb("m8s", [32, 16]); bimp32 = Buf("imp32")
        rd32 = sb("rd32", [32, 3, 4]); brd32 = Buf("rd32")
        sel4 = sb("sel4", [32, 128, 4]); bsel4 = Buf("sel4")
        selM = sb("selM", [128, 32, 4]); bselM = Buf("selM")
        RQ = sb("RQ", [128, 32, 8], BF16); bRQ = Buf("RQ")
        Pn = sb("Pn", [16, 4], BF16); Pnf = sb("Pnf", [16, 4]); bPn = Buf("Pn")
        g32 = sb("g32", [32, 12]); z32 = sb("z32", [32, 256]); acc32 = sb("acc32", [32, 4, 64]); tmp32 = sb("tmp32", [32, 4, 64]); coef32 = sb("coef32", [32, 4]); b32 = Buf("b32")
        a16 = tf16[:, 0:512]; ba16 = btf16
        aTs = sb("aTs", [128, 4, 16], BF16); baTs = Buf("aTs")
        rTs = sb("rTs", [128, 4, 16], BF16); brTs = Buf("rTs")
        mTs = sb("mTs", [128, 8, 16], BF16); bmTs = Buf("mTs")
        cwB = sb("cwB", [16, 4, 512]); cbB = sb("cbB", [16, 512]); scv = sb("scv", [16, 3, 512]); bcwB = Buf("cwB")
        xcs = sb("xcs", [16, 512]); bxcs = Buf("xcs")
        xcT = sb("xcT", [128, 4, 16]); xcTb = sb("xcTb", [128, 4, 16], BF16); bxcT = Buf("xcT")
        h0T = sb("h0T", [128, 4, 16]); hnT = sb("hnT", [128, 4, 16]); bhT = Buf("hT")
        zrT = sb("zrT", [128, 4, 16]); bzrT = Buf("zrT")
        gs = [sb("gs%d" % i, [128, 16]) for i in range(4)]; bgs = Buf("gs")
        bscr = {n: Buf(n) for n in ("Oc", "Os", "Ow", "Sel", "G", "Z", "A")}

        esC = ExitStack()
        cur_es[0] = esC
        Gb = sb("Gb", [128, 4, 16, 64], BF16); bGb = Buf("Gb")
        YT = sb("YT", [128, 4, 8, 512], BF16); bYT = Buf("YT")
        W1p = [sb("W1p%d" % i, [128, 16, 64], BF16) for i in range(2)]; bW1p = Buf("W1p")
        hidS = [sb("hidS%d" % i, [64, 2, 512], BF16) for i in range(2)]; bhidS = Buf("hidS")
        kcs = sb("kcs", [128, 8, 64]); bkcs = Buf("kcs")
        kcsb = sb("kcsb", [128, 8, 64], BF16); bkcsb = Buf("kcsb")
        KCTs = sb("KCTs", [64, 8, 128], BF16); bKCTs = Buf("KCTs")
        VCAs = sb("VCAs", [128, 4, 2, 194], BF16); bVCAs = Buf("VCAs")
        ropeCs = sb("ropeCs", [128, 4, 16]); bropeCs = Buf("ropeCs")
        cmS = sb("cmS", [128, 4]); bcmS = Buf("cmS")
        S.dma("sp", xs[:], xs_d, writes=[bxs])
        S.dma("sp", rpS[:], ropeS_d, writes=[brpS])
        S.op("act", lambda: A_.activation(out=xsn[:], in_=xs[:], func=AF.Square, accum_out=ss2[:, 0:1]), [bxs], [bxsn, bss2])
        S.op("act", lambda: A_.activation(out=ss2[:, 1:2], in_=ss2[:, 0:1], func=AF.Sqrt, scale=1.0 / D_MODEL, bias=EPS), [bss2], [bss2])
        S.op("dve", lambda: V.reciprocal(out=ss2[:, 1:2], in_=ss2[:, 1:2]), [bss2], [bss2])
        S.op("dve", lambda: V.tensor_scalar(out=xsn[:], in0=xs[:], scalar1=ss2[:, 1:2], scalar2=None, op0=ALU.mult), [bxs, bss2], [bxsn])
        for k in range(8):
            S.op("pe", lambda: T.transpose(out=ptr[:, k * 16:(k + 1) * 16], in_=xsn[:, k * 128:(k + 1) * 128], identity=identb[0:16, 0:16]), [bxsn, bidb], [bptr])
        S.op("act", lambda: A_.copy(out=uTs[:].rearrange("p k s -> p (k s)"), in_=ptr[:, 0:128]), [bptr], [buTs])
        for u in range(10):
            c0 = u * 512
            w = min(512, D_IN - c0)
            wv, bwv = unitw(WbD[:, :, c0:c0 + w], 8, w)
            a = nxt("pmm")
            for k in range(8):
                S.op("pe", lambda: T.matmul(pmm[a][0:16, 0:w], lhsT=uTs[:, k, :], rhs=wv[:, k, :], start=(k == 0), stop=(k == 7)), [buTs, bwv], [bpmm[a]])
            S.op("act", lambda: A_.copy(out=pS[:, c0:c0 + w], in_=pmm[a][0:16, 0:w]), [bpmm[a]], [bpS])
        cosS, sinS = rpS[:, 0:8], rpS[:, 8:16]
        norm_rope2(pS[:, 0:512].rearrange("p (h d) -> p h d", h=8), 16, 8, gqb, bgq, bc_mid(cosS, 16, 8, 8), bc_mid(sinS, 16, 8, 8), brpS, bpS)
        norm_rope2(pS[:, CKV + 256:CKV + 384].rearrange("p (h d) -> p h d", h=2), 16, 2, gksb, bgks, bc_mid(cosS, 16, 2, 8), bc_mid(sinS, 16, 2, 8), brpS, bpS)
        norm_rope2(pS[:, CKV + 512:CKV + 640].rearrange("p (h d) -> p h d", h=2), 16, 2, gkwb, bgkw, bc_mid(cosS, 16, 2, 8), bc_mid(sinS, 16, 2, 8), brpS, bpS)
        S.dma("pool", okvcs_d, pS[:, CKV:CKV + 256], reads=[bpS])
        S.dma("pool", okvss_d, pS[:, CKV + 256:CKV + 512], reads=[bpS])
        S.dma("pool", okvws_d[:, 511, :], pS[:, CKV + 512:CKV + 768], reads=[bpS])
        S.dma("pool", okvws_d[:, 0:511, :], ckw_d[:, 1:512, :])
        S.dma("pool", oconvs_d[:, 0:2, :], sconv_d[:, 1:3, :])
        S.dma("pool", oconvs_d[:, 2, :], pS[:, CXR:CXR + 512], reads=[bpS])
        S.op("pool", lambda: G.tensor_copy(out=tb16[:, 0:512], in_=pS[:, 0:512]), [bpS], [btb16])
        for h in range(8):
            S.op("pe", lambda: T.transpose(out=ptr[0:64, h * 16:(h + 1) * 16], in_=tb16[:, h * 64:(h + 1) * 64], identity=identb[0:16, 0:16]), [btb16, bidb], [bptr])
        S.op("act", lambda: A_.copy(out=qTs[0:64, :, :].rearrange("p h s -> p (h s)"), in_=ptr[0:64, 0:128]), [bptr], [bqTs])
        S.op("dve", lambda: V.tensor_copy(out=qTs[64:128, :, :], in_=qTs[0:64, :, :]), [bqTs], [bqTs])
        S.op("pool", lambda: G.tensor_copy(out=tb16[:, 512:640], in_=pS[:, CKV + 256:CKV + 384]), [bpS], [btb16])
        S.op("pool", lambda: G.tensor_copy(out=tb16[:, 640:768], in_=pS[:, CKV + 512:CKV + 640]), [bpS], [btb16])
        for n_ in range(4):
            S.op("pe", lambda: T.transpose(out=ptr[0:64, n_ * 16:(n_ + 1) * 16], in_=tb16[:, 512 + n_ * 64:512 + (n_ + 1) * 64], identity=identb[0:16, 0:16]), [btb16, bidb], [bptr])
        S.op("act", lambda: A_.copy(out=knT[:].rearrange("p n s -> p (n s)"), in_=ptr[0:64, 0:64]), [bptr], [bknT])
        S.op("pool", lambda: G.memset(Vsn[:], 1.0), [], [bVn])
        S.op("pool", lambda: G.memset(Vwn[:], 1.0), [], [bVn])
        S.op("pool", lambda: G.tensor_copy(out=Vsn[:, :, 0:64], in_=pS[:, CKV + 384:CKV + 512].rearrange("p (h d) -> p h d", h=2)), [bpS], [bVn])
        S.op("pool", lambda: G.tensor_copy(out=Vwn[:, :, 0:64], in_=pS[:, CKV + 640:CKV + 768].rearrange("p (h d) -> p h d", h=2)), [bpS], [bVn])
        S.dma("sp", None, None, writes=[bcwB], fn=lambda: nc.sync.dma_start(out=cwB[:].rearrange("p t c -> p (t c)"), in_=conv_w_d.rearrange("t c -> (t c)").partition_broadcast(16)))
        S.dma("sp", None, None, writes=[bcwB], fn=lambda: nc.sync.dma_start(out=cbB[:], in_=conv_b_d.partition_broadcast(16)))
        S.dma("sp", scv[:], sconv_d, writes=[bcwB])
        S.op("dve", lambda: V.tensor_tensor(out=xcs[:], in0=pS[:, CXR:CXR + 512], in1=cwB[:, 3, :], op=ALU.mult), [bpS, bcwB], [bxcs])
        S.op("dve", lambda: V.tensor_tensor(out=xcs[:], in0=xcs[:], in1=cbB[:], op=ALU.add), [bxcs, bcwB], [bxcs])
        for tap in range(3):
            S.op("dve", lambda: V.tensor_tensor(out=tf16[:, 0:512], in0=scv[:, tap, :], in1=cwB[:, tap, :], op=ALU.mult), [bcwB], [btf16])
            S.op("dve", lambda: V.tensor_tensor(out=xcs[:], in0=xcs[:], in1=tf16[:, 0:512], op=ALU.add), [bxcs, btf16], [bxcs])
        a = nxt("pmm")
        for g in range(4):
            S.op("pe", lambda: T.transpose(out=pmm[a][:, g * 16:(g + 1) * 16], in_=xcs[:, g * 128:(g + 1) * 128], identity=identf[0:16, 0:16]), [bxcs, bidf], [bpmm[a]])
        S.op("act", lambda: A_.copy(out=xcT[:].rearrange("p g s -> p (g s)"), in_=pmm[a][:, 0:64]), [bpmm[a]], [bxcT])
        S.op("dve", lambda: V.tensor_copy(out=xcTb[:], in_=xcT[:]), [bxcT], [bxcT])
        for g in range(4):
            S.dma("sp", None, None, writes=[bhT], fn=lambda: nc.sync.dma_start(out=h0T[:, g, :], in_=sh_d[:, g * 128:(g + 1) * 128].rearrange("s c -> c s"), allow_slow_non_contiguous=True))
        for g in range(4):
            a = nxt("pmm")
            S.op("pe", lambda: T.matmul(pmm[a][:, 0:16], lhsT=WraB[:, g, :], rhs=xcTb[:, g, :], start=True, stop=True), [bWr, bxcT], [bpmm[a]])
            S.op("pe", lambda: T.matmul(pmm[a][:, 16:32], lhsT=WrxB[:, g, :], rhs=xcTb[:, g, :], start=True, stop=True), [bWr, bxcT], [bpmm[a]])
            S.op("act", lambda: A_.activation(out=gs[0][:], in_=pmm[a][:, 0:16], func=AF.Sigmoid, bias=braT[:, g:g + 1]), [bpmm[a], bbr], [bgs])
            S.op("act", lambda: A_.activation(out=gs[1][:], in_=pmm[a][:, 16:32], func=AF.Sigmoid, bias=brxT[:, g:g + 1]), [bpmm[a], bbr], [bgs])
            S.op("act", lambda: A_.activation(out=gs[2][:], in_=gs[0][:], func=AF.Exp, scale=clT[:, g:g + 1]), [bgs, bcl], [bgs])
            S.op("act", lambda: A_.activation(out=gs[3][:], in_=gs[0][:], func=AF.Exp, scale=cl2T[:, g:g + 1]), [bgs, bcl], [bgs])
            S.op("act", lambda: A_.activation(out=gs[3][:], in_=gs[3][:], func=AF.Sqrt, scale=-1.0, bias=1.0), [bgs], [bgs])
            S.op("dve", lambda: V.tensor_tensor(out=gs[1][:], in0=gs[1][:], in1=xcT[:, g, :], op=ALU.mult), [bgs, bxcT], [bgs])
            S.op("dve", lambda: V.tensor_tensor(out=gs[1][:], in0=gs[1][:], in1=gs[3][:], op=ALU.mult), [bgs], [bgs])
            S.op("dve", lambda: V.tensor_tensor(out=gs[2][:], in0=gs[2][:], in1=h0T[:, g, :], op=ALU.mult), [bgs, bhT], [bgs])
            S.op("dve", lambda: V.tensor_tensor(out=hnT[:, g, :], in0=gs[2][:], in1=gs[1][:], op=ALU.add), [bgs], [bhT])
        for g in range(4):
            S.dma("pool", None, None, reads=[bhT], fn=lambda: nc.gpsimd.dma_start(out=ohs_d[:, g * 128:(g + 1) * 128].rearrange("s c -> c s"), in_=hnT[:, g, :], allow_slow_non_contiguous=True))
        S.op("act", lambda: A_.activation(out=tf16[:, 0:512], in_=pS[:, CZR:CZR + 512], func=AF.Silu), [bpS], [btf16])
        a = nxt("pmm")
        for g in range(4):
            S.op("pe", lambda: T.transpose(out=pmm[a][:, g * 16:(g + 1) * 16], in_=tf16[:, g * 128:(g + 1) * 128], identity=identf[0:16, 0:16]), [btf16, bidf], [bpmm[a]])
        S.op("dve", lambda: V.tensor_tensor(out=rTs[:].rearrange("p g s -> p (g s)"), in0=pmm[a][:, 0:64], in1=hnT[:].rearrange("p g s -> p (g s)"), op=ALU.mult), [bpmm[a], bhT], [brTs])
        for kind, wd in enumerate((w1_k_d, w1_v_d)):
            S.dma("sp", Gt[0][:, 0:1024].rearrange("p (c e) -> p c e", c=16), wd.rearrange("(c a) d e -> (a d) c e", a=2), writes=[bGt[0]])
            S.op("dve", lambda: V.tensor_copy(out=W1p[kind][:].rearrange("p c e -> p (c e)"), in_=Gt[0][:, 0:1024]), [bGt[0]], [bW1p])
        S.op("pool", lambda: G.memset(VCAs[:], 1.0), [], [bVCAs])
        for ch in range(4):
            S.dma("sp", Gt[1][:, 0:129], As_d[ch * 128:(ch + 1) * 128, :], writes=[bGt[1]])
            for hk in range(2):
                S.op("dve", lambda: V.tensor_copy(out=VCAs[:, ch, hk, 65:194], in_=Gt[1][:, 0:129]), [bGt[1]], [bVCAs])
        S.dma("sp", ropeCs[:], ropeC_d[1:513, :].rearrange("(c p) e -> p c e", p=128), writes=[bropeCs])
        S.dma("sp", cmS[:], cmS_d, writes=[bcmS])
        S.dma("sp", fS[:], fS_d, writes=[bfS])
        S.dma("sp", idxi[:], ptrep_d, writes=[bidx])
        S.dma("sp", pm8[:], pm8_d, writes=[bidx])
        S.op("dve", lambda: V.tensor_copy(out=idxf[:], in_=idxi[:]), [bidx], [bidx])
        S.op("dve", lambda: V.tensor_scalar(out=idxf[:], in0=idxf[:], scalar1=8.0, scalar2=None, op0=ALU.mult), [bidx], [bidx])
        S.op("dve", lambda: V.tensor_scalar(out=idxf[:], in0=idxf[:], scalar1=pm8[:, 0:1], scalar2=None, op0=ALU.add), [bidx], [bidx])
        S.op("dve", lambda: V.tensor_copy(out=idxi[:], in_=idxf[:]), [bidx], [bidx])
        S.op("pool", lambda: G.memset(hidS[0][:], 0.0), [], [bhidS])
        S.op("pool", lambda: G.memset(hidS[1][:], 0.0), [], [bhidS])
        gcount = {"n": 0}

        def gather(pool_d, s, t):
            sl = gcount["n"] % len(Gt)
            gcount["n"] += 1
            col = s * 4 + t
            S.dma("pool", None, None, reads=[bidx], writes=[bGt[sl]], fn=lambda: nc.gpsimd.indirect_dma_start(
                out=Gt[sl][:], out_offset=None, in_=pool_d, in_offset=bass.IndirectOffsetOnAxis(ap=idxi[:, col:col + 1], axis=0)))
            return Gt[sl], bGt[sl]

        for s in range(ns_seq):
            for t in range(4):
                gt_, bgt_ = gather(poolc_d, s, t)
                src = gt_[:].rearrange("p (j k d) -> p k j d", j=16, k=4)
                S.op("dve", lambda: V.tensor_copy(out=Gb[:, 0:2, :, :], in_=src[:, 0:2, :, :]), [bgt_], [bGb])
                S.op("act", lambda: A_.copy(out=Gb[:, 2:4, :, :], in_=src[:, 2:4, :, :]), [bgt_], [bGb])
                for kvh in range(4):
                    for jc in range(8):
                        S.op("pe", lambda: T.transpose(out=ptr[:, jc * 128:(jc + 1) * 128], in_=Gb[:, kvh, 2 * jc:2 * jc + 2, :].rearrange("p j d -> p (j d)"), identity=identb[:]), [bGb, bidb], [bptr])
                    eng = "act" if kvh % 2 == 0 else "dve"
                    if eng == "act":
                        S.op("act", lambda: A_.copy(out=YT[:, kvh, :, t * 128:(t + 1) * 128], in_=ptr[:].rearrange("p (c s) -> p c s", c=8)), [bptr], [bYT])
                    else:
                        S.op("dve", lambda: V.tensor_copy(out=YT[:, kvh, :, t * 128:(t + 1) * 128], in_=ptr[:].rearrange("p (c s) -> p c s", c=8)), [bptr], [bYT])
            for kind in range(2):
                for h in range(2):
                    kvh = kind * 2 + h
                    a = nxt("pst")
                    for jc in range(8):
                        S.op("pe", lambda: T.matmul(pst[a][0:64, 0:511], lhsT=W1p[kind][:, jc, :], rhs=YT[:, kvh, jc, 0:511], start=(jc == 0), stop=False), [bW1p, bYT], [bpst[a]])
                    for jc in range(8):
                        S.op("pe", lambda: T.matmul(pst[a][0:64, 0:511], lhsT=W1p[kind][:, 8 + jc, :], rhs=YT[:, kvh, jc, 1:512], start=False, stop=(jc == 7)), [bW1p, bYT], [bpst[a]])
                    pbt = pbk if kind == 0 else pbv
                    S.op("act", lambda: A_.activation(out=hidS[kind][:, h, 0:511], in_=pst[a][0:64, 0:511], func=AF.Silu, bias=pbt[:, 0:1]), [bpst[a], bpb], [bhidS])
            a = nxt("pmm")
            for h in range(2):
                for ch in range(4):
                    n_ = h * 4 + ch
                    S.op("pe", lambda: T.matmul(pmm[a][:, n_ * 64:(n_ + 1) * 64], lhsT=hidS[0][:, h, ch * 128:(ch + 1) * 128], rhs=W2k[:], start=True, stop=True), [bhidS, bW2], [bpmm[a]])
            S.op("act", lambda: A_.copy(out=kcs[:].rearrange("p n d -> p (n d)"), in_=pmm[a][:]), [bpmm[a]], [bkcs])
            for h in range(2):
                norm_rope2(kcs[:, h * 4:(h + 1) * 4, :], 128, 4, gkcb, bgkc, ropeCs[:, :, 0:8], ropeCs[:, :, 8:16], bropeCs, bkcs)
            S.op("pool", lambda: G.tensor_copy(out=kcsb[:], in_=kcs[:]), [bkcs], [bkcsb])
            for n_ in range(8):
                S.op("pe", lambda: T.transpose(out=ptr[0:64, n_ * 128:(n_ + 1) * 128], in_=kcsb[:, n_, :], identity=identb[:]), [bkcsb, bidb], [bptr])
            S.op("act", lambda: A_.copy(out=KCTs[:].rearrange("p n c -> p (n c)"), in_=ptr[0:64, :]), [bptr], [bKCTs])
            a = nxt("pmm")
            for h in range(2):
                for ch in range(4):
                    n_ = h * 4 + ch
                    S.op("pe", lambda: T.matmul(pmm[a][:, n_ * 64:(n_ + 1) * 64], lhsT=hidS[1][:, h, ch * 128:(ch + 1) * 128], rhs=W2v[:], start=True, stop=True), [bhidS, bW2], [bpmm[a]])
            S.op("dve", lambda: V.tensor_copy(out=VCAs[:, :, :, 0:64], in_=pmm[a][:].rearrange("p (h c d) -> p c h d", h=2, c=4)), [bpmm[a]], [bVCAs])
            for hk in range(2):
                for ch in range(4):
                    S.op("pe", lambda: T.matmul(pmk[:, 0, ch * 4:(ch + 1) * 4], lhsT=KCTs[:, hk * 4 + ch, :], rhs=qTs[0:64, 4 * hk:4 * hk + 4, s], start=True, stop=True), [bKCTs, bqTs], [bpmk[0]])
                S.op("act", lambda: A_.activation(out=Es[:, 0:16], in_=pmk[:, 0, 0:16], func=AF.Exp, scale=SCALE), [bpmk[0]], [bEs])
                S.op("dve", lambda: V.tensor_tensor(out=Pts[:, 0:16].rearrange("p (c h) -> p c h", c=4), in0=Es[:, 0:16].rearrange("p (c h) -> p c h", c=4),
                                                    in1=bc_last(cmS[:], 128, 4, 4), op=ALU.mult), [bEs, bcmS], [bPts])
                for ch in range(4):
                    S.op("pe", lambda: T.matmul(pO[0:4, 0, 0:194], lhsT=Pts[:, ch * 4:(ch + 1) * 4], rhs=VCAs[:, ch, hk, :], start=(ch == 0), stop=(ch == 3)), [bPts, bVCAs], [bpO])
                S.op("act", lambda: A_.copy(out=Oc[:, hk, :], in_=pO[0:4, 0, 0:194]), [bpO], [bOc])
            S.dma("sp", scrOc[:, 2 * s:2 * s + 2, :], Oc[:], reads=[bOc], writes=[bscr["Oc"]])
        S.dma("sp", Oc32[:], scrOc.rearrange("h q e -> q h e"), reads=[bscr["Oc"]], writes=[bO32])
        S.op("dve", lambda: V.tensor_scalar(out=rd32[:, 0, :], in0=Oc32[:, :, 64], scalar1=1e-30, scalar2=None, op0=ALU.max), [bO32], [brd32])
        S.op("dve", lambda: V.reciprocal(out=rd32[:, 0, :], in_=rd32[:, 0, :]), [brd32], [brd32])
        S.op("pool", lambda: G.memset(imp32[:], -1e30), [], [bimp32])
        for h in range(4):
            src1 = fS[:] if h == 0 else imp32[:, 0:129]
            S.op("dve", lambda: V.scalar_tensor_tensor(out=imp32[:, 0:129], in0=Oc32[:, h, 65:194], scalar=rd32[:, 0, h:h + 1], in1=src1, op0=ALU.mult, op1=ALU.add),
                 [bO32, brd32, bfS, bimp32], [bimp32])
        S.op("dve", lambda: V.max(out=m8s[:, 0:8], in_=imp32[:]), [bimp32], [bimp32])
        S.op("dve", lambda: V.match_replace(out=impw32[:], in_to_replace=m8s[:, 0:8], in_values=imp32[:], imm_value=-1e30), [bimp32], [bimp32])
        S.op("dve", lambda: V.max(out=m8s[:, 8:16], in_=impw32[:]), [bimp32], [bimp32])
        S.op("dve", lambda: V.tensor_scalar(out=impw32[:, 0:128], in0=imp32[:, 0:128], scalar1=m8s[:, 15:16], scalar2=None, op0=ALU.is_ge), [bimp32], [bimp32])
        S.op("dve", lambda: V.tensor_copy(out=sel4[:], in_=bc_last(impw32[:, 0:128], 32, 128, 4)), [bimp32], [bsel4])
        S.dma("sp", scrSel, sel4[:].rearrange("q b r -> q (b r)"), reads=[bsel4], writes=[bscr["Sel"]])
        for hf in range(2):
            S.dma("sp", None, None, reads=[bscr["Sel"]], writes=[bselM], fn=lambda: nc.sync.dma_start(
                out=selM[:, hf * 16:(hf + 1) * 16, :], in_=scrSel[hf * 16:(hf + 1) * 16, :].rearrange("q (t p) -> p q t", p=128), allow_slow_non_contiguous=True))
        S.barrier()
        esC.close()
        cur_es[0] = esS
        Kb = sb("Kb", [128, 2, 16, 64], BF16); bKb = Buf("Kb")
        Vs = sb("Vs", [128, 4, 16, 2, 65], BF16); bVs = Buf("Vs")
        KTs = sb("KTs", [128, 8, 128], BF16); bKTs = Buf("KTs")
        Wt = sb("Wt", [128, 4, 256]); bWt = Buf("Wt")
        Kwb = sb("Kwb", [128, 4, 2, 64], BF16); bKwb = Buf("Kwb")
        Vw = sb("Vw", [128, 4, 2, 65], BF16); bVw = Buf("Vw")
        KwT = sb("KwT", [64, 8, 128], BF16); bKwT = Buf("KwT")
        Gt.append(sb("Gt2", [128, 4096])); bGt.append(Buf("Gt2"))
        S.op("pool", lambda: G.memset(RQ[:], 0.0), [], [bRQ])
        for hk in range(2):
            S.op("dve", lambda: V.tensor_copy(out=RQ[0:64, :, :].rearrange("p (s k) e -> p s k e", k=2)[:, :, hk, 0:4], in_=qTs[0:64, 4 * hk:4 * hk + 4, :].rearrange("p h s -> p s h")), [bqTs], [bRQ])
            S.op("dve", lambda: V.tensor_copy(out=RQ[64:128, :, :].rearrange("p (s k) e -> p s k e", k=2)[:, :, hk, 4:8], in_=qTs[64:128, 4 * hk:4 * hk + 4, :].rearrange("p h s -> p s h")), [bqTs], [bRQ])
        S.op("pool", lambda: G.memset(Vs[:], 1.0), [], [bVs])
        for s in range(ns_seq):
            for t in range(4):
                gt_, bgt_ = gather(pools_d, s, t)
                src = gt_[:].rearrange("p (j k d) -> p k j d", j=16, k=4)
                S.op("dve", lambda: V.tensor_copy(out=Kb[:], in_=src[:, 0:2, :, :]), [bgt_], [bKb])
                S.op("act", lambda: A_.copy(out=Vs[:, t, :, :, 0:64], in_=gt_[:].rearrange("p (j k d) -> p j k d", j=16, k=4)[:, :, 2:4, :]), [bgt_], [bVs])
                for hk in range(2):
                    for jp in range(8):
                        S.op("pe", lambda: T.transpose(out=ptr[:, jp * 128:(jp + 1) * 128], in_=Kb[:, hk, 2 * jp:2 * jp + 2, :].rearrange("p j d -> p (j d)"), identity=identb[:]), [bKb, bidb], [bptr])
                    S.op("act", lambda: A_.copy(out=KTs[:].rearrange("p c s -> p (c s)"), in_=ptr[:]), [bptr], [bKTs])
                    for jp in range(8):
                        c0 = (t * 8 + jp) * 8
                        S.op("pe", lambda: T.matmul(pst[hk][:, c0:c0 + 8], lhsT=KTs[:, jp, :], rhs=RQ[:, s * 2 + hk, :], start=True, stop=True), [bKTs, bRQ], [bpst[hk]])
            for hk in range(2):
                q_ = s * 2 + hk
                S.op("act", lambda: A_.activation(out=Es[:], in_=pst[hk][:, 0:256], func=AF.Exp, scale=SCALE), [bpst[hk]], [bEs])
                S.op("dve", lambda: V.tensor_tensor(out=Pts[:].rearrange("p (t c) -> p t c", t=4), in0=Es[:].rearrange("p (t c) -> p t c", t=4),
                                                    in1=bc_last(selM[:, q_, :], 128, 4, 64), op=ALU.mult), [bEs, bselM], [bPts])
                first = True
                for t in range(4):
                    for jp in range(8):
                        for a2 in range(2):
                            c0 = ((t * 8 + jp) * 2 + a2) * 4
                            S.op("pe", lambda: T.matmul(pO[0:4, 0, 0:65], lhsT=Pts[:, c0:c0 + 4], rhs=Vs[:, t, 2 * jp + a2, hk, :], start=first, stop=False,
                                                        skip_group_check=True), [bPts, bVs], [bpO])
                            first = False
                S.op("pe", lambda: T.matmul(pmk[0:16, 0, 0:4], lhsT=knT[:, hk, :], rhs=qTs[0:64, 4 * hk:4 * hk + 4, s], start=True, stop=True), [bknT, bqTs], [bpmk[0]])
                S.op("act", lambda: A_.activation(out=Pnf[:], in_=pmk[0:16, 0, 0:4], func=AF.Exp, scale=SCALE), [bpmk[0]], [bPn])
                S.op("dve", lambda: V.tensor_scalar(out=Pn[:], in0=Pnf[:], scalar1=identf[0:16, s:s + 1], scalar2=None, op0=ALU.mult), [bPn, bidf], [bPn])
                S.op("pe", lambda: T.matmul(pO[0:4, 0, 0:65], lhsT=Pn[:], rhs=Vsn[:, hk, :], start=False, stop=True, skip_group_check=True), [bPn, bVn], [bpO])
                S.op("act", lambda: A_.copy(out=Os[:, hk, :], in_=pO[0:4, 0, 0:65]), [bpO], [bOs])
            S.dma("sp", scrOs[:, 2 * s:2 * s + 2, :], Os[:], reads=[bOs], writes=[bscr["Os"]])
        S.op("pool", lambda: G.memset(Vw[:], 1.0), [], [bVw])
        for s in range(ns_seq):
            S.dma("sp", Wt[:], ckw_d[s].rearrange("(t p) e -> p t e", p=128), writes=[bWt])
            S.op("dve", lambda: V.tensor_copy(out=Kwb[:], in_=Wt[:, :, 0:128].rearrange("p t (h d) -> p t h d", h=2)), [bWt], [bKwb])
            S.op("pool", lambda: G.tensor_copy(out=Vw[:, :, :, 0:64], in_=Wt[:, :, 128:256].rearrange("p t (h d) -> p t h d", h=2)), [bWt], [bVw])
            for t in range(4):
                for hk in range(2):
                    n_ = t * 2 + hk
                    S.op("pe", lambda: T.transpose(out=ptr[0:64, n_ * 128:(n_ + 1) * 128], in_=Kwb[:, t, hk, :], identity=identb[:]), [bKwb, bidb], [bptr])
            S.op("act", lambda: A_.copy(out=KwT[:].rearrange("p n c -> p (n c)"), in_=ptr[0:64, :]), [bptr], [bKwT])
            for hk in range(2):
                for t in range(4):
                    c0 = (hk * 4 + t) * 4
                    S.op("pe", lambda: T.matmul(pmk[:, 1, c0:c0 + 4], lhsT=KwT[:, t * 2 + hk, :], rhs=qTs[0:64, 4 * hk:4 * hk + 4, s], start=True, stop=True), [bKwT, bqTs], [bpmk[1]])
            S.op("act", lambda: A_.activation(out=Pts[:, 0:32], in_=pmk[:, 1, 0:32], func=AF.Exp, scale=SCALE), [bpmk[1]], [bPts])
            for hk in range(2):
                q_ = s * 2 + hk
                for t in range(4):
                    c0 = (hk * 4 + t) * 4
                    S.op("pe", lambda: T.matmul(pO[0:4, 0, 0:65], lhsT=Pts[:, c0:c0 + 4], rhs=Vw[:, t, hk, :], start=(t == 0), stop=False, skip_group_check=True), [bPts, bVw], [bpO])
                S.op("pe", lambda: T.matmul(pmk[0:16, 0, 0:4], lhsT=knT[:, 2 + hk, :], rhs=qTs[0:64, 4 * hk:4 * hk + 4, s], start=True, stop=True), [bknT, bqTs], [bpmk[0]])
                S.op("act", lambda: A_.activation(out=Pnf[:], in_=pmk[0:16, 0, 0:4], func=AF.Exp, scale=SCALE), [bpmk[0]], [bPn])
                S.op("dve", lambda: V.tensor_scalar(out=Pn[:], in0=Pnf[:], scalar1=identf[0:16, s:s + 1], scalar2=None, op0=ALU.mult), [bPn, bidf], [bPn])
                S.op("pe", lambda: T.matmul(pO[0:4, 0, 0:65], lhsT=Pn[:], rhs=Vwn[:, hk, :], start=False, stop=True, skip_group_check=True), [bPn, bVn], [bpO])
                S.op("act", lambda: A_.copy(out=Ow[:, hk, :], in_=pO[0:4, 0, 0:65]), [bpO], [bOw])
            S.dma("sp", scrOw[:, 2 * s:2 * s + 2, :], Ow[:], reads=[bOw], writes=[bscr["Ow"]])
        S.dma("sp", Os32[:], scrOs.rearrange("h q e -> q h e"), reads=[bscr["Os"]], writes=[bO32])
        S.dma("sp", Ow32[:], scrOw.rearrange("h q e -> q h e"), reads=[bscr["Ow"]], writes=[bO32])
        S.op("act", lambda: A_.activation(out=tf16[:, 0:24], in_=pS[:, CGN:CGN + 24], func=AF.Sigmoid), [bpS], [btf16])
        S.dma("sp", scrG, tf16[:, 0:24], reads=[btf16], writes=[bscr["G"]])
        S.dma("sp", g32[:], scrG.rearrange("s (k e) -> (s k) e", k=2), reads=[bscr["G"]], writes=[b32])
        S.op("act", lambda: A_.activation(out=tf16[:, 512:1024], in_=pS[:, CZN:CZN + 512], func=AF.Silu), [bpS], [btf16])
        S.dma("sp", scrZ, tf16[:, 512:1024], reads=[btf16], writes=[bscr["Z"]])
        S.dma("sp", z32[:], scrZ.rearrange("s (k e) -> (s k) e", k=2), reads=[bscr["Z"]], writes=[b32])
        for br, O32 in enumerate((Oc32, Os32, Ow32)):
            if br > 0:
                S.op("dve", lambda: V.tensor_scalar(out=rd32[:, br, :], in0=O32[:, :, 64], scalar1=1e-30, scalar2=None, op0=ALU.max), [bO32], [brd32])
                S.op("dve", lambda: V.reciprocal(out=rd32[:, br, :], in_=rd32[:, br, :]), [brd32], [brd32])
            S.op("dve", lambda: V.tensor_tensor(out=coef32[:], in0=rd32[:, br, :], in1=g32[:].rearrange("q (h b) -> q h b", b=3)[:, :, br], op=ALU.mult), [brd32, b32], [b32])
            if br == 0:
                S.op("dve", lambda: V.tensor_tensor(out=acc32[:], in0=O32[:, :, 0:64], in1=bc_last(coef32[:], 32, 4, 64), op=ALU.mult), [bO32, b32], [b32])
            else:
                S.op("dve", lambda: V.tensor_tensor(out=tmp32[:], in0=O32[:, :, 0:64], in1=bc_last(coef32[:], 32, 4, 64), op=ALU.mult), [bO32, b32], [b32])
                S.op("dve", lambda: V.tensor_tensor(out=acc32[:], in0=acc32[:], in1=tmp32[:], op=ALU.add), [b32], [b32])
        S.op("dve", lambda: V.tensor_tensor(out=acc32[:].rearrange("q h d -> q (h d)"), in0=acc32[:].rearrange("q h d -> q (h d)"), in1=z32[:], op=ALU.mult), [b32], [b32])
        S.dma("sp", scrA, acc32[:].rearrange("q h d -> q (h d)"), reads=[b32], writes=[bscr["A"]])
        S.dma("sp", a16[:], scrA.rearrange("(s k) e -> s (k e)", k=2), reads=[bscr["A"]], writes=[ba16])
        S.op("dve", lambda: V.tensor_copy(out=tb16[:, 0:512], in_=a16[:]), [ba16], [btb16])
        for k in range(4):
            S.op("pe", lambda: T.transpose(out=ptr[:, k * 16:(k + 1) * 16], in_=tb16[:, k * 128:(k + 1) * 128], identity=identb[0:16, 0:16]), [btb16, bidb], [bptr])
        S.op("act", lambda: A_.copy(out=aTs[:].rearrange("p k s -> p (k s)"), in_=ptr[:, 0:64]), [bptr], [baTs])
        wpa, bwpa = unit(WpaD, 4)
        wpb, bwpb = unit(WpbD, 4)
        mS = xs_m = sb("mS", [16, 1024]); bmS = Buf("mS")
        t1 = sb("t1s", [16, 512]); bt1 = Buf("t1s")
        for hf in range(2):
            S.op("act", lambda: A_.activation(out=tf16[:, 0:512], in_=pS[:, CGM + hf * 512:CGM + (hf + 1) * 512], func=AF.Sigmoid), [bpS], [btf16])
            S.op("act", lambda: A_.activation(out=tf16[:, 512:1024], in_=pS[:, CGM + 1024 + hf * 512:CGM + 1024 + (hf + 1) * 512], func=AF.Sigmoid), [bpS], [btf16])
            a = nxt("pmm")
            for k in range(4):
                S.op("pe", lambda: T.matmul(pmm[a][0:16, :], lhsT=aTs[:, k, :], rhs=wpa[:, k, hf * 512:(hf + 1) * 512], start=(k == 0), stop=(k == 3)), [baTs, bwpa], [bpmm[a]])
            S.op("dve", lambda: V.tensor_tensor(out=t1[:], in0=pmm[a][0:16, :], in1=tf16[:, 0:512], op=ALU.mult), [bpmm[a], btf16], [bt1])
            a = nxt("pmm")
            for k in range(4):
                S.op("pe", lambda: T.matmul(pmm[a][0:16, :], lhsT=rTs[:, k, :], rhs=wpb[:, k, hf * 512:(hf + 1) * 512], start=(k == 0), stop=(k == 3)), [brTs, bwpb], [bpmm[a]])
            S.op("dve", lambda: V.tensor_tensor(out=mS[:, hf * 512:(hf + 1) * 512], in0=pmm[a][0:16, :], in1=tf16[:, 512:1024], op=ALU.mult), [bpmm[a], btf16], [bmS])
            S.op("dve", lambda: V.tensor_tensor(out=mS[:, hf * 512:(hf + 1) * 512], in0=mS[:, hf * 512:(hf + 1) * 512], in1=t1[:], op=ALU.add), [bmS, bt1], [bmS])
        S.op("dve", lambda: V.tensor_copy(out=tb16[:], in_=mS[:]), [bmS], [btb16])
        for k in range(8):
            S.op("pe", lambda: T.transpose(out=ptr[:, k * 16:(k + 1) * 16], in_=tb16[:, k * 128:(k + 1) * 128], identity=identb[0:16, 0:16]), [btb16, bidb], [bptr])
        S.op("act", lambda: A_.copy(out=mTs[:].rearrange("p k s -> p (k s)"), in_=ptr[:, 0:128]), [bptr], [bmTs])
        wo0, bwo0 = unit(WoutD[:, 0:4, :], 4)
        wo1, bwo1 = unit(WoutD[:, 4:8, :], 4)
        for hf in range(2):
            a = nxt("pmm")
            for k in range(8):
                wsrc = wo0 if k < 4 else wo1
                S.op("pe", lambda: T.matmul(pmm[a][0:16, :], lhsT=mTs[:, k, :], rhs=wsrc[:, k % 4, hf * 512:(hf + 1) * 512], start=(k == 0), stop=(k == 7)), [bmTs, bwo0, bwo1], [bpmm[a]])
            S.op("dve", lambda: V.tensor_tensor(out=xs[:, hf * 512:(hf + 1) * 512], in0=pmm[a][0:16, :], in1=xs[:, hf * 512:(hf + 1) * 512], op=ALU.add), [bpmm[a], bxs], [bxs])
        S.dma("pool", oys_d, xs[:], reads=[bxs])
        S.finish("sp")
        esS.close()
        S.finish("sp")
    S.close()
    return nc, S


_WEIGHT_NAMES = ["w_in", "g_norm", "g_q", "g_kc", "g_ks", "g_kw", "pe_k", "w1_k", "w2_k", "pe_v", "w1_v", "w2_v",
                 "conv_w", "conv_b", "w_ra", "b_ra", "w_rx", "b_rx", "lam", "w_pa", "w_pb", "w_out"]


def kernel(**inputs):
    inp = {k: np.asarray(v) for k, v in inputs.items()}
    x_prompt = inp["x_prompt"]
    shared = _shared_tables()
    nc, S = build_program()
    print('ninst', S.ninst, flush=True)
    in_maps = []
    poolc = np.ascontiguousarray(inp["cache_kv_cmp"], dtype=np.float32).reshape(81920, 4096)
    pools = np.ascontiguousarray(inp["cache_kv_sel"], dtype=np.float32).reshape(81920, 4096)
    pt = np.asarray(inp["page_table"]).astype(np.int32)
    As = np.zeros((512, 129), np.float32)
    for j in range(129):
        for cc in range(4 * j - 1, 4 * j + 4):
            if 0 <= cc <= 510:
                As[cc, j] = 1.0
    cmS = (np.arange(4)[None, :] * 128 + np.arange(128)[:, None] <= 510).astype(np.float32)
    fS = np.zeros((32, 129), np.float32)
    fS[:, [0, 127, 128]] = 1.0e4
    pm8 = (np.arange(128) % 8).astype(np.float32).reshape(128, 1)
    for c in range(8):
        b, p = c // 2, c % 2
        m = {}
        sl = slice(16 * c, 16 * c + 16)
        m["xs"] = np.ascontiguousarray(inp["x_sample"][sl, 0, :], dtype=np.float32)
        ptc = pt[sl]
        m["ptrep"] = np.ascontiguousarray(np.repeat(ptc.reshape(16, 4, 16), 8, axis=2).transpose(2, 0, 1).reshape(128, 64))
        m["pm8"] = pm8
        m["poolc"] = poolc
        m["pools"] = pools
        m["ckw"] = np.ascontiguousarray(inp["cache_kv_win"][sl], dtype=np.float32).reshape(16, 512, 256)
        m["sconv"] = np.ascontiguousarray(inp["state_conv"][sl], dtype=np.float32)
        m["sh"] = np.ascontiguousarray(inp["state_h"][sl], dtype=np.float32)
        m["As"] = As
        m["cmS"] = cmS
        m["fS"] = fS
        xb = np.ascontiguousarray(x_prompt[b])
        m["xb"] = xb
        m["xown"] = np.ascontiguousarray(xb.reshape(NT, 128, D_MODEL)[p::2].reshape(SEQ // 2, D_MODEL))
        for n in _WEIGHT_NAMES:
            m[n] = np.ascontiguousarray(inp[n], dtype=np.float32)
        m.update(shared)
        m.update(_core_tables(p))
        in_maps.append(m)
    res = run_bass_kernel_spmd(nc, in_maps, core_ids=list(range(8)))
    R = res.results
    B = 4
    y_prompt = np.zeros((B, SEQ, D_MODEL), np.float32)
    kv_cmp_p = np.zeros((B, SEQ, 2, 2, 64), np.float32)
    kv_sel_p = np.zeros((B, SEQ, 2, 2, 64), np.float32)
    kv_win_p = np.zeros((B, 512, 2, 2, 64), np.float32)
    conv_p = np.zeros((B, 3, 512), np.float32)
    h_p = np.zeros((B, 512), np.float32)
    for c in range(8):
        b, p = c // 2, c % 2
        y_prompt[b].reshape(NT, 128, D_MODEL)[p::2] = R[c]["oy"].reshape(NPAIR, 128, D_MODEL)
        if p == 0:
            kv_cmp_p[b] = R[c]["okvc"].reshape(SEQ, 2, 2, 64)
            kv_sel_p[b] = R[c]["okvs"].reshape(SEQ, 2, 2, 64)
            kv_win_p[b] = R[c]["okvw"].reshape(512, 2, 2, 64)
            conv_p[b] = R[c]["oconv"]
            h_p[b] = R[c]["oh"]
    DB = 128
    y_sample = np.concatenate([R[c]["oys"] for c in range(8)], 0).reshape(DB, 1, D_MODEL)
    kv_cmp_s = np.concatenate([R[c]["okvcs"] for c in range(8)], 0).reshape(DB, 1, 2, 2, 64)
    kv_sel_s = np.concatenate([R[c]["okvss"] for c in range(8)], 0).reshape(DB, 1, 2, 2, 64)
    kv_win_s = np.concatenate([R[c]["okvws"] for c in range(8)], 0).reshape(DB, 512, 2, 2, 64)
    conv_s = np.concatenate([R[c]["oconvs"] for c in range(8)], 0).reshape(DB, 3, 512)
    h_s = np.concatenate([R[c]["ohs"] for c in range(8)], 0).reshape(DB, 512)
    return (y_prompt, y_sample, kv_cmp_p, kv_cmp_s, kv_sel_p, kv_sel_s, kv_win_p, kv_win_s, conv_p, conv_s, h_p, h_s)
```
